# Optimizing a Trainium2 kernel written in Bass

```python
import jax, jax.numpy as jnp
from jax import lax
import numpy as np

D_MODEL = 1024
BATCH = 8
SEQ = 2048
DEPTH = 2
DEC_BATCH = 4
DEC_SEQ = 8192
PAST_LEN = 128

HEAD_DIM = 64
ATT_Q_HEADS = 8
ATT_KV_HEADS = 2
ATT_GROUP = ATT_Q_HEADS // ATT_KV_HEADS
ATT_Q_W = ATT_Q_HEADS * HEAD_DIM
ATT_KV_W = ATT_KV_HEADS * HEAD_DIM
WINDOW = 128
BLOCK = 128
RET_HEADS = 4
RET_HEAD_DIM = 128
RET_WIDTH = RET_HEADS * RET_HEAD_DIM
CHUNK = 128
N_BRANCH = 2
D_FF = 2816
CONV_WIDTH = 3
ROPE_THETA = 10000.0
EPS = 1e-6
NEG_INF = -1e30
IN_WIDTH = ATT_Q_W + 2 * ATT_KV_W + 4 * RET_WIDTH + N_BRANCH * D_MODEL

kernel_name = "hybrid_swa_retention_gated_encoder"


def rmsnorm(x, g):
    xf = x.astype(jnp.float32)
    y = xf * lax.rsqrt(jnp.mean(xf * xf, axis=-1, keepdims=True) + EPS)
    return (y * g.astype(jnp.float32)).astype(x.dtype)


def rope(x):
    S, D = x.shape[1], x.shape[-1]
    half = D // 2
    freqs = ROPE_THETA ** (-jnp.arange(half, dtype=jnp.float32) / half)
    ang = jnp.arange(S, dtype=jnp.float32)[:, None] * freqs[None, :]
    cos = jnp.cos(ang)[None, :, None, :]
    sin = jnp.sin(ang)[None, :, None, :]
    xf = x.astype(jnp.float32)
    x1, x2 = xf[..., :half], xf[..., half:]
    out = jnp.concatenate([x1 * cos - x2 * sin, x2 * cos + x1 * sin], axis=-1)
    return out.astype(x.dtype)


def windowed_gqa(q, k, v, sink):
    B, S = q.shape[0], q.shape[1]
    nb = S // BLOCK
    qb = q.reshape(B, nb, BLOCK, ATT_KV_HEADS, ATT_GROUP, HEAD_DIM)
    pad = ((0, 0), (BLOCK, BLOCK), (0, 0), (0, 0))
    kp = jnp.pad(k, pad).reshape(B, nb + 2, BLOCK, ATT_KV_HEADS, HEAD_DIM)
    vp = jnp.pad(v, pad).reshape(B, nb + 2, BLOCK, ATT_KV_HEADS, HEAD_DIM)
    kw = jnp.concatenate([kp[:, :-2], kp[:, 1:-1], kp[:, 2:]], axis=2)
    vw = jnp.concatenate([vp[:, :-2], vp[:, 1:-1], vp[:, 2:]], axis=2)
    scores = jnp.einsum('bnqhgd,bnkhd->bnhgqk', qb, kw).astype(jnp.float32) * (HEAD_DIM ** -0.5)
    r = jnp.arange(BLOCK)[:, None]
    c = jnp.arange(3 * BLOCK)[None, :]
    rel = c - BLOCK - r
    kpos = jnp.arange(nb)[:, None, None] * BLOCK + c[None] - BLOCK
    mask = (jnp.abs(rel) <= WINDOW)[None] & (kpos >= 0) & (kpos < S)
    scores = jnp.where(mask[None, :, None, None], scores, NEG_INF)
    sink_l = sink.astype(jnp.float32).reshape(ATT_KV_HEADS, ATT_GROUP)[None, None, :, :, None, None]
    m = jnp.maximum(jnp.max(scores, axis=-1, keepdims=True), sink_l)
    p = jnp.exp(scores - m)
    denom = jnp.sum(p, axis=-1, keepdims=True) + jnp.exp(sink_l - m)
    out = jnp.einsum('bnhgqk,bnkhd->bnqhgd', (p / denom).astype(v.dtype), vw)
    return out.reshape(B, S, ATT_Q_W)


def retention_dir(q, k, v, log_gamma, strict):
    B, S, H, dk = q.shape
    dv = v.shape[-1]
    nc = S // CHUNK
    qc = q.reshape(B, nc, CHUNK, H, dk)
    kc = k.reshape(B, nc, CHUNK, H, dk)
    vc = v.reshape(B, nc, CHUNK, H, dv)
    idx = jnp.arange(CHUNK, dtype=jnp.float32)
    diff = idx[:, None] - idx[None, :]
    within = (diff > 0) if strict else (diff >= 0)
    dmat = jnp.where(within[None], jnp.exp(log_gamma[:, None, None] * jnp.maximum(diff, 0.0)[None]), 0.0)
    inner = jnp.einsum('bnqhd,bnkhd->bnhqk', qc, kc) * dmat[None, None]
    o_inner = jnp.einsum('bnhqk,bnkhe->bnqhe', inner, vc)
    zeta = jnp.exp(log_gamma[:, None] * (CHUNK - 1 - idx)[None, :])
    kv_chunk = jnp.einsum('bnkhd,hk,bnkhe->bnhde', kc, zeta, vc)
    chunk_decay = jnp.exp(log_gamma * CHUNK)[None, :, None, None]

    def step(state, kv):
        return state * chunk_decay + kv, state

    _, r_prev = lax.scan(step, jnp.zeros((B, H, dk, dv), jnp.float32), jnp.moveaxis(kv_chunk, 1, 0))
    r_prev = jnp.moveaxis(r_prev, 0, 1)
    xi = jnp.exp(log_gamma[:, None] * (idx + 1.0)[None, :])
    o_cross = jnp.einsum('bnqhd,bnhde->bnqhe', qc, r_prev) * xi.T[None, None, :, :, None]
    return (o_inner + o_cross).reshape(B, S, H, dv)


def retention(q, k, v, g, log_decay_f, log_decay_b, norm_g):
    B, S = q.shape[0], q.shape[1]
    dtype = q.dtype
    qh = rope(q.reshape(B, S, RET_HEADS, RET_HEAD_DIM)).astype(jnp.float32)
    kh = rope(k.reshape(B, S, RET_HEADS, RET_HEAD_DIM)).astype(jnp.float32) * (RET_HEAD_DIM ** -0.5)
    vh = v.reshape(B, S, RET_HEADS, RET_HEAD_DIM).astype(jnp.float32)
    lf = log_decay_f.astype(jnp.float32)
    lb = log_decay_b.astype(jnp.float32)
    fwd = retention_dir(qh, kh, vh, lf, strict=False)
    bwd = jnp.flip(retention_dir(jnp.flip(qh, 1), jnp.flip(kh, 1), jnp.flip(vh, 1), lb, strict=True), 1)
    o = fwd + bwd
    o = o * lax.rsqrt(jnp.mean(o * o, axis=-1, keepdims=True) + EPS)
    o = o.reshape(B, S, RET_WIDTH) * norm_g.astype(jnp.float32)
    o = o * jax.nn.silu(g.astype(jnp.float32))
    return o.astype(dtype)


def dwconv_centred(a, w, b):
    S = a.shape[1]
    half = CONV_WIDTH // 2
    ap = jnp.pad(a, ((0, 0), (half, half), (0, 0)))
    out = b[None, None, :]
    for j in range(CONV_WIDTH):
        out = out + ap[:, j:j + S] * w[j][None, None, :]
    return out


def layer(x, norm_mix_g, w_in, attn_sink, log_decay_f, log_decay_b, ret_norm_g,
          w_branch_attn, w_branch_ret, w_out, norm_ffn_g, w_ffn_in, conv_w, conv_b, w_ffn_out):
    B, S = x.shape[0], x.shape[1]
    h = rmsnorm(x, norm_mix_g)
    proj = h @ w_in
    sizes = [ATT_Q_W, ATT_KV_W, ATT_KV_W, RET_WIDTH, RET_WIDTH, RET_WIDTH, RET_WIDTH]
    cuts = []
    acc = 0
    for s_ in sizes:
        acc += s_
        cuts.append(acc)
    q_a, k_a, v_a, q_r, k_r, v_r, g_r, gate = jnp.split(proj, cuts, axis=-1)
    q_a = rope(q_a.reshape(B, S, ATT_Q_HEADS, HEAD_DIM))
    k_a = rope(k_a.reshape(B, S, ATT_KV_HEADS, HEAD_DIM))
    v_a = v_a.reshape(B, S, ATT_KV_HEADS, HEAD_DIM)
    att = windowed_gqa(q_a, k_a, v_a, attn_sink)
    ret = retention(q_r, k_r, v_r, g_r, log_decay_f, log_decay_b, ret_norm_g)
    gates = jax.nn.sigmoid(gate.astype(jnp.float32)).astype(x.dtype).reshape(B, S, N_BRANCH, D_MODEL)
    merged = gates[:, :, 0] * (att @ w_branch_attn) + gates[:, :, 1] * (ret @ w_branch_ret)
    x = x + merged @ w_out
    h = rmsnorm(x, norm_ffn_g)
    a, u = jnp.split(h @ w_ffn_in, 2, axis=-1)
    a = dwconv_centred(a, conv_w, conv_b)
    x = x + (jax.nn.gelu(a, approximate=False) * u) @ w_ffn_out
    return x


def setup_inputs(seed: int = 0) -> dict:
    key = jax.random.key(seed)
    ks = jax.random.split(key, 20)
    f32 = jnp.float32
    nrm = lambda k_, shape, scale: jax.random.normal(k_, shape, f32) * scale
    base_decay = jnp.log(1.0 - 2.0 ** (-5.0 - jnp.arange(RET_HEADS, dtype=f32)))
    return {
        "x_prompt": nrm(ks[0], (BATCH, SEQ, D_MODEL), 1.0),
        "x_sample": nrm(ks[1], (DEC_BATCH, DEC_SEQ, D_MODEL), 1.0),
        "norm_mix_g": 1.0 + nrm(ks[2], (DEPTH, D_MODEL), 0.02),
        "w_in": nrm(ks[3], (DEPTH, D_MODEL, IN_WIDTH), D_MODEL ** -0.5),
        "attn_sink": nrm(ks[4], (DEPTH, ATT_Q_HEADS), 0.5),
        "ret_log_decay_f": base_decay[None, :] * (1.0 + nrm(ks[5], (DEPTH, RET_HEADS), 0.05)),
        "ret_log_decay_b": base_decay[None, :] * (1.0 + nrm(ks[6], (DEPTH, RET_HEADS), 0.05)),
        "ret_norm_g": 1.0 + nrm(ks[7], (DEPTH, RET_WIDTH), 0.02),
        "w_branch_attn": nrm(ks[8], (DEPTH, ATT_Q_W, D_MODEL), ATT_Q_W ** -0.5),
        "w_branch_ret": nrm(ks[9], (DEPTH, RET_WIDTH, D_MODEL), RET_WIDTH ** -0.5),
        "w_out": nrm(ks[10], (DEPTH, D_MODEL, D_MODEL), D_MODEL ** -0.5),
        "norm_ffn_g": 1.0 + nrm(ks[11], (DEPTH, D_MODEL), 0.02),
        "w_ffn_in": nrm(ks[12], (DEPTH, D_MODEL, 2 * D_FF), D_MODEL ** -0.5),
        "conv_w": nrm(ks[13], (DEPTH, CONV_WIDTH, D_FF), CONV_WIDTH ** -0.5),
        "conv_b": nrm(ks[14], (DEPTH, D_FF), 0.01),
        "w_ffn_out": nrm(ks[15], (DEPTH, D_FF, D_MODEL), D_FF ** -0.5),
        "final_norm_g": 1.0 + nrm(ks[16], (D_MODEL,), 0.02),
    }


def reference(x_prompt, x_sample, norm_mix_g, w_in, attn_sink, ret_log_decay_f, ret_log_decay_b,
              ret_norm_g, w_branch_attn, w_branch_ret, w_out, norm_ffn_g, w_ffn_in, conv_w, conv_b,
              w_ffn_out, final_norm_g):
    yp = x_prompt
    ys = x_sample
    for l in range(DEPTH):
        args = (norm_mix_g[l], w_in[l], attn_sink[l], ret_log_decay_f[l], ret_log_decay_b[l], ret_norm_g[l],
                w_branch_attn[l], w_branch_ret[l], w_out[l], norm_ffn_g[l], w_ffn_in[l], conv_w[l], conv_b[l],
                w_ffn_out[l])
        yp = layer(yp, *args)
        ys = layer(ys, *args)
    y_prompt = rmsnorm(yp, final_norm_g)
    y_sample = rmsnorm(ys, final_norm_g)
    return (y_prompt, y_sample)
```

```python
from contextlib import ExitStack
import numpy as np
import concourse.bass as bass
import concourse.mybir as mybir
from concourse.bass_utils import run_bass_kernel_spmd

F32 = mybir.dt.float32
BF16 = mybir.dt.bfloat16
AF = mybir.ActivationFunctionType
ALU = mybir.AluOpType

D = 1024
DEPTH = 2
DFF = 2816
NFC = DFF // 128
INW = 5376
NFM = 13
TMW = 3712
KR0, VR0, GR0, GT0, VA0 = 0, 512, 1024, 1536, 3584
EPS = 1e-6
GROUPS = [[0, 1], [2, 3], [4, 5], [6, 7]]
ENGS = ("pe", "act", "dve", "pool", "sp")


class Sched:
    NDS = 14
    EPOCH = 12000
    UID = [0]

    NINST = [0]
    GLOBAL = {}

    def __init__(self):
        self.ops = []
        self.lastw = {}
        self.readers = {}
        self.inst = Sched.NINST[0]
        Sched.NINST[0] += 1

    PSUM_NAMES = {"PF", "PT", "TP", "PR", "SP", "OP", "IP", "ORp", "TPB", "APs", "RPs", "PA", "PU", "PY"}

    cap = None

    def interleave(self, streams):
        pos = [0] * len(streams)
        total = sum(len(x) for x in streams)
        for _ in range(total):
            best, bf = None, None
            for si, st_ in enumerate(streams):
                if pos[si] < len(st_):
                    f = pos[si] / len(st_)
                    if bf is None or f < bf:
                        best, bf = si, f
            a = streams[best][pos[best]]
            pos[best] += 1
            self.add(*a[0], **a[1])

    def add(self, eng, fn, reads=(), writes=(), dma=False, cc=False):
        if self.cap is not None:
            self.cap.append(((eng, fn), dict(reads=list(reads), writes=list(writes), dma=dma, cc=cc)))
            return None
        def _ps(k):
            return (k if isinstance(k, str) else k[0]) in self.PSUM_NAMES
        ps_reads = [k for k in reads if _ps(k) and k not in writes]
        reads = [k for k in reads if not _ps(k)]
        writes = list(writes)
        import os
        cut = int(os.environ.get("KCUT", "0"))
        if cut and len(self.ops) >= cut and self.inst == int(os.environ.get("KCUTPASS", "0")):
            return None
        idx = len(self.ops)
        deps = {}

        def dep(i, raw):
            if i is None or i == idx:
                return
            deps[i] = deps.get(i, False) or raw

        for k in list(reads) + ps_reads:
            dep(self.lastw.get(k), True)
        for k in ps_reads:
            for r in self.readers.get(k, ()):
                dep(r, False)
        for k in writes:
            dep(self.lastw.get(k), False)
            for r in self.readers.get(k, ()):
                dep(r, False)
        for k in writes + ps_reads:
            self.lastw[k] = idx
            self.readers[k] = []
        for k in reads:
            if k not in writes:
                self.readers.setdefault(k, []).append(idx)
        self.ops.append(dict(eng=eng, fn=fn, deps=sorted(deps), raw=deps, dma=dma, cc=cc, marked=False))
        return idx

    @staticmethod
    def _skip(op, dop, d):
        if dop["dma"] or dop["cc"] or op["dma"] or op["cc"] or dop["eng"] != op["eng"]:
            return False
        if op["eng"] == "pe":
            return True
        return not op["raw"][d]

    def mark(self, name):
        import os
        if os.environ.get("KMARK"):
            print("MARK", self.inst, name, len(self.ops))

    def emit(self, nc, st):
        ops = self.ops
        for op in ops:
            for d in op["deps"]:
                dop = ops[d]
                if dop["dma"] or dop["cc"]:
                    continue
                if self._skip(op, dop, d):
                    continue
                dop["marked"] = True
        last = {}
        for i, op in enumerate(ops):
            if not op["dma"] and not op["cc"]:
                last[op["eng"]] = i
        for e, i in last.items():
            ops[i]["marked"] = True
        gs = self.GLOBAL
        cnt = gs.setdefault("cnt", {e: 0 for e in ENGS})
        dcount = gs.setdefault("dcount", {"sp": 0, "pool": 0})
        duse = gs.setdefault("duse", {"sp": [0] * self.NDS, "pool": [0] * self.NDS})
        SEM = gs.setdefault("SEM", {})
        gst = gs["stack"]
        ncc0 = gs.get("ncc", 0)
        ncc = ncc0
        semkeys = set()
        for op in ops:
            if op["cc"]:
                op["sem"] = ("cc", ncc)
                op["val"] = 1
                ncc += 1
            elif op["dma"]:
                q = op["eng"]
                j = dcount[q] % self.NDS
                dcount[q] += 1
                duse[q][j] += 1
                op["sem"] = ("d", q, j)
                op["val"] = 16 * duse[q][j]
            elif op["marked"]:
                e = op["eng"]
                ep = cnt[e] // self.EPOCH
                cnt[e] += 1
                op["sem"] = ("c", e, ep)
                op["val"] = cnt[e] - ep * self.EPOCH
            else:
                continue
            semkeys.add(op["sem"])
        gs["ncc"] = ncc
        for k in sorted(semkeys, key=str):
            if k not in SEM:
                Sched.UID[0] += 1
                SEM[k] = gst.enter_context(nc.semaphore(f"s{Sched.UID[0]}_" + "_".join(str(x) for x in k)))
        finals = []
        for e, i in last.items():
            finals.append((ops[i]["sem"], ops[i]["val"]))
        for q in ("sp", "pool"):
            for j in range(self.NDS):
                if duse[q][j]:
                    finals.append((("d", q, j), 16 * duse[q][j]))
        for c in range(ncc0, ncc):
            finals.append((("cc", c), 1))

        def run(engname, eng):
            waited = {}

            def wait(sk, val):
                if waited.get(sk, 0) >= val:
                    return
                eng.wait_ge(SEM[sk], val)
                waited[sk] = val

            for op in ops:
                if op["eng"] != engname:
                    continue
                for d in op["deps"]:
                    dop = ops[d]
                    if self._skip(op, dop, d):
                        continue
                    wait(dop["sem"], dop["val"])
                if op["dma"] and op["val"] > 16:
                    wait(op["sem"], op["val"] - 16)
                ins = op["fn"](eng)
                if op["cc"]:
                    ins.then_inc(SEM[op["sem"]])
                elif op["dma"]:
                    ins.then_inc(SEM[op["sem"]], 16)
                elif op["marked"]:
                    ins.then_inc(SEM[op["sem"]], 1)
            for sk, val in finals:
                wait(sk, val)

        block = st.enter_context(nc.Block())

        @block.tensor
        def _(e):
            run("pe", e)

        @block.scalar
        def _(e):
            run("act", e)

        @block.vector
        def _(e):
            run("dve", e)

        @block.gpsimd
        def _(e):
            run("pool", e)

        @block.sync
        def _(e):
            run("sp", e)


class Geo:
    def __init__(self, PL, SL):
        self.PL, self.SL = PL, SL
        self.NCP, self.NCS = PL // 128, SL // 128
        self.NCH = self.NCP + self.NCS
        self.T = PL + SL
        self.TC = self.T + 256
        self.NT = self.T // 512
        self.NDT = self.T // 256
        self.NHB = (PL // 256 + 1) + (SL // 256 + 1)

    def seg(self, n):
        return 0 if n < self.NCP else 1

    def xrow(self, n):
        return n * 128

    def x1row(self, n):
        return n * 128 + (1 if n >= self.NCP else 0)

    def fmcol(self, n):
        return n * 128 + (128 if n >= self.NCP else 0)


def build(PL=2048, SL=4096, debug=False, npasses=6):
    G = Geo(PL, SL)
    T, TC, NCP, NCS, NCH = G.T, G.TC, G.NCP, G.NCS, G.NCH
    nc = bass.Bass("TRN2", target_bir_lowering=False)

    def din(name, shape, dt=F32):
        return nc.dram_tensor(name, list(shape), dt, kind="ExternalInput").ap()

    x_in = din("x", [T, D])
    w_in = din("w_in", [DEPTH, D, INW])
    wba = din("wba", [DEPTH, 512, D])
    wbr = din("wbr", [DEPTH, 512, D])
    wo = din("wo", [DEPTH, D, D])
    wf1 = din("wf1", [DEPTH, D, 2 * DFF])
    wf2 = din("wf2", [DEPTH, DFF, D])
    g_mix = din("g_mix", [DEPTH, D])
    g_ffn = din("g_ffn", [DEPTH, D])
    g_fin = din("g_fin", [1, D])
    g_ret = din("g_ret", [DEPTH, 512])
    sinkd = din("sink", [DEPTH, 8])
    ldf = din("ldf", [DEPTH, 4])
    ldb = din("ldb", [DEPTH, 4])
    cwd = din("cw", [DEPTH, 128, NFC * 3])
    cbd = din("cb", [DEPTH, 128, NFC])
    fmtab = din("fmtab", [128, 4, T])
    tmtab = din("tmtab", [T, 3 * 64])
    NCST = 528 + NCS
    cst = din("cst", [128, NCST])
    masks = din("masks", [128, 2 * 512])
    identd = din("ident", [128, 128])
    rotd = din("rot", [128, 256])

    okind = "ExternalOutput"
    y_out = nc.dram_tensor("y", [T, D], F32, kind=okind).ap()

    def scratch(name, shape, dt):
        if debug:
            return nc.dram_tensor(name, list(shape), dt, kind=okind)
        return nc.dram_tensor(name, list(shape), dt)

    FM_t = scratch("FM", [NFM, 128, TC], BF16)
    TM_t = scratch("TMs", [TC, TMW], BF16)
    RB_t = scratch("RB", [NCH, 128, 512], F32)
    X1_t = scratch("X1", [T + 2, D], F32)
    X2_t = scratch("X2", [T, D], F32)
    FM, TMs, RB, X1, X2 = FM_t.ap(), TM_t.ap(), RB_t.ap(), X1_t.ap(), X2_t.ap()
    PKG1_t = nc.dram_tensor("PKG1", [128, 1024], F32)
    G1_t = nc.dram_tensor("G1", [256, 1024], F32)
    PKGK_t = nc.dram_tensor("PKGK", [128, 512], BF16)
    GK_t = nc.dram_tensor("GK", [256, 512], BF16)
    PKG2_t = nc.dram_tensor("PKG2", [2, D], F32)
    G2_t = nc.dram_tensor("G2", [4, D], F32)
    PKG1, G1, PKGK, GK, PKG2, G2 = (t.ap() for t in (PKG1_t, G1_t, PKGK_t, GK_t, PKG2_t, G2_t))

    C_RELQK, C_RELKQ, C_QP1, C_Q128M = 0, 128, 256, 384
    C_IDXK, C_I127, C_FL, C_FR, C_HFL, C_ONE, C_NW = 512, 513, 514, 515, 516, 520, 528

    uid = [0]

    def sb(st, name, shape, dt):
        uid[0] += 1
        return st.enter_context(nc.sbuf_tensor(f"{name}_u{uid[0]}", list(shape), dt))

    def ps(st, name, shape, dt=F32):
        uid[0] += 1
        return st.enter_context(nc.psum_tensor(f"{name}_u{uid[0]}", list(shape), dt))

    def common(st, S):
        identf = sb(st, "identf", [128, 128], F32)
        ident = sb(st, "ident", [128, 128], BF16)
        CST = sb(st, "CST", [128, NCST], F32)
        S.add("sp", lambda e: e.dma_start(out=identf[:], in_=identd[:, :]), writes=["identf"], dma=True)
        S.add("sp", lambda e: e.dma_start(out=CST[:], in_=cst[:, :]), writes=["CST"], dma=True)
        S.add("dve", lambda e: e.tensor_copy(out=ident[:], in_=identf[:]), reads=["identf"], writes=["ident"])
        return ident, CST

    def load_w(S, dst, src, key, maxc=2048):
        ncols = src.shape[-1]
        c0 = 0
        keys = []
        while c0 < ncols:
            c1 = min(ncols, c0 + maxc)
            S.add("pool", lambda e, a=dst[:, c0:c1], b=src[:, c0:c1]: e.dma_start(out=a, in_=b),
                  writes=[(key, c0)], dma=True)
            keys.append((key, c0))
            c0 = c1
        return keys

    def prep_chunk(S, tag, src_ap, nrows, XTs, xkey, XNs, xnkey, SQJ, SS, ss_col, GB, gkey, TP, tpkey,
                   ident, dst_ap, dkeys, rowscale=None):
        xkeys = xkey if isinstance(xkey, list) else [xkey]
        if src_ap is not None:
            S.add("sp", lambda e: e.dma_start(out=XTs[0:nrows, :], in_=src_ap), writes=xkeys, dma=True)
        if rowscale is not None:
            S.add("dve", lambda e: e.tensor_scalar(out=XTs[:, :], in0=XTs[:, :], scalar1=rowscale, scalar2=None,
                                                   op0=ALU.mult), reads=["CST"], writes=xkeys)
        sc = SS[:, ss_col:ss_col + 1]
        sk = ("SS", tag, ss_col)
        S.add("act", lambda e: e.activation(out=SQJ[0:nrows, :], in_=XTs[0:nrows, :], func=AF.Square,
                                            accum_out=sc[0:nrows, :]),
              reads=xkeys, writes=[("SQJ", tag), sk])
        S.add("dve", lambda e: e.tensor_scalar(out=sc[0:nrows, :], in0=sc[0:nrows, :], scalar1=1.0 / D, scalar2=EPS,
                                               op0=ALU.mult, op1=ALU.add), reads=[sk], writes=[sk])
        S.add("act", lambda e: e.activation(out=sc[0:nrows, :], in_=sc[0:nrows, :], func=AF.Sqrt), reads=[sk], writes=[sk])
        S.add("dve", lambda e: e.reciprocal(out=sc[0:nrows, :], in_=sc[0:nrows, :]), reads=[sk], writes=[sk])
        S.add("dve", lambda e: e.scalar_tensor_tensor(out=XNs[0:nrows, :], in0=XTs[0:nrows, :], scalar=sc[0:nrows, :],
                                                      in1=GB[0:nrows, :], op0=ALU.mult, op1=ALU.mult),
              reads=xkeys + [sk, gkey], writes=[xnkey])
        for k in range(8):
            S.add("pe", lambda e, k=k: e.transpose(out=TP[:, k, 0:nrows], in_=XNs[0:nrows, k * 128:(k + 1) * 128],
                                                   identity=ident[0:nrows, 0:nrows]),
                  reads=[xnkey, "ident"], writes=[tpkey])
        S.add("act", lambda e: e.activation(out=dst_ap, in_=TP[:, :, 0:nrows], func=AF.Copy),
              reads=[tpkey], writes=dkeys)

    def pass_A(l, xsrc):
        with ExitStack() as st:
            S = Sched()
            ident, CST = common(st, S)
            WIN = sb(st, "WIN", [128, 8, INW], BF16)
            GB = sb(st, "GB", [128, D], F32)
            XT = [sb(st, f"XT{i}", [128, D], F32) for i in range(2)]
            XN = [sb(st, f"XN{i}", [128, D], BF16) for i in range(2)]
            SQJ = sb(st, "SQJ", [128, D], BF16)
            SS = sb(st, "SS", [128, 8], F32)
            HT = [sb(st, f"HT{i}", [128, 8, 512], BF16) for i in range(2)]
            TAB = [sb(st, f"TAB{i}", [128, 4, 512], F32) for i in range(2)]
            TMT = [sb(st, f"TMT{i}", [128, 3, 64], F32) for i in range(2)]
            T1 = [sb(st, f"T1_{i}", [128, 512], F32) for i in range(2)]
            T2 = [sb(st, f"T2_{i}", [128, 512], F32) for i in range(2)]
            T1T = [sb(st, f"T1T_{i}", [128, 512], F32) for i in range(2)]
            T2T = [sb(st, f"T2T_{i}", [128, 512], F32) for i in range(2)]
            FMO = [sb(st, f"FMO{i}", [128, 512], BF16) for i in range(2)]
            TMO = [sb(st, f"TMO{i}", [128, TMW], BF16) for i in range(2)]
            TP = [ps(st, "TP0", [128, 8, 128], BF16)] * 2
            PF = [ps(st, f"PF{i}", [128, 512]) for i in range(3)]
            PT = [ps(st, f"PT{i}", [128, 512]) for i in range(3)]
            PR = ps(st, "PR", [128, 512])
            XB = [sb(st, f"XB{i}", [128, 512], BF16) for i in range(2)]
            ROTF = sb(st, "ROTF", [128, 256], F32)
            ROT = sb(st, "ROT", [128, 256], BF16)
            S.add("sp", lambda e: e.dma_start(out=ROTF[:], in_=rotd[:, :]), writes=["ROTF"], dma=True)
            S.add("dve", lambda e: e.tensor_copy(out=ROT[:], in_=ROTF[:]), reads=["ROTF"], writes=["ROT"])

            S.add("sp", lambda e: e.dma_start(out=GB[:], in_=g_mix[l:l + 1, :].partition_broadcast(128)),
                  writes=["GB"], dma=True)
            WINK = {}
            for k in range(8):
                WINK[k] = load_w(S, WIN[:, k, :], w_in[l, k * 128:(k + 1) * 128, :], ("WIN", k), maxc=1792)

            cctr = [0]
            fctr = [0]
            tctr = [0]

            def prep_tile(tt):
                hs = tt % 2
                for c in range(4):
                    n = tt * 4 + c
                    i = cctr[0] % 2
                    cctr[0] += 1
                    r0 = G.xrow(n)
                    prep_chunk(S, "A", xsrc[r0:r0 + 128, :], 128, XT[i], ("XT", i), XN[i], ("XN", i), SQJ, SS, i,
                               GB, "GB", TP[i], ("TP", 0), ident, HT[hs][:, :, c * 128:(c + 1) * 128],
                               [("HT", hs, c)])

            def main_tile(tt):
                hs = tt % 2
                hkeys = [("HT", hs, c) for c in range(4)]
                n0 = tt * 4
                c0 = G.fmcol(n0)
                r0 = G.xrow(n0)
                S.add("sp", lambda e: e.dma_start(out=TAB[hs][:], in_=fmtab[:, :, r0:r0 + 512]),
                      writes=[("TAB", hs)], dma=True)
                for f in range(NFM):
                    pf = fctr[0] % 3
                    p = fctr[0] % 2
                    fctr[0] += 1
                    for k in range(8):
                        S.add("pe", lambda e, k=k, f=f, pf=pf: e.matmul(PF[pf][:], lhsT=WIN[:, k, f * 128:(f + 1) * 128],
                                                                       rhs=HT[hs][:, k, :], start=(k == 0), stop=(k == 7)),
                              reads=WINK[k] + hkeys, writes=[("PF", pf)])
                    att = f < 5
                    ci, si = (0, 1) if att else (2, 3)
                    ro = 0 if att else 128
                    S.add("act", lambda e, p=p, pf=pf: e.activation(out=XB[p][:], in_=PF[pf][:], func=AF.Copy),
                          reads=[("PF", pf)], writes=[("XB", p)])
                    S.add("dve", lambda e, p=p, pf=pf, ci=ci: e.tensor_tensor(out=T1[p][:], in0=PF[pf][:], in1=TAB[hs][:, ci, :],
                                                                             op=ALU.mult),
                          reads=[("PF", pf), ("TAB", hs)], writes=[("T1", p)])
                    S.add("pe", lambda e, p=p, ro=ro: e.matmul(PR[:], lhsT=ROT[:, ro:ro + 128], rhs=XB[p][:], start=True, stop=True),
                          reads=[("XB", p), "ROT"], writes=["PR"])
                    S.add("dve", lambda e, p=p, si=si: e.tensor_tensor(out=T2[p][:], in0=PR[:], in1=TAB[hs][:, si, :], op=ALU.mult),
                          reads=["PR", ("TAB", hs)], writes=[("T2", p)])
                    S.add("pool", lambda e, p=p: e.tensor_tensor(out=FMO[p][:], in0=T1[p][:], in1=T2[p][:], op=ALU.add),
                          reads=[("T1", p), ("T2", p)], writes=[("FMO", p)])
                    S.add("sp", lambda e, p=p, f=f: e.dma_start(out=FM[f, :, c0:c0 + 512], in_=FMO[p][:]),
                          reads=[("FMO", p)], writes=[("FMd", f, tt)], dma=True)
                S.mark(f"A tile {tt} FM done")
                for c in range(4):
                    S.mark(f"A tile {tt} TM chunk {c}")
                    n = n0 + c
                    o_ = n % 2
                    rr = G.xrow(n)
                    S.add("sp", lambda e, o_=o_, rr=rr: e.dma_start(
                        out=TMT[o_][:], in_=tmtab[rr:rr + 128, :].rearrange("p (a b) -> p a b", a=3)),
                          writes=[("TMT", o_)], dma=True)
                    for ct in range(8):
                        p = tctr[0] % 3
                        tctr[0] += 1
                        w = 128 if ct == 7 else 512
                        wc0 = 1664 + ct * 512
                        for k in range(8):
                            S.add("pe", lambda e, k=k, p=p, w=w, wc0=wc0, c=c: e.matmul(
                                PT[p][:, 0:w], lhsT=HT[hs][:, k, c * 128:(c + 1) * 128], rhs=WIN[:, k, wc0:wc0 + w],
                                start=(k == 0), stop=(k == 7)),
                                  reads=WINK[k] + [("HT", hs, c)], writes=[("PT", p)])
                        okey = ("TMO", o_, ct)
                        if ct == 0:
                            q = p % 2
                            pv = PT[p][:].rearrange("p (h t d) -> p h t d", h=4, t=2)
                            t1v = T1T[q][:].rearrange("p (g d) -> p g d", d=64)
                            t2v = T2T[q][:].rearrange("p (h t d) -> p h t d", h=4, t=2)
                            S.add("dve", lambda e, p=p, o_=o_, t1v=t1v: e.tensor_tensor(
                                out=t1v, in0=PT[p][:].rearrange("p (g d) -> p g d", d=64),
                                in1=TMT[o_][:, 0:1, :].broadcast_to([128, 8, 64]), op=ALU.mult),
                                  reads=[("PT", p), ("TMT", o_)], writes=[("T1T", q)])
                            S.add("dve", lambda e, pv=pv, t2v=t2v, o_=o_: e.tensor_tensor(
                                out=t2v[:, :, 0, :], in0=pv[:, :, 1, :],
                                in1=TMT[o_][:, 1:2, :].broadcast_to([128, 4, 64]), op=ALU.mult),
                                  reads=[("PT", p), ("TMT", o_)], writes=[("T2T", q, "a")])
                            S.add("dve", lambda e, pv=pv, t2v=t2v, o_=o_: e.tensor_tensor(
                                out=t2v[:, :, 1, :], in0=pv[:, :, 0, :],
                                in1=TMT[o_][:, 2:3, :].broadcast_to([128, 4, 64]), op=ALU.mult),
                                  reads=[("PT", p), ("TMT", o_)], writes=[("T2T", q, "b")])
                            S.add("pool", lambda e, q=q, o_=o_: e.tensor_tensor(
                                out=TMO[o_][:, KR0:KR0 + 512], in0=T1T[q][:], in1=T2T[q][:], op=ALU.add),
                                  reads=[("T1T", q), ("T2T", q, "a"), ("T2T", q, "b")], writes=[okey])
                        else:
                            func = AF.Copy if ct in (1, 7) else (AF.Silu if ct == 2 else AF.Sigmoid)
                            dc0 = {1: VR0, 2: GR0, 7: VA0}.get(ct, GT0 + (ct - 3) * 512)
                            S.add("act", lambda e, p=p, w=w, dc0=dc0, func=func, o_=o_: e.activation(
                                out=TMO[o_][:, dc0:dc0 + w], in_=PT[p][:, 0:w], func=func),
                                  reads=[("PT", p)], writes=[okey])
                    rc = G.fmcol(n)
                    S.add("sp", lambda e, o_=o_, rc=rc: e.dma_start(out=TMs[rc:rc + 128, :], in_=TMO[o_][:]),
                          reads=[("TMO", o_, ct) for ct in range(8)], writes=[("TMd", n)], dma=True)

            prep_tile(0)
            for tt in range(G.NT):
                if tt + 1 < G.NT:
                    prep_tile(tt + 1)
                main_tile(tt)
            S.emit(nc, st)

    def pass_B(l, xsrc):
        with ExitStack() as st:
            S = Sched()
            ident, CST = common(st, S)
            WBA = sb(st, "WBA", [64, 8, D], BF16)
            WBR = sb(st, "WBR", [128, 4, D], BF16)
            WO = sb(st, "WO", [128, 8, D], BF16)
            MK = sb(st, "MK", [128, 2, 512], BF16)
            MKF = sb(st, "MKF", [128, 2, 512], BF16)
            LFB = sb(st, "LFB", [128, 8], F32)
            ZET = sb(st, "ZET", [128, 8], F32)
            CDC = sb(st, "CDC", [128, 8], F32)
            ESK = sb(st, "ESK", [128, 8], F32)
            DTOT = sb(st, "DTOT", [128, 4, 128], BF16)
            DTMP = sb(st, "DTMP", [128, 128], F32)
            DTMP2 = sb(st, "DTMP2", [128, 128], F32)
            XIF = sb(st, "XIF", [128, 4, 128], F32)
            XIB = sb(st, "XIB", [128, 4, 128], F32)
            WBW = sb(st, "WBW", [128, NCS, 4], F32)
            NG = sb(st, "NG", [128, 512], F32)
            SINF = sb(st, "SINF", [128, 512], F32)
            SINB = sb(st, "SINB", [128, 512], F32)
            RBST = [sb(st, f"RBST{i}", [128, 512], F32) for i in range(2)]
            RFST = [sb(st, f"RFST{i}", [128, 512], F32) for i in range(2)]
            KVL = [sb(st, f"KVL{i}", [128, 1024], BF16) for i in range(2)]
            VZ = [sb(st, f"VZ{i}", [128, 512], BF16) for i in range(2)]
            QA = [sb(st, f"QA{i}", [128, 4, 128], BF16) for i in range(2)]
            KA = [sb(st, f"KA{i}", [128, 384], BF16) for i in range(2)]
            VA1 = [sb(st, f"VA1_{i}", [128, 3, 2, 128], BF16) for i in range(2)]
            QR = [sb(st, f"QR{i}", [128, 4, 128], BF16) for i in range(2)]
            KR = [sb(st, f"KR{i}", [128, 4, 128], BF16) for i in range(2)]
            GG = [sb(st, f"GG{i}", [128, 2560], BF16) for i in range(3)]
            RBL = [sb(st, f"RBL{i}", [128, 512], F32) for i in range(2)]
            XR = [sb(st, f"XR{i}", [128, D], F32) for i in range(3)]
            PTs = [sb(st, f"PTs{i}", [128, 512], BF16) for i in range(4)]
            def two(name, shape, dt):
                return [sb(st, f"{name}{i}", shape, dt) for i in range(2)]
            DENs = two("DEN", [64, 512], F32)
            ATTs = [[sb(st, f"ATT{i}_{k}", [64, 4, 128], BF16) for k in range(2)] for i in range(2)]
            INMs = two("INM", [128, 4, 128], BF16)
            QXFs = two("QXF", [128, 4, 128], BF16)
            QXBs = two("QXB", [128, 4, 128], BF16)
            RFBs = two("RFB", [128, 512], BF16)
            RBBs = two("RBB", [128, 512], BF16)
            RBTs = two("RBT", [128, 512], F32)
            SQs = two("SQ", [128, 512], F32)
            SSRs = two("SSR", [128, 4], F32)
            TRs = two("TR", [128, 512], F32)
            T2Rs = two("T2R", [128, 512], F32)
            RETs = two("RET", [128, 512], BF16)
            RETTs = two("RETT", [128, 4, 128], BF16)
            M1s = two("M1", [128, 512], F32)
            M2s = two("M2", [128, 512], F32)
            MERs = two("MER", [128, D], BF16)
            MTs = two("MT", [128, 8, 128], BF16)
            SP = [ps(st, f"SP{i}", [128, 512]) for i in range(2)]
            OP = ps(st, "OP", [128, 512])
            IP = ps(st, "IP", [128, 4, 128])
            ORp = ps(st, "ORp", [128, 4, 128])
            TPB = ps(st, "TPB", [128, 8, 128], BF16)
            APs = ps(st, "APs", [128, 512])
            RPs = ps(st, "RPs", [128, 512])

            WBAK, WBRK, WOK = {}, {}, {}
            for h in range(8):
                WBAK[h] = load_w(S, WBA[:, h, :], wba[l, h * 64:(h + 1) * 64, :], ("WBA", h), maxc=1024)
            for h in range(4):
                WBRK[h] = load_w(S, WBR[:, h, :], wbr[l, h * 128:(h + 1) * 128, :], ("WBR", h), maxc=1024)
            for k in range(8):
                WOK[k] = load_w(S, WO[:, k, :], wo[l, k * 128:(k + 1) * 128, :], ("WO", k), maxc=1024)
            S.add("pool", lambda e: e.dma_start(out=MK[:].rearrange("p a b -> p (a b)"), in_=masks[:, :]),
                  writes=["MK"], dma=True)
            S.add("sp", lambda e: e.dma_start(out=LFB[:, 0:4], in_=ldf[l:l + 1, :].partition_broadcast(128)),
                  writes=["LFB"], dma=True)
            S.add("sp", lambda e: e.dma_start(out=LFB[:, 4:8], in_=ldb[l:l + 1, :].partition_broadcast(128)),
                  writes=["LFB"], dma=True)
            S.add("sp", lambda e: e.dma_start(out=ESK[:], in_=sinkd[l:l + 1, :].partition_broadcast(128)),
                  writes=["ESK"], dma=True)
            S.add("sp", lambda e: e.dma_start(out=NG[:], in_=g_ret[l:l + 1, :].partition_broadcast(128)),
                  writes=["NG"], dma=True)
            S.add("act", lambda e: e.activation(out=ESK[:], in_=ESK[:], func=AF.Exp), reads=["ESK"], writes=["ESK"])
            for j, cf in ((0, C_FL), (1, C_FR)):
                S.add("dve", lambda e, j=j, cf=cf: e.tensor_scalar(out=MKF[:, j, :], in0=MK[:, j, :],
                                                                  scalar1=CST[:, cf:cf + 1], scalar2=None, op0=ALU.mult),
                      reads=["MK", "CST"], writes=[("MKF", j)])
            KSC = 128.0 ** -0.5
            S.add("act", lambda e: e.activation(out=ZET[:, 0:4], in_=LFB[:, 0:4], func=AF.Exp,
                                                scale=CST[:, C_I127:C_I127 + 1]), reads=["LFB", "CST"], writes=["ZETa"])
            S.add("act", lambda e: e.activation(out=ZET[:, 4:8], in_=LFB[:, 4:8], func=AF.Exp,
                                                scale=CST[:, C_IDXK:C_IDXK + 1]), reads=["LFB", "CST"], writes=["ZETb"])
            S.add("dve", lambda e: e.tensor_scalar(out=ZET[:], in0=ZET[:], scalar1=KSC, scalar2=None, op0=ALU.mult),
                  reads=["ZETa", "ZETb"], writes=["ZET"])
            S.add("act", lambda e: e.activation(out=CDC[:], in_=LFB[:], func=AF.Exp, scale=128.0),
                  reads=["LFB"], writes=["CDC"])
            for h in range(4):
                S.add("dve", lambda e, h=h: e.tensor_scalar(out=DTMP[:], in0=CST[:, C_RELQK:C_RELQK + 128],
                                                           scalar1=LFB[:, h:h + 1], scalar2=None, op0=ALU.mult),
                      reads=["CST", "LFB"], writes=["DTMP"])
                S.add("dve", lambda e, h=h: e.scalar_tensor_tensor(out=DTMP2[:], in0=CST[:, C_RELKQ:C_RELKQ + 128],
                                                                  scalar=LFB[:, 4 + h:5 + h], in1=DTMP[:],
                                                                  op0=ALU.mult, op1=ALU.add),
                      reads=["CST", "LFB", "DTMP"], writes=["DTMP2"])
                S.add("act", lambda e: e.activation(out=DTMP[:], in_=DTMP2[:], func=AF.Exp),
                      reads=["DTMP2"], writes=["DTMP"])
                S.add("dve", lambda e, h=h: e.tensor_scalar(out=DTOT[:, h, :], in0=DTMP[:], scalar1=KSC, scalar2=None,
                                                           op0=ALU.mult), reads=["DTMP"], writes=[("DTOT", h)])
                S.add("act", lambda e, h=h: e.activation(out=XIF[:, h, :], in_=CST[:, C_QP1:C_QP1 + 128], func=AF.Exp,
                                                        scale=LFB[:, h:h + 1]), reads=["CST", "LFB"], writes=[("XIF", h)])
                S.add("act", lambda e, h=h: e.activation(out=XIB[:, h, :], in_=CST[:, C_Q128M:C_Q128M + 128], func=AF.Exp,
                                                        scale=LFB[:, 4 + h:5 + h]), reads=["CST", "LFB"], writes=[("XIB", h)])
            DTK = [("DTOT", h) for h in range(4)]
            XIFK = [("XIF", h) for h in range(4)]
            XIBK = [("XIB", h) for h in range(4)]
            S.add("dve", lambda e: e.tensor_tensor(out=WBW[:], in0=CST[:, C_NW:C_NW + NCS].unsqueeze(2).broadcast_to([128, NCS, 4]),
                                                   in1=LFB[:, 4:8].unsqueeze(1).broadcast_to([128, NCS, 4]), op=ALU.mult),
                  reads=["CST", "LFB"], writes=["WBW"])
            S.add("act", lambda e: e.activation(out=WBW[:], in_=WBW[:], func=AF.Exp), reads=["WBW"], writes=["WBW"])
            for i in range(2):
                S.add("dve", lambda e, i=i: e.memset(VA1[i][:, :, :, 64:128], 1.0), writes=[("VA1o", i)])

            def v4(t):
                return t[:].rearrange("p (h e) -> p h e", h=4)

            def bc4(col_ap):
                return col_ap.unsqueeze(2).broadcast_to([128, 4, 128])

            ldc = [0]

            def scan(chunks, direction, ST, skey, store_rb):
                cur = 0
                first = True
                for n in chunks:
                    if first:
                        S.add("dve", lambda e, cur=cur: e.memset(ST[cur][:], 0.0), writes=[(skey, cur)])
                        first = False
                    if store_rb:
                        S.add("sp", lambda e, n=n, cur=cur: e.dma_start(out=RB[n, :, :], in_=ST[cur][:]),
                              reads=[(skey, cur)], writes=[("RB", n)], dma=True)
                    i = ldc[0] % 2
                    ldc[0] += 1
                    rc = G.fmcol(n)
                    S.add("sp", lambda e, i=i, rc=rc: e.dma_start(out=KVL[i][:], in_=TMs[rc:rc + 128, 0:1024]),
                          writes=[("KVL", i)], dma=True)
                    zc = 0 if direction == "f" else 4
                    S.add("dve", lambda e, i=i, zc=zc: e.tensor_tensor(
                        out=v4(VZ[i]), in0=KVL[i][:, 512:1024].rearrange("p (h e) -> p h e", h=4),
                        in1=bc4(ZET[:, zc:zc + 4]), op=ALU.mult),
                          reads=[("KVL", i), "ZET"], writes=[("VZ", i)])
                    for h in range(4):
                        S.add("pe", lambda e, i=i, h=h: e.matmul(IP[:, h, :], lhsT=KVL[i][:, h * 128:(h + 1) * 128],
                                                                rhs=VZ[i][:, h * 128:(h + 1) * 128], start=True, stop=True),
                              reads=[("KVL", i), ("VZ", i)], writes=["IP"])
                    nxt = 1 - cur
                    S.add("dve", lambda e, cur=cur, nxt=nxt, zc=zc: e.tensor_tensor(
                        out=v4(ST[nxt]), in0=v4(ST[cur]), in1=bc4(CDC[:, zc:zc + 4]), op=ALU.mult),
                          reads=[(skey, cur), "CDC"], writes=[(skey, nxt)])
                    S.add("dve", lambda e, nxt=nxt: e.tensor_tensor(
                        out=ST[nxt][:], in0=ST[nxt][:], in1=IP[:].rearrange("p h e -> p (h e)"), op=ALU.add),
                          reads=["IP", (skey, nxt)], writes=[(skey, nxt)])
                    cur = nxt
                return cur

            cur = scan(list(range(NCH - 1, NCP - 1, -1)), "b", RBST, "RBST", True)
            S.add("sp", lambda e, cur=cur: e.dma_start(out=PKG1[:, 512:1024], in_=RBST[cur][:]),
                  reads=[("RBST", cur)], writes=["PKG1"], dma=True)
            cur = scan(list(range(NCP, NCH)), "f", RFST, "RFST", False)
            S.add("sp", lambda e, cur=cur: e.dma_start(out=PKG1[:, 0:512], in_=RFST[cur][:]),
                  reads=[("RFST", cur)], writes=["PKG1"], dma=True)
            cS0 = G.fmcol(NCP)
            cS1 = G.fmcol(NCH - 1)
            S.add("sp", lambda e: e.dma_start(out=PKGK[:, 0:128], in_=FM[4, :, cS0:cS0 + 128]), writes=["PKGK"], dma=True)
            S.add("sp", lambda e: e.dma_start(out=PKGK[:, 128:256], in_=FM[4, :, cS1:cS1 + 128]), writes=["PKGK"], dma=True)
            S.add("sp", lambda e: e.dma_start(out=PKGK[:, 256:384], in_=TMs[cS0:cS0 + 128, VA0:VA0 + 128]), writes=["PKGK"], dma=True)
            S.add("sp", lambda e: e.dma_start(out=PKGK[:, 384:512], in_=TMs[cS1:cS1 + 128, VA0:VA0 + 128]), writes=["PKGK"], dma=True)
            S.add("pool", lambda e: e.collective_compute("AllGather", ALU.bypass, replica_groups=GROUPS,
                                                         ins=[PKG1_t.ap().opt()], outs=[G1_t.ap().opt()]),
                  reads=["PKG1"], writes=["G1"], cc=True)
            S.add("pool", lambda e: e.collective_compute("AllGather", ALU.bypass, replica_groups=GROUPS,
                                                         ins=[PKGK_t.ap().opt()], outs=[GK_t.ap().opt()]),
                  reads=["PKGK"], writes=["GK"], cc=True)
            cur = scan(list(range(NCP - 1, -1, -1)), "b", RBST, "RBST", True)

            def unpack():
                cL = PL
                cR = PL + 128 + SL
                S.add("sp", lambda e: e.dma_start(out=FM[4, :, cL:cL + 128], in_=GK[0:128, 128:256]),
                      reads=["GK"], writes=[("FMh", "L")], dma=True)
                S.add("sp", lambda e: e.dma_start(out=FM[4, :, cR:cR + 128], in_=GK[128:256, 0:128]),
                      reads=["GK"], writes=[("FMh", "R")], dma=True)
                S.add("sp", lambda e: e.dma_start(out=TMs[cL:cL + 128, VA0:VA0 + 128], in_=GK[0:128, 384:512]),
                      reads=["GK"], writes=[("TMh", "L")], dma=True)
                S.add("sp", lambda e: e.dma_start(out=TMs[cR:cR + 128, VA0:VA0 + 128], in_=GK[128:256, 256:384]),
                      reads=["GK"], writes=[("TMh", "R")], dma=True)
                S.add("sp", lambda e: e.dma_start(out=SINF[:], in_=G1[0:128, 0:512]), reads=["G1"], writes=["SINF"], dma=True)
                S.add("sp", lambda e: e.dma_start(out=SINB[:], in_=G1[128:256, 512:1024]), reads=["G1"], writes=["SINB"], dma=True)
                S.add("dve", lambda e: e.tensor_scalar(out=SINF[:], in0=SINF[:], scalar1=CST[:, C_FL:C_FL + 1], scalar2=None,
                                                       op0=ALU.mult), reads=["SINF", "CST"], writes=["SINF"])
                S.add("dve", lambda e: e.tensor_scalar(out=SINB[:], in0=SINB[:], scalar1=CST[:, C_FR:C_FR + 1], scalar2=None,
                                                       op0=ALU.mult), reads=["SINB", "CST"], writes=["SINB"])

            def loads(n):
                i = n % 2
                sg = G.seg(n)
                c0 = G.fmcol(n)
                lo = NCP if sg else 0
                hi = NCH if sg else NCP
                jl = 0 if (n > lo or sg == 1) else 1
                jh = 2 if (n < hi - 1 or sg == 1) else 1
                extra = []
                if sg == 1 and n == lo:
                    extra = [("FMh", "L"), ("TMh", "L")]
                if sg == 1 and n == hi - 1:
                    extra = extra + [("FMh", "R"), ("TMh", "R")]
                S.add("sp", lambda e: e.dma_start(out=QA[i][:], in_=FM[0:4, :, c0:c0 + 128].rearrange("f p t -> p f t")),
                      writes=[("QA", i)], dma=True)
                ka0 = c0 + (jl - 1) * 128
                nk = (jh - jl + 1) * 128
                S.add("sp", lambda e: e.dma_start(out=KA[i][:, jl * 128:jl * 128 + nk], in_=FM[4, :, ka0:ka0 + nk]),
                      reads=extra, writes=[("KA", i)], dma=True)
                for j in range(jl, jh + 1):
                    rj = c0 + (j - 1) * 128
                    S.add("sp", lambda e, j=j, rj=rj: e.dma_start(
                        out=VA1[i][:, j, :, 0:64], in_=TMs[rj:rj + 128, VA0:VA0 + 128].rearrange("p (k d) -> p k d", k=2)),
                          reads=extra, writes=[("VA1", i, j)], dma=True)
                S.add("sp", lambda e: e.dma_start(out=QR[i][:], in_=FM[5:9, :, c0:c0 + 128].rearrange("f p t -> p f t")),
                      writes=[("QR", i)], dma=True)
                S.add("sp", lambda e: e.dma_start(out=KR[i][:], in_=FM[9:13, :, c0:c0 + 128].rearrange("f p t -> p f t")),
                      writes=[("KR", i)], dma=True)
                S.add("sp", lambda e: e.dma_start(out=KVL[i][:], in_=TMs[c0:c0 + 128, 0:1024]), writes=[("KVL", i)], dma=True)
                i3 = n % 3
                S.add("sp", lambda e: e.dma_start(out=GG[i3][:], in_=TMs[c0:c0 + 128, GR0:GR0 + 2560]), writes=[("GG", i3)], dma=True)
                S.add("sp", lambda e: e.dma_start(out=RBL[i][:], in_=RB[n, :, :]), reads=[("RB", n)], writes=[("RBL", i)], dma=True)
                r0 = G.xrow(n)
                S.add("sp", lambda e: e.dma_start(out=XR[i3][:], in_=xsrc[r0:r0 + 128, :]), writes=[("XR", i3)], dma=True)
                return jl, jh

            rf = [0]
            ptc = [0]
            spc = [0]

            def phase1a(n, jl, jh):
                i = n % 2
                sg = G.seg(n)
                lo = NCP if sg else 0
                hi = NCH if sg else NCP
                DEN, ATT = DENs[i], ATTs[i]
                for kvh in range(2):
                    pts = []
                    for j in range(jl, jh + 1):
                        sp_ = spc[0] % 2
                        spc[0] += 1
                        pt = ptc[0] % 4
                        ptc[0] += 1
                        pts.append((j, pt))
                        S.add("pe", lambda e, j=j, sp_=sp_, kvh=kvh: e.matmul(
                            SP[sp_][:], lhsT=KA[i][kvh * 64:(kvh + 1) * 64, j * 128:(j + 1) * 128],
                            rhs=QA[i][kvh * 64:(kvh + 1) * 64, :, :].rearrange("p g q -> p (g q)"), start=True, stop=True),
                              reads=[("KA", i), ("QA", i)], writes=[("SP", sp_)])
                        S.add("act", lambda e, sp_=sp_, pt=pt: e.activation(out=PTs[pt][:], in_=SP[sp_][:], func=AF.Exp, scale=0.125),
                              reads=[("SP", sp_)], writes=[("PTs", pt)])
                        if j != 1:
                            mj = 0 if j == 0 else 1
                            edge = sg == 1 and ((j == 0 and n == lo) or (j == 2 and n == hi - 1))
                            msk = MKF if edge else MK
                            mkey = ("MKF", mj) if edge else "MK"
                            S.add("pool", lambda e, pt=pt, msk=msk, mj=mj: e.tensor_tensor(
                                out=PTs[pt][:], in0=PTs[pt][:], in1=msk[:, mj, :], op=ALU.mult),
                                  reads=[("PTs", pt), mkey], writes=[("PTs", pt)])
                    for idx, (j, pt) in enumerate(pts):
                        S.add("pe", lambda e, j=j, pt=pt, kvh=kvh, idx=idx, np_=len(pts): e.matmul(
                            OP[:], lhsT=VA1[i][:, j, kvh, :], rhs=PTs[pt][:], start=(idx == 0), stop=(idx == np_ - 1)),
                              reads=[("VA1", i, j), ("VA1o", i), ("PTs", pt)], writes=["OP"])
                    S.add("dve", lambda e, kvh=kvh: e.tensor_tensor(
                        out=DEN[:].rearrange("p (g q) -> p g q", g=4), in0=OP[64:128, :].rearrange("p (g q) -> p g q", g=4),
                        in1=ESK[0:64, kvh * 4:(kvh + 1) * 4].unsqueeze(2).broadcast_to([64, 4, 128]), op=ALU.add),
                          reads=["OP", "ESK"], writes=[("DEN", i)])
                    S.add("dve", lambda e: e.reciprocal(out=DEN[:], in_=DEN[:]), reads=[("DEN", i)], writes=[("DEN", i)])
                    S.add("dve", lambda e, kvh=kvh: e.tensor_tensor(
                        out=ATT[kvh][:].rearrange("p g q -> p (g q)"), in0=OP[0:64, :], in1=DEN[:], op=ALU.mult),
                          reads=["OP", ("DEN", i)], writes=[("ATT", i, kvh)])
            def phase1r(n):
                i = n % 2
                sg = G.seg(n)
                lo = NCP if sg else 0
                hi = NCH if sg else NCP
                INM, QXF, QXB, RFB, RBB, RBT = INMs[i], QXFs[i], QXBs[i], RFBs[i], RBBs[i], RBTs[i]
                SQ, SSR, TR, T2R, RET = SQs[i], SSRs[i], TRs[i], T2Rs[i], RETs[i]
                if n == lo:
                    if sg == 0:
                        S.add("dve", lambda e, c=rf[0]: e.memset(RFST[c][:], 0.0), writes=[("RFST", rf[0])])
                    else:
                        S.add("dve", lambda e, c=rf[0]: e.tensor_copy(out=RFST[c][:], in_=SINF[:]),
                              reads=["SINF"], writes=[("RFST", rf[0])])
                cur = rf[0]
                for h in range(4):
                    S.add("pe", lambda e, h=h: e.matmul(IP[:, h, :], lhsT=KR[i][:, h, :], rhs=QR[i][:, h, :], start=True, stop=True),
                          reads=[("KR", i), ("QR", i)], writes=["IP"])
                S.add("dve", lambda e: e.tensor_tensor(out=INM[:], in0=IP[:], in1=DTOT[:], op=ALU.mult),
                      reads=["IP"] + DTK, writes=[("INM", i)])
                S.add("pool", lambda e: e.tensor_tensor(out=QXF[:], in0=QR[i][:], in1=XIF[:], op=ALU.mult),
                      reads=[("QR", i)] + XIFK, writes=[("QXF", i)])
                S.add("pool", lambda e: e.tensor_tensor(out=QXB[:], in0=QR[i][:], in1=XIB[:], op=ALU.mult),
                      reads=[("QR", i)] + XIBK, writes=[("QXB", i)])
                S.add("act", lambda e, cur=cur: e.activation(out=RFB[:], in_=RFST[cur][:], func=AF.Copy),
                      reads=[("RFST", cur)], writes=[("RFB", i)])
                if sg == 0:
                    S.add("act", lambda e: e.activation(out=RBB[:], in_=RBL[i][:], func=AF.Copy),
                          reads=[("RBL", i)], writes=[("RBB", i)])
                else:
                    jloc = n - lo
                    S.add("pool", lambda e, jloc=jloc: e.tensor_tensor(out=v4(RBT), in0=v4(SINB), in1=bc4(WBW[:, jloc, :]), op=ALU.mult),
                          reads=["SINB", "WBW"], writes=[("RBT", i)])
                    S.add("pool", lambda e: e.tensor_tensor(out=RBB[:], in0=RBT[:], in1=RBL[i][:], op=ALU.add),
                          reads=[("RBT", i), ("RBL", i)], writes=[("RBB", i)])
                for h in range(4):
                    hs_ = slice(h * 128, (h + 1) * 128)
                    S.add("pe", lambda e, h=h, hs_=hs_: e.matmul(ORp[:, h, :], lhsT=INM[:, h, :], rhs=KVL[i][:, 512 + h * 128:512 + (h + 1) * 128],
                                                               start=True, stop=False),
                          reads=[("INM", i), ("KVL", i)], writes=["ORp"])
                    S.add("pe", lambda e, h=h, hs_=hs_: e.matmul(ORp[:, h, :], lhsT=QXF[:, h, :], rhs=RFB[:, hs_], start=False, stop=False),
                          reads=[("QXF", i), ("RFB", i)], writes=["ORp"])
                    S.add("pe", lambda e, h=h, hs_=hs_: e.matmul(ORp[:, h, :], lhsT=QXB[:, h, :], rhs=RBB[:, hs_], start=False, stop=True),
                          reads=[("QXB", i), ("RBB", i)], writes=["ORp"])
                vz = i
                S.add("pool", lambda e: e.tensor_tensor(out=v4(VZ[vz]), in0=KVL[i][:, 512:1024].rearrange("p (h e) -> p h e", h=4),
                                                        in1=bc4(ZET[:, 0:4]), op=ALU.mult),
                      reads=[("KVL", i), "ZET"], writes=[("VZ", vz)])
                S.add("act", lambda e: e.activation(out=SQ[:], in_=ORp[:].rearrange("p h e -> p (h e)"), func=AF.Square),
                      reads=["ORp"], writes=[("SQ", i)])
                S.add("dve", lambda e: e.tensor_tensor(out=v4(TR), in0=ORp[:], in1=bc4(CST[:, C_ONE:C_ONE + 4]), op=ALU.mult),
                      reads=["ORp", "CST"], writes=[("TR", i)])
                for h in range(4):
                    S.add("pe", lambda e, h=h: e.matmul(IP[:, h, :], lhsT=KVL[i][:, h * 128:(h + 1) * 128],
                                                        rhs=VZ[vz][:, h * 128:(h + 1) * 128], start=True, stop=True),
                          reads=[("KVL", i), ("VZ", vz)], writes=["IP"])
                nxt = 1 - cur
                S.add("pool", lambda e, cur=cur, nxt=nxt: e.tensor_tensor(out=v4(RFST[nxt]), in0=v4(RFST[cur]), in1=bc4(CDC[:, 0:4]), op=ALU.mult),
                      reads=[("RFST", cur), "CDC"], writes=[("RFST", nxt)])
                S.add("dve", lambda e, nxt=nxt: e.tensor_tensor(out=RFST[nxt][:], in0=RFST[nxt][:], in1=IP[:].rearrange("p h e -> p (h e)"), op=ALU.add),
                      reads=["IP", ("RFST", nxt)], writes=[("RFST", nxt)])
                rf[0] = nxt
                S.add("dve", lambda e: e.tensor_reduce(out=SSR[:], in_=v4(SQ), axis=mybir.AxisListType.X, op=ALU.add),
                      reads=[("SQ", i)], writes=[("SSR", i)])
                S.add("dve", lambda e: e.tensor_scalar(out=SSR[:], in0=SSR[:], scalar1=1.0 / 128, scalar2=EPS, op0=ALU.mult, op1=ALU.add),
                      reads=[("SSR", i)], writes=[("SSR", i)])
                S.add("act", lambda e: e.activation(out=SSR[:], in_=SSR[:], func=AF.Sqrt), reads=[("SSR", i)], writes=[("SSR", i)])
                S.add("dve", lambda e: e.reciprocal(out=SSR[:], in_=SSR[:]), reads=[("SSR", i)], writes=[("SSR", i)])
                S.add("pool", lambda e: e.tensor_tensor(out=T2R[:], in0=NG[:], in1=GG[n % 3][:, 0:512], op=ALU.mult),
                      reads=["NG", ("GG", n % 3)], writes=[("T2R", i)])
                S.add("pool", lambda e: e.tensor_tensor(out=v4(TR), in0=v4(TR), in1=bc4(SSR[:, 0:4]), op=ALU.mult),
                      reads=[("TR", i), ("SSR", i)], writes=[("TR", i)])
                S.add("pool", lambda e: e.tensor_tensor(out=RET[:], in0=TR[:], in1=T2R[:], op=ALU.mult),
                      reads=[("TR", i), ("T2R", i)], writes=[("RET", i)])

            def phase2(n):
                i = n % 2
                i3 = n % 3
                ATT, RET, RETT, M1, M2, MER, MT = ATTs[i], RETs[i], RETTs[i], M1s[i], M2s[i], MERs[i], MTs[i]
                for h in range(4):
                    S.add("pe", lambda e, h=h: e.transpose(out=TPB[:, h, :], in_=RET[:, h * 128:(h + 1) * 128], identity=ident[:]),
                          reads=[("RET", i), "ident"], writes=["TPB"])
                S.add("act", lambda e: e.activation(out=RETT[:], in_=TPB[:, 0:4, :], func=AF.Copy), reads=["TPB"], writes=[("RETT", i)])
                for ct in range(2):
                    cs = slice(ct * 512, (ct + 1) * 512)
                    for hh in range(8):
                        kvh, g = hh // 4, hh % 4
                        S.add("pe", lambda e, hh=hh, kvh=kvh, g=g, cs=cs: e.matmul(APs[:], lhsT=ATT[kvh][:, g, :], rhs=WBA[:, hh, cs],
                                                                                 start=(hh == 0), stop=(hh == 7)),
                              reads=[("ATT", i, kvh)] + WBAK[hh], writes=["APs"])
                    for h in range(4):
                        S.add("pe", lambda e, h=h, cs=cs: e.matmul(RPs[:], lhsT=RETT[:, h, :], rhs=WBR[:, h, cs], start=(h == 0), stop=(h == 3)),
                              reads=[("RETT", i)] + WBRK[h], writes=["RPs"])
                    S.add("dve", lambda e, ct=ct: e.tensor_tensor(out=M1[:], in0=APs[:], in1=GG[i3][:, 512 + ct * 512:1024 + ct * 512], op=ALU.mult),
                          reads=["APs", ("GG", i3)], writes=[("M1", i)])
                    S.add("dve", lambda e, ct=ct: e.tensor_tensor(out=M2[:], in0=RPs[:], in1=GG[i3][:, 1536 + ct * 512:2048 + ct * 512], op=ALU.mult),
                          reads=["RPs", ("GG", i3)], writes=[("M2", i)])
                    S.add("pool", lambda e, cs=cs: e.tensor_tensor(out=MER[:, cs], in0=M1[:], in1=M2[:], op=ALU.add),
                          reads=[("M1", i), ("M2", i)], writes=[("MER", i, ct)])
                for k in range(8):
                    S.add("pe", lambda e, k=k: e.transpose(out=TPB[:, k, :], in_=MER[:, k * 128:(k + 1) * 128], identity=ident[:]),
                          reads=[("MER", i, k // 4), "ident"], writes=["TPB"])
                S.add("act", lambda e: e.activation(out=MT[:], in_=TPB[:], func=AF.Copy), reads=["TPB"], writes=[("MT", i)])
                for ct in range(2):
                    cs = slice(ct * 512, (ct + 1) * 512)
                    yp, ypk = (APs, "APs") if ct == 0 else (RPs, "RPs")
                    for k in range(8):
                        S.add("pe", lambda e, k=k, cs=cs, yp=yp: e.matmul(yp[:], lhsT=MT[:, k, :], rhs=WO[:, k, cs], start=(k == 0), stop=(k == 7)),
                              reads=[("MT", i)] + WOK[k], writes=[ypk])
                    S.add("dve", lambda e, cs=cs, yp=yp: e.tensor_tensor(out=XR[i3][:, cs], in0=yp[:], in1=XR[i3][:, cs], op=ALU.add),
                          reads=[ypk, ("XR", i3)], writes=[("XR", i3)])
                r1 = G.x1row(n)
                S.add("sp", lambda e: e.dma_start(out=X1[r1:r1 + 128, :], in_=XR[i3][:]), reads=[("XR", i3)], writes=[("X1", n)], dma=True)

            jj = {}
            jj[0] = loads(0)
            for n in range(NCH):
                if n + 1 < NCH:
                    if n + 1 == NCP:
                        unpack()
                    jj[n + 1] = loads(n + 1)
                streams = []
                for f_ in ((lambda: phase1a(n, *jj[n])), (lambda: phase1r(n)), ((lambda: phase2(n - 1)) if n >= 1 else None)):
                    if f_ is None:
                        continue
                    S.cap = []
                    f_()
                    streams.append(S.cap)
                    S.cap = None
                S.interleave(streams)
            phase2(NCH - 1)
            S.emit(nc, st)

    def pass_D(l, last):
        with ExitStack() as st:
            S = Sched()
            ident, CST = common(st, S)
            WF1 = sb(st, "WF1", [128, 8, 2 * DFF], BF16)
            WF2 = sb(st, "WF2", [128, NFC, D], BF16)
            GB = sb(st, "GBF", [128, D], F32)
            GBL = sb(st, "GBL", [128, D], F32)
            CW = sb(st, "CW", [128, NFC * 3], F32)
            CB = sb(st, "CB", [128, NFC], F32)
            XT = [sb(st, f"XT{i}", [128, D], F32) for i in range(4)]
            XN = [sb(st, f"XN{i}", [128, D], BF16) for i in range(2)]
            SQJ = sb(st, "SQJ", [128, D], BF16)
            SS = sb(st, "SS", [128, 16], F32)
            H2T = [sb(st, f"H2T{i}", [128, 8, 258], BF16) for i in range(2)]
            HALOX = sb(st, "HALOX", [128, D], F32)
            HALOT = sb(st, "HALOT", [128, 8, 128], BF16)
            GU = [sb(st, f"GU{i}", [128, 256], BF16) for i in range(3)]
            C1 = [sb(st, f"C1_{i}", [128, 256], F32) for i in range(2)]
            C2 = [sb(st, f"C2_{i}", [128, 256], F32) for i in range(2)]
            C3 = [sb(st, f"C3_{i}", [128, 256], F32) for i in range(2)]
            GE = [sb(st, f"GE{i}", [128, 256], F32) for i in range(2)]
            AS = [sb(st, f"AS{i}", [128, 258], F32) for i in range(2)]
            US = [sb(st, f"US{i}", [128, 256], F32) for i in range(2)]
            TP = ps(st, "TP", [128, 8, 128], BF16)
            PA = [ps(st, f"PA{i}", [128, 512]) for i in range(2)]
            PU = ps(st, "PU", [128, 512])
            PY = [ps(st, f"PY{i}", [128, 512]) for i in range(4)]

            S.add("sp", lambda e: e.dma_start(out=GB[:], in_=g_ffn[l:l + 1, :].partition_broadcast(128)), writes=["GB"], dma=True)
            S.add("sp", lambda e: e.dma_start(out=GBL[:], in_=g_fin[0:1, :].partition_broadcast(128)), writes=["GBL"], dma=True)
            S.add("sp", lambda e: e.dma_start(out=CW[:], in_=cwd[l, :, :]), writes=["CW"], dma=True)
            S.add("sp", lambda e: e.dma_start(out=CB[:], in_=cbd[l, :, :]), writes=["CB"], dma=True)
            rS0 = G.x1row(NCP)
            rS1 = G.x1row(NCH - 1) + 127
            S.add("sp", lambda e: e.dma_start(out=PKG2[0:1, :], in_=X1[rS0:rS0 + 1, :]), writes=["PKG2"], dma=True)
            S.add("sp", lambda e: e.dma_start(out=PKG2[1:2, :], in_=X1[rS1:rS1 + 1, :]), writes=["PKG2"], dma=True)
            S.add("pool", lambda e: e.collective_compute("AllGather", ALU.bypass, replica_groups=GROUPS,
                                                         ins=[PKG2_t.ap().opt()], outs=[G2_t.ap().opt()]),
                  reads=["PKG2"], writes=["G2"], cc=True)
            WF1K, WF2K = {}, {}
            for k in range(8):
                WF1K[k] = load_w(S, WF1[:, k, :], wf1[l, k * 128:(k + 1) * 128, :], ("WF1", k), maxc=1408)
            for fc in range(NFC):
                WF2K[fc] = load_w(S, WF2[:, fc, :], wf2[l, fc * 128:(fc + 1) * 128, :], ("WF2", fc), maxc=1024)
            S.add("dve", lambda e: e.memset(HALOX[:], 0.0), writes=[("HALOX", q_) for q_ in range(1, 64)])
            nbP = PL // 256
            nbS = SL // 256
            S.add("sp", lambda e: e.dma_start(out=HALOX[1:2, :], in_=X1[0:1, :]), writes=[("HALOX", 1)], dma=True)
            for b in range(1, nbP):
                S.add("sp", lambda e, b=b: e.dma_start(out=HALOX[2 * b:2 * b + 2, :], in_=X1[256 * b - 1:256 * b + 1, :]),
                      writes=[("HALOX", 10 + b)], dma=True)
            S.add("sp", lambda e: e.dma_start(out=HALOX[2 * nbP:2 * nbP + 1, :], in_=X1[PL - 1:PL, :]), writes=[("HALOX", 3)], dma=True)
            hb = 2 * (nbP + 1)
            for b in range(1, nbS):
                r = PL + 1 + 256 * b - 1
                S.add("sp", lambda e, b=b, r=r: e.dma_start(out=HALOX[hb + 2 * b:hb + 2 * b + 2, :], in_=X1[r:r + 2, :]),
                      writes=[("HALOX", 30 + b)], dma=True)
            S.add("sp", lambda e: e.dma_start(out=HALOX[hb + 1:hb + 2, :], in_=X1[PL + 1:PL + 2, :]), writes=[("HALOX", 5)], dma=True)
            S.add("sp", lambda e: e.dma_start(out=HALOX[hb + 2 * nbS:hb + 2 * nbS + 1, :], in_=X1[PL + SL:PL + SL + 1, :]),
                  writes=[("HALOX", 6)], dma=True)
            S.add("sp", lambda e: e.dma_start(out=HALOX[hb:hb + 1, :], in_=G2[1:2, :]), reads=["G2"], writes=[("HALOX", 7)], dma=True)
            S.add("sp", lambda e: e.dma_start(out=HALOX[hb + 2 * nbS + 1:hb + 2 * nbS + 2, :], in_=G2[2:3, :]),
                  reads=["G2"], writes=[("HALOX", 8)], dma=True)
            prep_chunk(S, "H", None, 128, HALOX, [("HALOX", q_) for q_ in range(1, 64)], XN[0], ("XN", 0), SQJ, SS, 15, GB, "GB", TP, "TP", ident,
                       HALOT[:], ["HALOT"], rowscale=CST[:, C_HFL:C_HFL + 1])

            S.mark("D halo done")
            tiles = []
            for b in range(nbP):
                tiles.append((b * 2, 2 * b, 2 * (b + 1) + 1))
            for b in range(nbS):
                tiles.append((NCP + b * 2, hb + 2 * b, hb + 2 * (b + 1) + 1))
            cctr = [0]

            def prep_tile(ti):
                n0, hl, hr = tiles[ti]
                hs = ti % 2
                for c in range(2):
                    n = n0 + c
                    xs = (ti % 2) * 2 + c
                    i = cctr[0] % 2
                    cctr[0] += 1
                    r1 = G.x1row(n)
                    prep_chunk(S, "D", X1[r1:r1 + 128, :], 128, XT[xs], ("XT", xs), XN[i], ("XN", i), SQJ, SS, xs,
                               GB, "GB", TP, "TP", ident, H2T[hs][:, :, 1 + c * 128:1 + (c + 1) * 128], [("H2T", hs, c)])
                S.add("dve", lambda e: e.tensor_copy(out=H2T[hs][:, :, 0:1], in_=HALOT[:, :, hl:hl + 1]),
                      reads=["HALOT"], writes=[("H2T", hs, "l")])
                S.add("dve", lambda e: e.tensor_copy(out=H2T[hs][:, :, 257:258], in_=HALOT[:, :, hr:hr + 1]),
                      reads=["HALOT"], writes=[("H2T", hs, "r")])

            fcc = [0]

            def ffn_out(ti, fc, g):
                for sbk in range(2):
                    for ct in range(2):
                        S.add("pe", lambda e, sbk=sbk, ct=ct, fc=fc, g=g: e.matmul(
                            PY[sbk * 2 + ct][:], lhsT=GU[g][:, sbk * 128:(sbk + 1) * 128], rhs=WF2[:, fc, ct * 512:(ct + 1) * 512],
                            start=(fc == 0), stop=(fc == NFC - 1)),
                              reads=[("GU", g)] + WF2K[fc], writes=[("PY", sbk * 2 + ct)])

            def main_tile(ti, mid):
                n0, hl, hr = tiles[ti]
                hs = ti % 2
                hk = [("H2T", hs, 0), ("H2T", hs, 1), ("H2T", hs, "l"), ("H2T", hs, "r")]
                pend = []
                for fc in range(NFC):
                    S.mark(f"D tile {ti} fc {fc}")
                    if fc == NFC // 2 and mid is not None:
                        mid()
                    p = fcc[0] % 2
                    g = fcc[0] % 3
                    fcc[0] += 1
                    for k in range(8):
                        S.add("pe", lambda e, k=k, fc=fc, p=p: e.matmul(PA[p][:, 0:258], lhsT=WF1[:, k, fc * 128:(fc + 1) * 128],
                                                                       rhs=H2T[hs][:, k, :], start=(k == 0), stop=(k == 7)),
                              reads=WF1K[k] + hk, writes=[("PA", p)])
                    for k in range(8):
                        S.add("pe", lambda e, k=k, fc=fc: e.matmul(PU[:, 0:256], lhsT=WF1[:, k, DFF + fc * 128:DFF + (fc + 1) * 128],
                                                                  rhs=H2T[hs][:, k, 1:257], start=(k == 0), stop=(k == 7)),
                              reads=WF1K[k] + hk, writes=["PU"])
                    S.add("act", lambda e, p=p: e.activation(out=AS[p][:], in_=PA[p][:, 0:258], func=AF.Copy),
                          reads=[("PA", p)], writes=[("AS", p)])
                    S.add("act", lambda e, p=p: e.activation(out=US[p][:], in_=PU[:, 0:256], func=AF.Copy),
                          reads=["PU"], writes=[("US", p)])
                    S.add("act", lambda e, fc=fc, p=p: e.activation(out=C1[p][:], in_=AS[p][:, 1:257], func=AF.Identity,
                                                                   scale=CW[:, fc * 3 + 1:fc * 3 + 2], bias=CB[:, fc:fc + 1]),
                          reads=[("AS", p), "CW", "CB"], writes=[("C1", p)])
                    S.add("dve", lambda e, fc=fc, p=p: e.scalar_tensor_tensor(out=C2[p][:], in0=AS[p][:, 0:256], scalar=CW[:, fc * 3:fc * 3 + 1],
                                                                             in1=C1[p][:], op0=ALU.mult, op1=ALU.add),
                          reads=[("AS", p), ("C1", p), "CW"], writes=[("C2", p)])
                    S.add("dve", lambda e, fc=fc, p=p: e.scalar_tensor_tensor(out=C3[p][:], in0=AS[p][:, 2:258], scalar=CW[:, fc * 3 + 2:fc * 3 + 3],
                                                                             in1=C2[p][:], op0=ALU.mult, op1=ALU.add),
                          reads=[("AS", p), ("C2", p), "CW"], writes=[("C3", p)])
                    S.add("act", lambda e, p=p: e.activation(out=GE[p][:], in_=C3[p][:], func=AF.Gelu),
                          reads=[("C3", p)], writes=[("GE", p)])
                    S.add("pool", lambda e, p=p, g=g: e.tensor_tensor(out=GU[g][:], in0=US[p][:], in1=GE[p][:], op=ALU.mult),
                          reads=[("US", p), ("GE", p)], writes=[("GU", g)])
                    pend.append((fc, g))
                    if len(pend) > 1:
                        ffn_out(ti, *pend.pop(0))
                while pend:
                    ffn_out(ti, *pend.pop(0))
                S.mark(f"D tile {ti} ffn done")
                for sbk in range(2):
                    xs = (ti % 2) * 2 + sbk
                    n = n0 + sbk
                    for ct in range(2):
                        cs = slice(ct * 512, (ct + 1) * 512)
                        S.add("dve", lambda e, xs=xs, cs=cs, sbk=sbk, ct=ct: e.tensor_tensor(
                            out=XT[xs][:, cs], in0=PY[sbk * 2 + ct][:], in1=XT[xs][:, cs], op=ALU.add),
                              reads=[("PY", sbk * 2 + ct), ("XT", xs)], writes=[("XT", xs)])
                    r0 = G.xrow(n)
                    if not last:
                        S.add("sp", lambda e, xs=xs, r0=r0: e.dma_start(out=X2[r0:r0 + 128, :], in_=XT[xs][:]),
                              reads=[("XT", xs)], writes=[("X2", n)], dma=True)
                    else:
                        sc = SS[:, 8 + xs:9 + xs]
                        sk = ("SSF", xs)
                        S.add("act", lambda e, xs=xs, sc=sc: e.activation(out=SQJ[:], in_=XT[xs][:], func=AF.Square, accum_out=sc),
                              reads=[("XT", xs)], writes=[("SQJ", "D"), sk])
                        S.add("dve", lambda e, sc=sc: e.tensor_scalar(out=sc, in0=sc, scalar1=1.0 / D, scalar2=EPS, op0=ALU.mult, op1=ALU.add),
                              reads=[sk], writes=[sk])
                        S.add("act", lambda e, sc=sc: e.activation(out=sc, in_=sc, func=AF.Sqrt), reads=[sk], writes=[sk])
                        S.add("dve", lambda e, sc=sc: e.reciprocal(out=sc, in_=sc), reads=[sk], writes=[sk])
                        S.add("dve", lambda e, xs=xs, sc=sc: e.scalar_tensor_tensor(out=XT[xs][:], in0=XT[xs][:], scalar=sc, in1=GBL[:],
                                                                                   op0=ALU.mult, op1=ALU.mult),
                              reads=[("XT", xs), sk, "GBL"], writes=[("XT", xs)])
                        S.add("sp", lambda e, xs=xs, r0=r0: e.dma_start(out=y_out[r0:r0 + 128, :], in_=XT[xs][:]),
                              reads=[("XT", xs)], writes=[("Y", n)], dma=True)

            prep_tile(0)
            for ti in range(len(tiles)):
                mid = (lambda t=ti + 1: prep_tile(t)) if ti + 1 < len(tiles) else None
                main_tile(ti, mid)
            S.emit(nc, st)

    Sched.GLOBAL.clear()
    Sched.NINST[0] = 0
    gstack = ExitStack()
    Sched.GLOBAL["stack"] = gstack
    plist = []
    for l in range(DEPTH):
        xsrc = x_in if l == 0 else X2
        plist.append(lambda l=l, xsrc=xsrc: pass_A(l, xsrc))
        plist.append(lambda l=l, xsrc=xsrc: pass_B(l, xsrc))
        plist.append(lambda l=l: pass_D(l, l == DEPTH - 1))
    with gstack:
        for f in plist[:npasses]:
            f()
    return nc, G


def _consts(G, core):
    PL, SL, T, NCS = G.PL, G.SL, G.T, G.NCS
    h = core % 2
    pos = np.concatenate([np.arange(PL), h * SL + np.arange(SL)]).astype(np.float32)

    def tabs(half):
        fr = (np.float32(10000.0) ** (-np.arange(half, dtype=np.float32) / np.float32(half))).astype(np.float32)
        ang = (pos[:, None] * fr[None, :]).astype(np.float32)
        return np.cos(ang).astype(np.float32), np.sin(ang).astype(np.float32)

    ca, sa = tabs(32)
    cr, sr = tabs(64)
    fmtab = np.zeros((128, 4, T), np.float32)
    p = np.arange(128)
    da = p % 64
    fmtab[:, 0, :] = ca[:, da % 32].T
    fmtab[:, 1, :] = sa[:, da % 32].T
    fmtab[:, 2, :] = cr[:, p % 64].T
    fmtab[:, 3, :] = sr[:, p % 64].T
    rot = np.zeros((128, 256), np.float32)
    for m in range(128):
        d = m % 64
        if d < 32:
            rot[m + 32, m] = -1.0
        else:
            rot[m - 32, m] = 1.0
        if m < 64:
            rot[m + 64, 128 + m] = -1.0
        else:
            rot[m - 64, 128 + m] = 1.0
    tmtab = np.concatenate([cr, -sr, sr], axis=1).astype(np.float32)
    NCST = 528 + NCS
    cst = np.zeros((128, NCST), np.float32)
    k = np.arange(128)[:, None].astype(np.float32)
    q = np.arange(128)[None, :].astype(np.float32)
    cst[:, 0:128] = np.maximum(q - k, 0)
    cst[:, 128:256] = np.maximum(k - q, 0)
    cst[:, 256:384] = q + 1
    cst[:, 384:512] = 128 - q
    cst[:, 512] = k[:, 0]
    cst[:, 513] = 127 - k[:, 0]
    cst[:, 514] = float(h)
    cst[:, 515] = float(1 - h)
    hfl = np.ones(128, np.float32)
    hb = 2 * (PL // 256 + 1)
    hfl[hb] = float(h)
    hfl[hb + 2 * (SL // 256) + 1] = float(1 - h)
    cst[:, 516] = hfl
    cst[:, 520:524] = 1.0
    cst[:, 528:528 + NCS] = (128.0 * (NCS - 1 - np.arange(NCS)))[None, :]
    mk = np.zeros((128, 2, 512), np.float32)
    kk = np.arange(128)[:, None]
    qq = np.arange(128)[None, :]
    mk[:, 0, :] = np.tile((kk >= qq).astype(np.float32), (1, 4))
    mk[:, 1, :] = np.tile((kk <= qq).astype(np.float32), (1, 4))
    return dict(fmtab=fmtab, tmtab=tmtab, cst=cst, masks=mk.reshape(128, 1024), ident=np.eye(128, dtype=np.float32), rot=rot)


def _perm_w_in(w_in):
    cols = []
    for g in range(4):
        cols += list(range(g * 64, (g + 1) * 64)) + list(range((4 + g) * 64, (5 + g) * 64))
    cols += list(range(512, 640))
    cols += list(range(768, 1280))
    cols += list(range(1280, 1792))
    cols += list(range(1280, 1792)) + list(range(1792, 2304)) + list(range(2304, 2816)) + list(range(2816, 4864)) + list(range(640, 768))
    return np.ascontiguousarray(w_in[:, :, np.array(cols)])


def _shared_inputs(inp):
    f = lambda a: np.ascontiguousarray(np.asarray(a, dtype=np.float32))
    cw = f(inp["conv_w"])
    cwl = np.ascontiguousarray(cw.reshape(DEPTH, 3, NFC, 128).transpose(0, 3, 2, 1).reshape(DEPTH, 128, NFC * 3))
    cb = f(inp["conv_b"])
    cbl = np.ascontiguousarray(cb.reshape(DEPTH, NFC, 128).transpose(0, 2, 1))
    return dict(
        w_in=_perm_w_in(f(inp["w_in"])), wba=f(inp["w_branch_attn"]), wbr=f(inp["w_branch_ret"]), wo=f(inp["w_out"]),
        wf1=f(inp["w_ffn_in"]), wf2=f(inp["w_ffn_out"]), g_mix=f(inp["norm_mix_g"]), g_ffn=f(inp["norm_ffn_g"]),
        g_fin=f(inp["final_norm_g"]).reshape(1, D), g_ret=f(inp["ret_norm_g"]), sink=f(inp["attn_sink"]),
        ldf=f(inp["ret_log_decay_f"]), ldb=f(inp["ret_log_decay_b"]), cw=cwl, cb=cbl)


_CACHE = {}


def run(inp, PL, SL, debug=False, npasses=6, trace=False):
    key = (PL, SL, debug, npasses)
    if key not in _CACHE:
        _CACHE[key] = build(PL, SL, debug, npasses)
    nc, G = _CACHE[key]
    shared = _shared_inputs(inp)
    xp = np.asarray(inp["x_prompt"], dtype=np.float32)
    xs = np.asarray(inp["x_sample"], dtype=np.float32)
    in_maps = []
    for c in range(8):
        m = dict(shared)
        m.update(_consts(G, c))
        h = c % 2
        m["x"] = np.ascontiguousarray(np.concatenate([xp[c], xs[c // 2, h * SL:(h + 1) * SL]], axis=0))
        in_maps.append(m)
    if trace:
        res = run_bass_kernel_spmd(nc, in_maps, core_ids=list(range(8)), trace=True)
    else:
        res = run_bass_kernel_spmd(nc, in_maps, core_ids=list(range(8)))
    yp = np.stack([res.results[c]["y"][:PL] for c in range(8)], axis=0)
    ys = np.stack([np.concatenate([res.results[2 * s]["y"][PL:], res.results[2 * s + 1]["y"][PL:]], axis=0) for s in range(4)], axis=0)
    return (yp.astype(np.float32), ys.astype(np.float32)), res


def kernel(**inputs):
    out, _ = run(inputs, 2048, 4096)
    return out
```

```python
from contextlib import ExitStack
import numpy as np
import concourse.bass as bass
import concourse.mybir as mybir
from concourse.bass_utils import run_bass_kernel_spmd

F32 = mybir.dt.float32
BF16 = mybir.dt.bfloat16
AF = mybir.ActivationFunctionType
ALU = mybir.AluOpType

D = 1024
DEPTH = 2
DFF = 2816
NFC = DFF // 128
INW = 5376
NFM = 13
TMW = 3712
KR0, VR0, GR0, GT0, VA0 = 0, 512, 1024, 1536, 3584
EPS = 1e-6
GROUPS = [[0, 1], [2, 3], [4, 5], [6, 7]]
ENGS = ("pe", "act", "dve", "pool", "sp")


class Sched:
    NDS = 14
    EPOCH = 12000
    UID = [0]

    NINST = [0]
    GLOBAL = {}

    def __init__(self):
        self.ops = []
        self.lastw = {}
        self.readers = {}
        self.inst = Sched.NINST[0]
        Sched.NINST[0] += 1

    PSUM_NAMES = {"PF", "PT", "TP", "PR", "SP", "OP", "IP", "ORp", "TPB", "APs", "RPs", "PA", "PU", "PY"}

    cap = None

    def interleave(self, streams):
        pos = [0] * len(streams)
        total = sum(len(x) for x in streams)
        for _ in range(total):
            best, bf = None, None
            for si, st_ in enumerate(streams):
                if pos[si] < len(st_):
                    f = pos[si] / len(st_)
                    if bf is None or f < bf:
                        best, bf = si, f
            a = streams[best][pos[best]]
            pos[best] += 1
            self.add(*a[0], **a[1])

    def add(self, eng, fn, reads=(), writes=(), dma=False, cc=False):
        if self.cap is not None:
            self.cap.append(((eng, fn), dict(reads=list(reads), writes=list(writes), dma=dma, cc=cc)))
            return None
        def _ps(k):
            return (k if isinstance(k, str) else k[0]) in self.PSUM_NAMES
        ps_reads = [k for k in reads if _ps(k) and k not in writes]
        reads = [k for k in reads if not _ps(k)]
        writes = list(writes)
        import os
        cut = int(os.environ.get("KCUT", "0"))
        if cut and len(self.ops) >= cut and self.inst == int(os.environ.get("KCUTPASS", "0")):
            return None
        idx = len(self.ops)
        deps = {}

        def dep(i, raw):
            if i is None or i == idx:
                return
            deps[i] = deps.get(i, False) or raw

        for k in list(reads) + ps_reads:
            dep(self.lastw.get(k), True)
        for k in ps_reads:
            for r in self.readers.get(k, ()):
                dep(r, False)
        for k in writes:
            dep(self.lastw.get(k), False)
            for r in self.readers.get(k, ()):
                dep(r, False)
        for k in writes + ps_reads:
            self.lastw[k] = idx
            self.readers[k] = []
        for k in reads:
            if k not in writes:
                self.readers.setdefault(k, []).append(idx)
        self.ops.append(dict(eng=eng, fn=fn, deps=sorted(deps), raw=deps, dma=dma, cc=cc, marked=False))
        return idx

    @staticmethod
    def _skip(op, dop, d):
        if dop["dma"] or dop["cc"] or op["dma"] or op["cc"] or dop["eng"] != op["eng"]:
            return False
        if op["eng"] == "pe":
            return True
        return not op["raw"][d]

    def mark(self, name):
        import os
        if os.environ.get("KMARK"):
            print("MARK", self.inst, name, len(self.ops))

    def emit(self, nc, st):
        ops = self.ops
        for op in ops:
            for d in op["deps"]:
                dop = ops[d]
                if dop["dma"] or dop["cc"]:
                    continue
                if self._skip(op, dop, d):
                    continue
                dop["marked"] = True
        last = {}
        for i, op in enumerate(ops):
            if not op["dma"] and not op["cc"]:
                last[op["eng"]] = i
        for e, i in last.items():
            ops[i]["marked"] = True
        gs = self.GLOBAL
        cnt = gs.setdefault("cnt", {e: 0 for e in ENGS})
        dcount = gs.setdefault("dcount", {"sp": 0, "pool": 0})
        duse = gs.setdefault("duse", {"sp": [0] * self.NDS, "pool": [0] * self.NDS})
        SEM = gs.setdefault("SEM", {})
        gst = gs["stack"]
        ncc0 = gs.get("ncc", 0)
        ncc = ncc0
        semkeys = set()
        for op in ops:
            if op["cc"]:
                op["sem"] = ("cc", ncc)
                op["val"] = 1
                ncc += 1
            elif op["dma"]:
                q = op["eng"]
                j = dcount[q] % self.NDS
                dcount[q] += 1
                duse[q][j] += 1
                op["sem"] = ("d", q, j)
                op["val"] = 16 * duse[q][j]
            elif op["marked"]:
                e = op["eng"]
                ep = cnt[e] // self.EPOCH
                cnt[e] += 1
                op["sem"] = ("c", e, ep)
                op["val"] = cnt[e] - ep * self.EPOCH
            else:
                continue
            semkeys.add(op["sem"])
        gs["ncc"] = ncc
        for k in sorted(semkeys, key=str):
            if k not in SEM:
                Sched.UID[0] += 1
                SEM[k] = gst.enter_context(nc.semaphore(f"s{Sched.UID[0]}_" + "_".join(str(x) for x in k)))
        finals = []
        for e, i in last.items():
            finals.append((ops[i]["sem"], ops[i]["val"]))
        for q in ("sp", "pool"):
            for j in range(self.NDS):
                if duse[q][j]:
                    finals.append((("d", q, j), 16 * duse[q][j]))
        for c in range(ncc0, ncc):
            finals.append((("cc", c), 1))

        def run(engname, eng):
            waited = {}

            def wait(sk, val):
                if waited.get(sk, 0) >= val:
                    return
                eng.wait_ge(SEM[sk], val)
                waited[sk] = val

            for op in ops:
                if op["eng"] != engname:
                    continue
                for d in op["deps"]:
                    dop = ops[d]
                    if self._skip(op, dop, d):
                        continue
                    wait(dop["sem"], dop["val"])
                if op["dma"] and op["val"] > 16:
                    wait(op["sem"], op["val"] - 16)
                ins = op["fn"](eng)
                if op["cc"]:
                    ins.then_inc(SEM[op["sem"]])
                elif op["dma"]:
                    ins.then_inc(SEM[op["sem"]], 16)
                elif op["marked"]:
                    ins.then_inc(SEM[op["sem"]], 1)
            for sk, val in finals:
                wait(sk, val)

        block = st.enter_context(nc.Block())

        @block.tensor
        def _(e):
            run("pe", e)

        @block.scalar
        def _(e):
            run("act", e)

        @block.vector
        def _(e):
            run("dve", e)

        @block.gpsimd
        def _(e):
            run("pool", e)

        @block.sync
        def _(e):
            run("sp", e)


class Geo:
    def __init__(self, PL, SL):
        self.PL, self.SL = PL, SL
        self.NCP, self.NCS = PL // 128, SL // 128
        self.NCH = self.NCP + self.NCS
        self.T = PL + SL
        self.TC = self.T + 256
        self.NT = self.T // 512
        self.NDT = self.T // 256
        self.NHB = (PL // 256 + 1) + (SL // 256 + 1)

    def seg(self, n):
        return 0 if n < self.NCP else 1

    def xrow(self, n):
        return n * 128

    def x1row(self, n):
        return n * 128 + (1 if n >= self.NCP else 0)

    def fmcol(self, n):
        return n * 128 + (128 if n >= self.NCP else 0)


def build(PL=2048, SL=4096, debug=False, npasses=6):
    G = Geo(PL, SL)
    T, TC, NCP, NCS, NCH = G.T, G.TC, G.NCP, G.NCS, G.NCH
    nc = bass.Bass("TRN2", target_bir_lowering=False)

    def din(name, shape, dt=F32):
        return nc.dram_tensor(name, list(shape), dt, kind="ExternalInput").ap()

    x_in = din("x", [T, D])
    w_in = din("w_in", [DEPTH, D, INW])
    wba = din("wba", [DEPTH, 512, D])
    wbr = din("wbr", [DEPTH, 512, D])
    wo = din("wo", [DEPTH, D, D])
    wf1 = din("wf1", [DEPTH, D, 2 * DFF])
    wf2 = din("wf2", [DEPTH, DFF, D])
    g_mix = din("g_mix", [DEPTH, D])
    g_ffn = din("g_ffn", [DEPTH, D])
    g_fin = din("g_fin", [1, D])
    g_ret = din("g_ret", [DEPTH, 512])
    sinkd = din("sink", [DEPTH, 8])
    ldf = din("ldf", [DEPTH, 4])
    ldb = din("ldb", [DEPTH, 4])
    cwd = din("cw", [DEPTH, 128, NFC * 3])
    cbd = din("cb", [DEPTH, 128, NFC])
    fmtab = din("fmtab", [128, 4, T])
    tmtab = din("tmtab", [T, 3 * 64])
    NCST = 528 + NCS
    cst = din("cst", [128, NCST])
    masks = din("masks", [128, 2 * 512])
    identd = din("ident", [128, 128])
    rotd = din("rot", [128, 256])

    okind = "ExternalOutput"
    y_out = nc.dram_tensor("y", [T, D], F32, kind=okind).ap()

    def scratch(name, shape, dt):
        if debug:
            return nc.dram_tensor(name, list(shape), dt, kind=okind)
        return nc.dram_tensor(name, list(shape), dt)

    FM_t = scratch("FM", [NFM, 128, TC], BF16)
    TM_t = scratch("TMs", [TC, TMW], BF16)
    RB_t = scratch("RB", [NCH, 128, 512], F32)
    X1_t = scratch("X1", [T + 2, D], F32)
    X2_t = scratch("X2", [T, D], F32)
    FM, TMs, RB, X1, X2 = FM_t.ap(), TM_t.ap(), RB_t.ap(), X1_t.ap(), X2_t.ap()
    PKG1_t = nc.dram_tensor("PKG1", [128, 1024], F32)
    G1_t = nc.dram_tensor("G1", [256, 1024], F32)
    PKGK_t = nc.dram_tensor("PKGK", [128, 512], BF16)
    GK_t = nc.dram_tensor("GK", [256, 512], BF16)
    PKG2_t = nc.dram_tensor("PKG2", [2, D], F32)
    G2_t = nc.dram_tensor("G2", [4, D], F32)
    PKG1, G1, PKGK, GK, PKG2, G2 = (t.ap() for t in (PKG1_t, G1_t, PKGK_t, GK_t, PKG2_t, G2_t))

    C_RELQK, C_RELKQ, C_QP1, C_Q128M = 0, 128, 256, 384
    C_IDXK, C_I127, C_FL, C_FR, C_HFL, C_ONE, C_NW = 512, 513, 514, 515, 516, 520, 528

    uid = [0]

    def sb(st, name, shape, dt):
        uid[0] += 1
        return st.enter_context(nc.sbuf_tensor(f"{name}_u{uid[0]}", list(shape), dt))

    def ps(st, name, shape, dt=F32):
        uid[0] += 1
        return st.enter_context(nc.psum_tensor(f"{name}_u{uid[0]}", list(shape), dt))

    def common(st, S):
        identf = sb(st, "identf", [128, 128], F32)
        ident = sb(st, "ident", [128, 128], BF16)
        CST = sb(st, "CST", [128, NCST], F32)
        S.add("sp", lambda e: e.dma_start(out=identf[:], in_=identd[:, :]), writes=["identf"], dma=True)
        S.add("sp", lambda e: e.dma_start(out=CST[:], in_=cst[:, :]), writes=["CST"], dma=True)
        S.add("dve", lambda e: e.tensor_copy(out=ident[:], in_=identf[:]), reads=["identf"], writes=["ident"])
        return ident, CST

    def load_w(S, dst, src, key, maxc=2048):
        ncols = src.shape[-1]
        c0 = 0
        keys = []
        while c0 < ncols:
            c1 = min(ncols, c0 + maxc)
            S.add("pool", lambda e, a=dst[:, c0:c1], b=src[:, c0:c1]: e.dma_start(out=a, in_=b),
                  writes=[(key, c0)], dma=True)
            keys.append((key, c0))
            c0 = c1
        return keys

    def prep_chunk(S, tag, src_ap, nrows, XTs, xkey, XNs, xnkey, SQJ, SS, ss_col, GB, gkey, TP, tpkey,
                   ident, dst_ap, dkeys, rowscale=None):
        xkeys = xkey if isinstance(xkey, list) else [xkey]
        if src_ap is not None:
            S.add("sp", lambda e: e.dma_start(out=XTs[0:nrows, :], in_=src_ap), writes=xkeys, dma=True)
        if rowscale is not None:
            S.add("dve", lambda e: e.tensor_scalar(out=XTs[:, :], in0=XTs[:, :], scalar1=rowscale, scalar2=None,
                                                   op0=ALU.mult), reads=["CST"], writes=xkeys)
        sc = SS[:, ss_col:ss_col + 1]
        sk = ("SS", tag, ss_col)
        S.add("act", lambda e: e.activation(out=SQJ[0:nrows, :], in_=XTs[0:nrows, :], func=AF.Square,
                                            accum_out=sc[0:nrows, :]),
              reads=xkeys, writes=[("SQJ", tag), sk])
        S.add("dve", lambda e: e.tensor_scalar(out=sc[0:nrows, :], in0=sc[0:nrows, :], scalar1=1.0 / D, scalar2=EPS,
                                               op0=ALU.mult, op1=ALU.add), reads=[sk], writes=[sk])
        S.add("act", lambda e: e.activation(out=sc[0:nrows, :], in_=sc[0:nrows, :], func=AF.Sqrt), reads=[sk], writes=[sk])
        S.add("dve", lambda e: e.reciprocal(out=sc[0:nrows, :], in_=sc[0:nrows, :]), reads=[sk], writes=[sk])
        S.add("dve", lambda e: e.scalar_tensor_tensor(out=XNs[0:nrows, :], in0=XTs[0:nrows, :], scalar=sc[0:nrows, :],
                                                      in1=GB[0:nrows, :], op0=ALU.mult, op1=ALU.mult),
              reads=xkeys + [sk, gkey], writes=[xnkey])
        for k in range(8):
            S.add("pe", lambda e, k=k: e.transpose(out=TP[:, k, 0:nrows], in_=XNs[0:nrows, k * 128:(k + 1) * 128],
                                                   identity=ident[0:nrows, 0:nrows]),
                  reads=[xnkey, "ident"], writes=[tpkey])
        S.add("act", lambda e: e.activation(out=dst_ap, in_=TP[:, :, 0:nrows], func=AF.Copy),
              reads=[tpkey], writes=dkeys)

    def pass_A(l, xsrc):
        with ExitStack() as st:
            S = Sched()
            ident, CST = common(st, S)
            WIN = sb(st, "WIN", [128, 8, INW], BF16)
            GB = sb(st, "GB", [128, D], F32)
            XT = [sb(st, f"XT{i}", [128, D], F32) for i in range(2)]
            XN = [sb(st, f"XN{i}", [128, D], BF16) for i in range(2)]
            SQJ = sb(st, "SQJ", [128, D], BF16)
            SS = sb(st, "SS", [128, 8], F32)
            HT = [sb(st, f"HT{i}", [128, 8, 512], BF16) for i in range(2)]
            TAB = [sb(st, f"TAB{i}", [128, 4, 512], F32) for i in range(2)]
            TMT = [sb(st, f"TMT{i}", [128, 3, 64], F32) for i in range(2)]
            T1 = [sb(st, f"T1_{i}", [128, 512], F32) for i in range(2)]
            T2 = [sb(st, f"T2_{i}", [128, 512], F32) for i in range(2)]
            T1T = [sb(st, f"T1T_{i}", [128, 512], F32) for i in range(2)]
            T2T = [sb(st, f"T2T_{i}", [128, 512], F32) for i in range(2)]
            FMO = [sb(st, f"FMO{i}", [128, 512], BF16) for i in range(2)]
            TMO = [sb(st, f"TMO{i}", [128, TMW], BF16) for i in range(2)]
            TP = [ps(st, "TP0", [128, 8, 128], BF16)] * 2
            PF = [ps(st, f"PF{i}", [128, 512]) for i in range(3)]
            PT = [ps(st, f"PT{i}", [128, 512]) for i in range(3)]
            PR = ps(st, "PR", [128, 512])
            XB = [sb(st, f"XB{i}", [128, 512], BF16) for i in range(2)]
            ROTF = sb(st, "ROTF", [128, 256], F32)
            ROT = sb(st, "ROT", [128, 256], BF16)
            S.add("sp", lambda e: e.dma_start(out=ROTF[:], in_=rotd[:, :]), writes=["ROTF"], dma=True)
            S.add("dve", lambda e: e.tensor_copy(out=ROT[:], in_=ROTF[:]), reads=["ROTF"], writes=["ROT"])

            S.add("sp", lambda e: e.dma_start(out=GB[:], in_=g_mix[l:l + 1, :].partition_broadcast(128)),
                  writes=["GB"], dma=True)
            WINK = {}
            for k in range(8):
                WINK[k] = load_w(S, WIN[:, k, :], w_in[l, k * 128:(k + 1) * 128, :], ("WIN", k), maxc=1792)

            cctr = [0]
            fctr = [0]
            tctr = [0]

            def prep_tile(tt):
                hs = tt % 2
                for c in range(4):
                    n = tt * 4 + c
                    i = cctr[0] % 2
                    cctr[0] += 1
                    r0 = G.xrow(n)
                    prep_chunk(S, "A", xsrc[r0:r0 + 128, :], 128, XT[i], ("XT", i), XN[i], ("XN", i), SQJ, SS, i,
                               GB, "GB", TP[i], ("TP", 0), ident, HT[hs][:, :, c * 128:(c + 1) * 128],
                               [("HT", hs, c)])

            def main_tile(tt):
                hs = tt % 2
                hkeys = [("HT", hs, c) for c in range(4)]
                n0 = tt * 4
                c0 = G.fmcol(n0)
                r0 = G.xrow(n0)
                S.add("sp", lambda e: e.dma_start(out=TAB[hs][:], in_=fmtab[:, :, r0:r0 + 512]),
                      writes=[("TAB", hs)], dma=True)
                fm_pend = []
                for f in range(NFM):
                    pf = fctr[0] % 3
                    p = fctr[0] % 2
                    fctr[0] += 1
                    for k in range(8):
                        S.add("pe", lambda e, k=k, f=f, pf=pf: e.matmul(PF[pf][:], lhsT=WIN[:, k, f * 128:(f + 1) * 128],
                                                                       rhs=HT[hs][:, k, :], start=(k == 0), stop=(k == 7)),
                              reads=WINK[k] + hkeys, writes=[("PF", pf)])
                    att = f < 5
                    ci, si = (0, 1) if att else (2, 3)
                    ro = 0 if att else 128
                    S.add("act", lambda e, p=p, pf=pf: e.activation(out=XB[p][:], in_=PF[pf][:], func=AF.Copy),
                          reads=[("PF", pf)], writes=[("XB", p)])
                    S.add("dve", lambda e, p=p, pf=pf, ci=ci: e.tensor_tensor(out=T1[p][:], in0=PF[pf][:], in1=TAB[hs][:, ci, :],
                                                                             op=ALU.mult),
                          reads=[("PF", pf), ("TAB", hs)], writes=[("T1", p)])

                    def stage2(p=p, ro=ro, si=si, f=f):
                        S.add("pe", lambda e: e.matmul(PR[:], lhsT=ROT[:, ro:ro + 128], rhs=XB[p][:], start=True, stop=True),
                              reads=[("XB", p), "ROT"], writes=["PR"])
                        S.add("dve", lambda e: e.tensor_tensor(out=T2[p][:], in0=PR[:], in1=TAB[hs][:, si, :], op=ALU.mult),
                              reads=["PR", ("TAB", hs)], writes=[("T2", p)])
                        S.add("pool", lambda e: e.tensor_tensor(out=FMO[p][:], in0=T1[p][:], in1=T2[p][:], op=ALU.add),
                              reads=[("T1", p), ("T2", p)], writes=[("FMO", p)])
                        S.add("sp", lambda e: e.dma_start(out=FM[f, :, c0:c0 + 512], in_=FMO[p][:]),
                              reads=[("FMO", p)], writes=[("FMd", f, tt)], dma=True)
                    if fm_pend:
                        fm_pend.pop(0)()
                    fm_pend.append(stage2)
                while fm_pend:
                    fm_pend.pop(0)()
                S.mark(f"A tile {tt} FM done")
                for c in range(4):
                    S.mark(f"A tile {tt} TM chunk {c}")
                    n = n0 + c
                    o_ = n % 2
                    rr = G.xrow(n)
                    S.add("sp", lambda e, o_=o_, rr=rr: e.dma_start(
                        out=TMT[o_][:], in_=tmtab[rr:rr + 128, :].rearrange("p (a b) -> p a b", a=3)),
                          writes=[("TMT", o_)], dma=True)
                    for ct in range(8):
                        p = tctr[0] % 3
                        tctr[0] += 1
                        w = 128 if ct == 7 else 512
                        wc0 = 1664 + ct * 512
                        for k in range(8):
                            S.add("pe", lambda e, k=k, p=p, w=w, wc0=wc0, c=c: e.matmul(
                                PT[p][:, 0:w], lhsT=HT[hs][:, k, c * 128:(c + 1) * 128], rhs=WIN[:, k, wc0:wc0 + w],
                                start=(k == 0), stop=(k == 7)),
                                  reads=WINK[k] + [("HT", hs, c)], writes=[("PT", p)])
                        okey = ("TMO", o_, ct)
                        if ct == 0:
                            q = p % 2
                            pv = PT[p][:].rearrange("p (h t d) -> p h t d", h=4, t=2)
                            t1v = T1T[q][:].rearrange("p (g d) -> p g d", d=64)
                            t2v = T2T[q][:].rearrange("p (h t d) -> p h t d", h=4, t=2)
                            S.add("dve", lambda e, p=p, o_=o_, t1v=t1v: e.tensor_tensor(
                                out=t1v, in0=PT[p][:].rearrange("p (g d) -> p g d", d=64),
                                in1=TMT[o_][:, 0:1, :].broadcast_to([128, 8, 64]), op=ALU.mult),
                                  reads=[("PT", p), ("TMT", o_)], writes=[("T1T", q)])
                            S.add("dve", lambda e, pv=pv, t2v=t2v, o_=o_: e.tensor_tensor(
                                out=t2v[:, :, 0, :], in0=pv[:, :, 1, :],
                                in1=TMT[o_][:, 1:2, :].broadcast_to([128, 4, 64]), op=ALU.mult),
                                  reads=[("PT", p), ("TMT", o_)], writes=[("T2T", q, "a")])
                            S.add("dve", lambda e, pv=pv, t2v=t2v, o_=o_: e.tensor_tensor(
                                out=t2v[:, :, 1, :], in0=pv[:, :, 0, :],
                                in1=TMT[o_][:, 2:3, :].broadcast_to([128, 4, 64]), op=ALU.mult),
                                  reads=[("PT", p), ("TMT", o_)], writes=[("T2T", q, "b")])
                            S.add("pool", lambda e, q=q, o_=o_: e.tensor_tensor(
                                out=TMO[o_][:, KR0:KR0 + 512], in0=T1T[q][:], in1=T2T[q][:], op=ALU.add),
                                  reads=[("T1T", q), ("T2T", q, "a"), ("T2T", q, "b")], writes=[okey])
                        else:
                            func = AF.Copy if ct in (1, 7) else (AF.Silu if ct == 2 else AF.Sigmoid)
                            dc0 = {1: VR0, 2: GR0, 7: VA0}.get(ct, GT0 + (ct - 3) * 512)
                            S.add("act", lambda e, p=p, w=w, dc0=dc0, func=func, o_=o_: e.activation(
                                out=TMO[o_][:, dc0:dc0 + w], in_=PT[p][:, 0:w], func=func),
                                  reads=[("PT", p)], writes=[okey])
                    rc = G.fmcol(n)
                    S.add("sp", lambda e, o_=o_, rc=rc: e.dma_start(out=TMs[rc:rc + 128, :], in_=TMO[o_][:]),
                          reads=[("TMO", o_, ct) for ct in range(8)], writes=[("TMd", n)], dma=True)

            prep_tile(0)
            for tt in range(G.NT):
                if tt + 1 < G.NT:
                    prep_tile(tt + 1)
                main_tile(tt)
            S.emit(nc, st)

    def pass_B(l, xsrc):
        with ExitStack() as st:
            S = Sched()
            ident, CST = common(st, S)
            WBA = sb(st, "WBA", [64, 8, D], BF16)
            WBR = sb(st, "WBR", [128, 4, D], BF16)
            WO = sb(st, "WO", [128, 8, D], BF16)
            MK = sb(st, "MK", [128, 2, 512], BF16)
            MKF = sb(st, "MKF", [128, 2, 512], BF16)
            LFB = sb(st, "LFB", [128, 8], F32)
            ZET = sb(st, "ZET", [128, 8], F32)
            CDC = sb(st, "CDC", [128, 8], F32)
            ESK = sb(st, "ESK", [128, 8], F32)
            DTOT = sb(st, "DTOT", [128, 4, 128], BF16)
            DTMP = sb(st, "DTMP", [128, 128], F32)
            DTMP2 = sb(st, "DTMP2", [128, 128], F32)
            XIF = sb(st, "XIF", [128, 4, 128], F32)
            XIB = sb(st, "XIB", [128, 4, 128], F32)
            WBW = sb(st, "WBW", [128, NCS, 4], F32)
            NG = sb(st, "NG", [128, 512], F32)
            SINF = sb(st, "SINF", [128, 512], F32)
            SINB = sb(st, "SINB", [128, 512], F32)
            RBST = [sb(st, f"RBST{i}", [128, 512], F32) for i in range(2)]
            RFST = [sb(st, f"RFST{i}", [128, 512], F32) for i in range(2)]
            KVL = [sb(st, f"KVL{i}", [128, 1024], BF16) for i in range(2)]
            VZ = [sb(st, f"VZ{i}", [128, 512], BF16) for i in range(2)]
            QA = [sb(st, f"QA{i}", [128, 4, 128], BF16) for i in range(2)]
            KA = [sb(st, f"KA{i}", [128, 384], BF16) for i in range(2)]
            VA1 = [sb(st, f"VA1_{i}", [128, 3, 2, 128], BF16) for i in range(2)]
            QR = [sb(st, f"QR{i}", [128, 4, 128], BF16) for i in range(2)]
            KR = [sb(st, f"KR{i}", [128, 4, 128], BF16) for i in range(2)]
            GG = [sb(st, f"GG{i}", [128, 2560], BF16) for i in range(3)]
            RBL = [sb(st, f"RBL{i}", [128, 512], F32) for i in range(2)]
            XR = [sb(st, f"XR{i}", [128, D], F32) for i in range(3)]
            PTs = [sb(st, f"PTs{i}", [128, 512], BF16) for i in range(4)]
            def two(name, shape, dt):
                return [sb(st, f"{name}{i}", shape, dt) for i in range(2)]
            DENs = two("DEN", [64, 512], F32)
            ATTs = [[sb(st, f"ATT{i}_{k}", [64, 4, 128], BF16) for k in range(2)] for i in range(2)]
            INMs = two("INM", [128, 4, 128], BF16)
            QXFs = two("QXF", [128, 4, 128], BF16)
            QXBs = two("QXB", [128, 4, 128], BF16)
            RFBs = two("RFB", [128, 512], BF16)
            RBBs = two("RBB", [128, 512], BF16)
            RBTs = two("RBT", [128, 512], F32)
            SQs = two("SQ", [128, 512], F32)
            SSRs = two("SSR", [128, 4], F32)
            TRs = two("TR", [128, 512], F32)
            T2Rs = two("T2R", [128, 512], F32)
            RETs = two("RET", [128, 512], BF16)
            RETTs = two("RETT", [128, 4, 128], BF16)
            M1s = two("M1", [128, 512], F32)
            M2s = two("M2", [128, 512], F32)
            MERs = two("MER", [128, D], BF16)
            MTs = two("MT", [128, 8, 128], BF16)
            SP = [ps(st, f"SP{i}", [128, 512]) for i in range(2)]
            OP = ps(st, "OP", [128, 512])
            IP = ps(st, "IP", [128, 4, 128])
            ORp = ps(st, "ORp", [128, 4, 128])
            TPB = ps(st, "TPB", [128, 8, 128], BF16)
            APs = ps(st, "APs", [128, 512])
            RPs = ps(st, "RPs", [128, 512])

            WBAK, WBRK, WOK = {}, {}, {}
            for h in range(8):
                WBAK[h] = load_w(S, WBA[:, h, :], wba[l, h * 64:(h + 1) * 64, :], ("WBA", h), maxc=1024)
            for h in range(4):
                WBRK[h] = load_w(S, WBR[:, h, :], wbr[l, h * 128:(h + 1) * 128, :], ("WBR", h), maxc=1024)
            for k in range(8):
                WOK[k] = load_w(S, WO[:, k, :], wo[l, k * 128:(k + 1) * 128, :], ("WO", k), maxc=1024)
            S.add("pool", lambda e: e.dma_start(out=MK[:].rearrange("p a b -> p (a b)"), in_=masks[:, :]),
                  writes=["MK"], dma=True)
            S.add("sp", lambda e: e.dma_start(out=LFB[:, 0:4], in_=ldf[l:l + 1, :].partition_broadcast(128)),
                  writes=["LFB"], dma=True)
            S.add("sp", lambda e: e.dma_start(out=LFB[:, 4:8], in_=ldb[l:l + 1, :].partition_broadcast(128)),
                  writes=["LFB"], dma=True)
            S.add("sp", lambda e: e.dma_start(out=ESK[:], in_=sinkd[l:l + 1, :].partition_broadcast(128)),
                  writes=["ESK"], dma=True)
            S.add("sp", lambda e: e.dma_start(out=NG[:], in_=g_ret[l:l + 1, :].partition_broadcast(128)),
                  writes=["NG"], dma=True)
            S.add("act", lambda e: e.activation(out=ESK[:], in_=ESK[:], func=AF.Exp), reads=["ESK"], writes=["ESK"])
            for j, cf in ((0, C_FL), (1, C_FR)):
                S.add("dve", lambda e, j=j, cf=cf: e.tensor_scalar(out=MKF[:, j, :], in0=MK[:, j, :],
                                                                  scalar1=CST[:, cf:cf + 1], scalar2=None, op0=ALU.mult),
                      reads=["MK", "CST"], writes=[("MKF", j)])
            KSC = 128.0 ** -0.5
            S.add("act", lambda e: e.activation(out=ZET[:, 0:4], in_=LFB[:, 0:4], func=AF.Exp,
                                                scale=CST[:, C_I127:C_I127 + 1]), reads=["LFB", "CST"], writes=["ZETa"])
            S.add("act", lambda e: e.activation(out=ZET[:, 4:8], in_=LFB[:, 4:8], func=AF.Exp,
                                                scale=CST[:, C_IDXK:C_IDXK + 1]), reads=["LFB", "CST"], writes=["ZETb"])
            S.add("dve", lambda e: e.tensor_scalar(out=ZET[:], in0=ZET[:], scalar1=KSC, scalar2=None, op0=ALU.mult),
                  reads=["ZETa", "ZETb"], writes=["ZET"])
            S.add("act", lambda e: e.activation(out=CDC[:], in_=LFB[:], func=AF.Exp, scale=128.0),
                  reads=["LFB"], writes=["CDC"])
            for h in range(4):
                S.add("dve", lambda e, h=h: e.tensor_scalar(out=DTMP[:], in0=CST[:, C_RELQK:C_RELQK + 128],
                                                           scalar1=LFB[:, h:h + 1], scalar2=None, op0=ALU.mult),
                      reads=["CST", "LFB"], writes=["DTMP"])
                S.add("dve", lambda e, h=h: e.scalar_tensor_tensor(out=DTMP2[:], in0=CST[:, C_RELKQ:C_RELKQ + 128],
                                                                  scalar=LFB[:, 4 + h:5 + h], in1=DTMP[:],
                                                                  op0=ALU.mult, op1=ALU.add),
                      reads=["CST", "LFB", "DTMP"], writes=["DTMP2"])
                S.add("act", lambda e: e.activation(out=DTMP[:], in_=DTMP2[:], func=AF.Exp),
                      reads=["DTMP2"], writes=["DTMP"])
                S.add("dve", lambda e, h=h: e.tensor_scalar(out=DTOT[:, h, :], in0=DTMP[:], scalar1=KSC, scalar2=None,
                                                           op0=ALU.mult), reads=["DTMP"], writes=[("DTOT", h)])
                S.add("act", lambda e, h=h: e.activation(out=XIF[:, h, :], in_=CST[:, C_QP1:C_QP1 + 128], func=AF.Exp,
                                                        scale=LFB[:, h:h + 1]), reads=["CST", "LFB"], writes=[("XIF", h)])
                S.add("act", lambda e, h=h: e.activation(out=XIB[:, h, :], in_=CST[:, C_Q128M:C_Q128M + 128], func=AF.Exp,
                                                        scale=LFB[:, 4 + h:5 + h]), reads=["CST", "LFB"], writes=[("XIB", h)])
            DTK = [("DTOT", h) for h in range(4)]
            XIFK = [("XIF", h) for h in range(4)]
            XIBK = [("XIB", h) for h in range(4)]
            S.add("dve", lambda e: e.tensor_tensor(out=WBW[:], in0=CST[:, C_NW:C_NW + NCS].unsqueeze(2).broadcast_to([128, NCS, 4]),
                                                   in1=LFB[:, 4:8].unsqueeze(1).broadcast_to([128, NCS, 4]), op=ALU.mult),
                  reads=["CST", "LFB"], writes=["WBW"])
            S.add("act", lambda e: e.activation(out=WBW[:], in_=WBW[:], func=AF.Exp), reads=["WBW"], writes=["WBW"])
            for i in range(2):
                S.add("dve", lambda e, i=i: e.memset(VA1[i][:, :, :, 64:128], 1.0), writes=[("VA1o", i)])

            def v4(t):
                return t[:].rearrange("p (h e) -> p h e", h=4)

            def bc4(col_ap):
                return col_ap.unsqueeze(2).broadcast_to([128, 4, 128])

            ldc = [0]

            def scan(chunks, direction, ST, skey, store_rb):
                cur = 0
                first = True
                for n in chunks:
                    if first:
                        S.add("dve", lambda e, cur=cur: e.memset(ST[cur][:], 0.0), writes=[(skey, cur)])
                        first = False
                    if store_rb:
                        S.add("sp", lambda e, n=n, cur=cur: e.dma_start(out=RB[n, :, :], in_=ST[cur][:]),
                              reads=[(skey, cur)], writes=[("RB", n)], dma=True)
                    i = ldc[0] % 2
                    ldc[0] += 1
                    rc = G.fmcol(n)
                    S.add("sp", lambda e, i=i, rc=rc: e.dma_start(out=KVL[i][:], in_=TMs[rc:rc + 128, 0:1024]),
                          writes=[("KVL", i)], dma=True)
                    zc = 0 if direction == "f" else 4
                    S.add("dve", lambda e, i=i, zc=zc: e.tensor_tensor(
                        out=v4(VZ[i]), in0=KVL[i][:, 512:1024].rearrange("p (h e) -> p h e", h=4),
                        in1=bc4(ZET[:, zc:zc + 4]), op=ALU.mult),
                          reads=[("KVL", i), "ZET"], writes=[("VZ", i)])
                    for h in range(4):
                        S.add("pe", lambda e, i=i, h=h: e.matmul(IP[:, h, :], lhsT=KVL[i][:, h * 128:(h + 1) * 128],
                                                                rhs=VZ[i][:, h * 128:(h + 1) * 128], start=True, stop=True),
                              reads=[("KVL", i), ("VZ", i)], writes=["IP"])
                    nxt = 1 - cur
                    S.add("dve", lambda e, cur=cur, nxt=nxt, zc=zc: e.tensor_tensor(
                        out=v4(ST[nxt]), in0=v4(ST[cur]), in1=bc4(CDC[:, zc:zc + 4]), op=ALU.mult),
                          reads=[(skey, cur), "CDC"], writes=[(skey, nxt)])
                    S.add("dve", lambda e, nxt=nxt: e.tensor_tensor(
                        out=ST[nxt][:], in0=ST[nxt][:], in1=IP[:].rearrange("p h e -> p (h e)"), op=ALU.add),
                          reads=["IP", (skey, nxt)], writes=[(skey, nxt)])
                    cur = nxt
                return cur

            cur = scan(list(range(NCH - 1, NCP - 1, -1)), "b", RBST, "RBST", True)
            S.add("sp", lambda e, cur=cur: e.dma_start(out=PKG1[:, 512:1024], in_=RBST[cur][:]),
                  reads=[("RBST", cur)], writes=["PKG1"], dma=True)
            cur = scan(list(range(NCP, NCH)), "f", RFST, "RFST", False)
            S.add("sp", lambda e, cur=cur: e.dma_start(out=PKG1[:, 0:512], in_=RFST[cur][:]),
                  reads=[("RFST", cur)], writes=["PKG1"], dma=True)
            cS0 = G.fmcol(NCP)
            cS1 = G.fmcol(NCH - 1)
            S.add("sp", lambda e: e.dma_start(out=PKGK[:, 0:128], in_=FM[4, :, cS0:cS0 + 128]), writes=["PKGK"], dma=True)
            S.add("sp", lambda e: e.dma_start(out=PKGK[:, 128:256], in_=FM[4, :, cS1:cS1 + 128]), writes=["PKGK"], dma=True)
            S.add("sp", lambda e: e.dma_start(out=PKGK[:, 256:384], in_=TMs[cS0:cS0 + 128, VA0:VA0 + 128]), writes=["PKGK"], dma=True)
            S.add("sp", lambda e: e.dma_start(out=PKGK[:, 384:512], in_=TMs[cS1:cS1 + 128, VA0:VA0 + 128]), writes=["PKGK"], dma=True)
            S.add("pool", lambda e: e.collective_compute("AllGather", ALU.bypass, replica_groups=GROUPS,
                                                         ins=[PKG1_t.ap().opt()], outs=[G1_t.ap().opt()]),
                  reads=["PKG1"], writes=["G1"], cc=True)
            S.add("pool", lambda e: e.collective_compute("AllGather", ALU.bypass, replica_groups=GROUPS,
                                                         ins=[PKGK_t.ap().opt()], outs=[GK_t.ap().opt()]),
                  reads=["PKGK"], writes=["GK"], cc=True)
            cur = scan(list(range(NCP - 1, -1, -1)), "b", RBST, "RBST", True)

            def unpack():
                cL = PL
                cR = PL + 128 + SL
                S.add("sp", lambda e: e.dma_start(out=FM[4, :, cL:cL + 128], in_=GK[0:128, 128:256]),
                      reads=["GK"], writes=[("FMh", "L")], dma=True)
                S.add("sp", lambda e: e.dma_start(out=FM[4, :, cR:cR + 128], in_=GK[128:256, 0:128]),
                      reads=["GK"], writes=[("FMh", "R")], dma=True)
                S.add("sp", lambda e: e.dma_start(out=TMs[cL:cL + 128, VA0:VA0 + 128], in_=GK[0:128, 384:512]),
                      reads=["GK"], writes=[("TMh", "L")], dma=True)
                S.add("sp", lambda e: e.dma_start(out=TMs[cR:cR + 128, VA0:VA0 + 128], in_=GK[128:256, 256:384]),
                      reads=["GK"], writes=[("TMh", "R")], dma=True)
                S.add("sp", lambda e: e.dma_start(out=SINF[:], in_=G1[0:128, 0:512]), reads=["G1"], writes=["SINF"], dma=True)
                S.add("sp", lambda e: e.dma_start(out=SINB[:], in_=G1[128:256, 512:1024]), reads=["G1"], writes=["SINB"], dma=True)
                S.add("dve", lambda e: e.tensor_scalar(out=SINF[:], in0=SINF[:], scalar1=CST[:, C_FL:C_FL + 1], scalar2=None,
                                                       op0=ALU.mult), reads=["SINF", "CST"], writes=["SINF"])
                S.add("dve", lambda e: e.tensor_scalar(out=SINB[:], in0=SINB[:], scalar1=CST[:, C_FR:C_FR + 1], scalar2=None,
                                                       op0=ALU.mult), reads=["SINB", "CST"], writes=["SINB"])

            def loads(n):
                i = n % 2
                sg = G.seg(n)
                c0 = G.fmcol(n)
                lo = NCP if sg else 0
                hi = NCH if sg else NCP
                jl = 0 if (n > lo or sg == 1) else 1
                jh = 2 if (n < hi - 1 or sg == 1) else 1
                extra = []
                if sg == 1 and n == lo:
                    extra = [("FMh", "L"), ("TMh", "L")]
                if sg == 1 and n == hi - 1:
                    extra = extra + [("FMh", "R"), ("TMh", "R")]
                S.add("sp", lambda e: e.dma_start(out=QA[i][:], in_=FM[0:4, :, c0:c0 + 128].rearrange("f p t -> p f t")),
                      writes=[("QA", i)], dma=True)
                ka0 = c0 + (jl - 1) * 128
                nk = (jh - jl + 1) * 128
                S.add("sp", lambda e: e.dma_start(out=KA[i][:, jl * 128:jl * 128 + nk], in_=FM[4, :, ka0:ka0 + nk]),
                      reads=extra, writes=[("KA", i)], dma=True)
                for j in range(jl, jh + 1):
                    rj = c0 + (j - 1) * 128
                    S.add("sp", lambda e, j=j, rj=rj: e.dma_start(
                        out=VA1[i][:, j, :, 0:64], in_=TMs[rj:rj + 128, VA0:VA0 + 128].rearrange("p (k d) -> p k d", k=2)),
                          reads=extra, writes=[("VA1", i, j)], dma=True)
                S.add("sp", lambda e: e.dma_start(out=QR[i][:], in_=FM[5:9, :, c0:c0 + 128].rearrange("f p t -> p f t")),
                      writes=[("QR", i)], dma=True)
                S.add("sp", lambda e: e.dma_start(out=KR[i][:], in_=FM[9:13, :, c0:c0 + 128].rearrange("f p t -> p f t")),
                      writes=[("KR", i)], dma=True)
                S.add("sp", lambda e: e.dma_start(out=KVL[i][:], in_=TMs[c0:c0 + 128, 0:1024]), writes=[("KVL", i)], dma=True)
                i3 = n % 3
                S.add("sp", lambda e: e.dma_start(out=GG[i3][:], in_=TMs[c0:c0 + 128, GR0:GR0 + 2560]), writes=[("GG", i3)], dma=True)
                S.add("sp", lambda e: e.dma_start(out=RBL[i][:], in_=RB[n, :, :]), reads=[("RB", n)], writes=[("RBL", i)], dma=True)
                r0 = G.xrow(n)
                S.add("sp", lambda e: e.dma_start(out=XR[i3][:], in_=xsrc[r0:r0 + 128, :]), writes=[("XR", i3)], dma=True)
                return jl, jh

            rf = [0]
            ptc = [0]
            spc = [0]

            def phase1a(n, jl, jh):
                i = n % 2
                sg = G.seg(n)
                lo = NCP if sg else 0
                hi = NCH if sg else NCP
                DEN, ATT = DENs[i], ATTs[i]
                for kvh in range(2):
                    pts = []
                    for j in range(jl, jh + 1):
                        sp_ = spc[0] % 2
                        spc[0] += 1
                        pt = ptc[0] % 4
                        ptc[0] += 1
                        pts.append((j, pt))
                        S.add("pe", lambda e, j=j, sp_=sp_, kvh=kvh: e.matmul(
                            SP[sp_][:], lhsT=KA[i][kvh * 64:(kvh + 1) * 64, j * 128:(j + 1) * 128],
                            rhs=QA[i][kvh * 64:(kvh + 1) * 64, :, :].rearrange("p g q -> p (g q)"), start=True, stop=True),
                              reads=[("KA", i), ("QA", i)], writes=[("SP", sp_)])
                        S.add("act", lambda e, sp_=sp_, pt=pt: e.activation(out=PTs[pt][:], in_=SP[sp_][:], func=AF.Exp, scale=0.125),
                              reads=[("SP", sp_)], writes=[("PTs", pt)])
                        if j != 1:
                            mj = 0 if j == 0 else 1
                            edge = sg == 1 and ((j == 0 and n == lo) or (j == 2 and n == hi - 1))
                            msk = MKF if edge else MK
                            mkey = ("MKF", mj) if edge else "MK"
                            S.add("pool", lambda e, pt=pt, msk=msk, mj=mj: e.tensor_tensor(
                                out=PTs[pt][:], in0=PTs[pt][:], in1=msk[:, mj, :], op=ALU.mult),
                                  reads=[("PTs", pt), mkey], writes=[("PTs", pt)])
                    for idx, (j, pt) in enumerate(pts):
                        S.add("pe", lambda e, j=j, pt=pt, kvh=kvh, idx=idx, np_=len(pts): e.matmul(
                            OP[:], lhsT=VA1[i][:, j, kvh, :], rhs=PTs[pt][:], start=(idx == 0), stop=(idx == np_ - 1)),
                              reads=[("VA1", i, j), ("VA1o", i), ("PTs", pt)], writes=["OP"])
                    S.add("dve", lambda e, kvh=kvh: e.tensor_tensor(
                        out=DEN[:].rearrange("p (g q) -> p g q", g=4), in0=OP[64:128, :].rearrange("p (g q) -> p g q", g=4),
                        in1=ESK[0:64, kvh * 4:(kvh + 1) * 4].unsqueeze(2).broadcast_to([64, 4, 128]), op=ALU.add),
                          reads=["OP", "ESK"], writes=[("DEN", i)])
                    S.add("dve", lambda e: e.reciprocal(out=DEN[:], in_=DEN[:]), reads=[("DEN", i)], writes=[("DEN", i)])
                    S.add("dve", lambda e, kvh=kvh: e.tensor_tensor(
                        out=ATT[kvh][:].rearrange("p g q -> p (g q)"), in0=OP[0:64, :], in1=DEN[:], op=ALU.mult),
                          reads=["OP", ("DEN", i)], writes=[("ATT", i, kvh)])
            def phase1r(n):
                i = n % 2
                sg = G.seg(n)
                lo = NCP if sg else 0
                hi = NCH if sg else NCP
                INM, QXF, QXB, RFB, RBB, RBT = INMs[i], QXFs[i], QXBs[i], RFBs[i], RBBs[i], RBTs[i]
                SQ, SSR, TR, T2R, RET = SQs[i], SSRs[i], TRs[i], T2Rs[i], RETs[i]
                if n == lo:
                    if sg == 0:
                        S.add("dve", lambda e, c=rf[0]: e.memset(RFST[c][:], 0.0), writes=[("RFST", rf[0])])
                    else:
                        S.add("dve", lambda e, c=rf[0]: e.tensor_copy(out=RFST[c][:], in_=SINF[:]),
                              reads=["SINF"], writes=[("RFST", rf[0])])
                cur = rf[0]
                for h in range(4):
                    S.add("pe", lambda e, h=h: e.matmul(IP[:, h, :], lhsT=KR[i][:, h, :], rhs=QR[i][:, h, :], start=True, stop=True),
                          reads=[("KR", i), ("QR", i)], writes=["IP"])
                S.add("dve", lambda e: e.tensor_tensor(out=INM[:], in0=IP[:], in1=DTOT[:], op=ALU.mult),
                      reads=["IP"] + DTK, writes=[("INM", i)])
                S.add("pool", lambda e: e.tensor_tensor(out=QXF[:], in0=QR[i][:], in1=XIF[:], op=ALU.mult),
                      reads=[("QR", i)] + XIFK, writes=[("QXF", i)])
                S.add("pool", lambda e: e.tensor_tensor(out=QXB[:], in0=QR[i][:], in1=XIB[:], op=ALU.mult),
                      reads=[("QR", i)] + XIBK, writes=[("QXB", i)])
                S.add("act", lambda e, cur=cur: e.activation(out=RFB[:], in_=RFST[cur][:], func=AF.Copy),
                      reads=[("RFST", cur)], writes=[("RFB", i)])
                if sg == 0:
                    S.add("act", lambda e: e.activation(out=RBB[:], in_=RBL[i][:], func=AF.Copy),
                          reads=[("RBL", i)], writes=[("RBB", i)])
                else:
                    jloc = n - lo
                    S.add("pool", lambda e, jloc=jloc: e.tensor_tensor(out=v4(RBT), in0=v4(SINB), in1=bc4(WBW[:, jloc, :]), op=ALU.mult),
                          reads=["SINB", "WBW"], writes=[("RBT", i)])
                    S.add("pool", lambda e: e.tensor_tensor(out=RBB[:], in0=RBT[:], in1=RBL[i][:], op=ALU.add),
                          reads=[("RBT", i), ("RBL", i)], writes=[("RBB", i)])
                for h in range(4):
                    hs_ = slice(h * 128, (h + 1) * 128)
                    S.add("pe", lambda e, h=h, hs_=hs_: e.matmul(ORp[:, h, :], lhsT=INM[:, h, :], rhs=KVL[i][:, 512 + h * 128:512 + (h + 1) * 128],
                                                               start=True, stop=False),
                          reads=[("INM", i), ("KVL", i)], writes=["ORp"])
                    S.add("pe", lambda e, h=h, hs_=hs_: e.matmul(ORp[:, h, :], lhsT=QXF[:, h, :], rhs=RFB[:, hs_], start=False, stop=False),
                          reads=[("QXF", i), ("RFB", i)], writes=["ORp"])
                    S.add("pe", lambda e, h=h, hs_=hs_: e.matmul(ORp[:, h, :], lhsT=QXB[:, h, :], rhs=RBB[:, hs_], start=False, stop=True),
                          reads=[("QXB", i), ("RBB", i)], writes=["ORp"])
                vz = i
                S.add("pool", lambda e: e.tensor_tensor(out=v4(VZ[vz]), in0=KVL[i][:, 512:1024].rearrange("p (h e) -> p h e", h=4),
                                                        in1=bc4(ZET[:, 0:4]), op=ALU.mult),
                      reads=[("KVL", i), "ZET"], writes=[("VZ", vz)])
                S.add("act", lambda e: e.activation(out=SQ[:], in_=ORp[:].rearrange("p h e -> p (h e)"), func=AF.Square),
                      reads=["ORp"], writes=[("SQ", i)])
                S.add("dve", lambda e: e.tensor_tensor(out=v4(TR), in0=ORp[:], in1=bc4(CST[:, C_ONE:C_ONE + 4]), op=ALU.mult),
                      reads=["ORp", "CST"], writes=[("TR", i)])
                for h in range(4):
                    S.add("pe", lambda e, h=h: e.matmul(IP[:, h, :], lhsT=KVL[i][:, h * 128:(h + 1) * 128],
                                                        rhs=VZ[vz][:, h * 128:(h + 1) * 128], start=True, stop=True),
                          reads=[("KVL", i), ("VZ", vz)], writes=["IP"])
                nxt = 1 - cur
                S.add("pool", lambda e, cur=cur, nxt=nxt: e.tensor_tensor(out=v4(RFST[nxt]), in0=v4(RFST[cur]), in1=bc4(CDC[:, 0:4]), op=ALU.mult),
                      reads=[("RFST", cur), "CDC"], writes=[("RFST", nxt)])
                S.add("dve", lambda e, nxt=nxt: e.tensor_tensor(out=RFST[nxt][:], in0=RFST[nxt][:], in1=IP[:].rearrange("p h e -> p (h e)"), op=ALU.add),
                      reads=["IP", ("RFST", nxt)], writes=[("RFST", nxt)])
                rf[0] = nxt
                S.add("dve", lambda e: e.tensor_reduce(out=SSR[:], in_=v4(SQ), axis=mybir.AxisListType.X, op=ALU.add),
                      reads=[("SQ", i)], writes=[("SSR", i)])
                S.add("dve", lambda e: e.tensor_scalar(out=SSR[:], in0=SSR[:], scalar1=1.0 / 128, scalar2=EPS, op0=ALU.mult, op1=ALU.add),
                      reads=[("SSR", i)], writes=[("SSR", i)])
                S.add("act", lambda e: e.activation(out=SSR[:], in_=SSR[:], func=AF.Sqrt), reads=[("SSR", i)], writes=[("SSR", i)])
                S.add("dve", lambda e: e.reciprocal(out=SSR[:], in_=SSR[:]), reads=[("SSR", i)], writes=[("SSR", i)])
                S.add("pool", lambda e: e.tensor_tensor(out=T2R[:], in0=NG[:], in1=GG[n % 3][:, 0:512], op=ALU.mult),
                      reads=["NG", ("GG", n % 3)], writes=[("T2R", i)])
                S.add("pool", lambda e: e.tensor_tensor(out=v4(TR), in0=v4(TR), in1=bc4(SSR[:, 0:4]), op=ALU.mult),
                      reads=[("TR", i), ("SSR", i)], writes=[("TR", i)])
                S.add("pool", lambda e: e.tensor_tensor(out=RET[:], in0=TR[:], in1=T2R[:], op=ALU.mult),
                      reads=[("TR", i), ("T2R", i)], writes=[("RET", i)])

            def phase2(n):
                i = n % 2
                i3 = n % 3
                ATT, RET, RETT, M1, M2, MER, MT = ATTs[i], RETs[i], RETTs[i], M1s[i], M2s[i], MERs[i], MTs[i]
                for h in range(4):
                    S.add("pe", lambda e, h=h: e.transpose(out=TPB[:, h, :], in_=RET[:, h * 128:(h + 1) * 128], identity=ident[:]),
                          reads=[("RET", i), "ident"], writes=["TPB"])
                S.add("act", lambda e: e.activation(out=RETT[:], in_=TPB[:, 0:4, :], func=AF.Copy), reads=["TPB"], writes=[("RETT", i)])
                for ct in range(2):
                    cs = slice(ct * 512, (ct + 1) * 512)
                    for hh in range(8):
                        kvh, g = hh // 4, hh % 4
                        S.add("pe", lambda e, hh=hh, kvh=kvh, g=g, cs=cs: e.matmul(APs[:], lhsT=ATT[kvh][:, g, :], rhs=WBA[:, hh, cs],
                                                                                 start=(hh == 0), stop=(hh == 7)),
                              reads=[("ATT", i, kvh)] + WBAK[hh], writes=["APs"])
                    for h in range(4):
                        S.add("pe", lambda e, h=h, cs=cs: e.matmul(RPs[:], lhsT=RETT[:, h, :], rhs=WBR[:, h, cs], start=(h == 0), stop=(h == 3)),
                              reads=[("RETT", i)] + WBRK[h], writes=["RPs"])
                    S.add("dve", lambda e, ct=ct: e.tensor_tensor(out=M1[:], in0=APs[:], in1=GG[i3][:, 512 + ct * 512:1024 + ct * 512], op=ALU.mult),
                          reads=["APs", ("GG", i3)], writes=[("M1", i)])
                    S.add("dve", lambda e, ct=ct: e.tensor_tensor(out=M2[:], in0=RPs[:], in1=GG[i3][:, 1536 + ct * 512:2048 + ct * 512], op=ALU.mult),
                          reads=["RPs", ("GG", i3)], writes=[("M2", i)])
                    S.add("pool", lambda e, cs=cs: e.tensor_tensor(out=MER[:, cs], in0=M1[:], in1=M2[:], op=ALU.add),
                          reads=[("M1", i), ("M2", i)], writes=[("MER", i, ct)])
                for k in range(8):
                    S.add("pe", lambda e, k=k: e.transpose(out=TPB[:, k, :], in_=MER[:, k * 128:(k + 1) * 128], identity=ident[:]),
                          reads=[("MER", i, k // 4), "ident"], writes=["TPB"])
                S.add("act", lambda e: e.activation(out=MT[:], in_=TPB[:], func=AF.Copy), reads=["TPB"], writes=[("MT", i)])
                for ct in range(2):
                    cs = slice(ct * 512, (ct + 1) * 512)
                    yp, ypk = (APs, "APs") if ct == 0 else (RPs, "RPs")
                    for k in range(8):
                        S.add("pe", lambda e, k=k, cs=cs, yp=yp: e.matmul(yp[:], lhsT=MT[:, k, :], rhs=WO[:, k, cs], start=(k == 0), stop=(k == 7)),
                              reads=[("MT", i)] + WOK[k], writes=[ypk])
                    S.add("dve", lambda e, cs=cs, yp=yp: e.tensor_tensor(out=XR[i3][:, cs], in0=yp[:], in1=XR[i3][:, cs], op=ALU.add),
                          reads=[ypk, ("XR", i3)], writes=[("XR", i3)])
                r1 = G.x1row(n)
                S.add("sp", lambda e: e.dma_start(out=X1[r1:r1 + 128, :], in_=XR[i3][:]), reads=[("XR", i3)], writes=[("X1", n)], dma=True)

            jj = {}
            jj[0] = loads(0)
            for n in range(NCH):
                if n + 1 < NCH:
                    if n + 1 == NCP:
                        unpack()
                    jj[n + 1] = loads(n + 1)
                streams = []
                for f_ in ((lambda: phase1a(n, *jj[n])), (lambda: phase1r(n)), ((lambda: phase2(n - 1)) if n >= 1 else None)):
                    if f_ is None:
                        continue
                    S.cap = []
                    f_()
                    streams.append(S.cap)
                    S.cap = None
                S.interleave(streams)
            phase2(NCH - 1)
            S.emit(nc, st)

    def pass_D(l, last):
        with ExitStack() as st:
            S = Sched()
            ident, CST = common(st, S)
            WF1 = sb(st, "WF1", [128, 8, 2 * DFF], BF16)
            WF2 = sb(st, "WF2", [128, NFC, D], BF16)
            GB = sb(st, "GBF", [128, D], F32)
            GBL = sb(st, "GBL", [128, D], F32)
            CW = sb(st, "CW", [128, NFC * 3], F32)
            CB = sb(st, "CB", [128, NFC], F32)
            XT = [sb(st, f"XT{i}", [128, D], F32) for i in range(4)]
            XN = [sb(st, f"XN{i}", [128, D], BF16) for i in range(2)]
            SQJ = sb(st, "SQJ", [128, D], BF16)
            SS = sb(st, "SS", [128, 16], F32)
            H2T = [sb(st, f"H2T{i}", [128, 8, 258], BF16) for i in range(2)]
            HALOX = sb(st, "HALOX", [128, D], F32)
            HALOT = sb(st, "HALOT", [128, 8, 128], BF16)
            GU = [sb(st, f"GU{i}", [128, 256], BF16) for i in range(3)]
            C1 = [sb(st, f"C1_{i}", [128, 256], F32) for i in range(2)]
            C2 = [sb(st, f"C2_{i}", [128, 256], F32) for i in range(2)]
            C3 = [sb(st, f"C3_{i}", [128, 256], F32) for i in range(2)]
            GE = [sb(st, f"GE{i}", [128, 256], F32) for i in range(2)]
            AS = [sb(st, f"AS{i}", [128, 258], F32) for i in range(2)]
            US = [sb(st, f"US{i}", [128, 256], F32) for i in range(3)]
            TP = ps(st, "TP", [128, 8, 128], BF16)
            PA = [ps(st, f"PA{i}", [128, 512]) for i in range(2)]
            PU = ps(st, "PU", [128, 512])
            PY = [ps(st, f"PY{i}", [128, 512]) for i in range(4)]

            S.add("sp", lambda e: e.dma_start(out=GB[:], in_=g_ffn[l:l + 1, :].partition_broadcast(128)), writes=["GB"], dma=True)
            S.add("sp", lambda e: e.dma_start(out=GBL[:], in_=g_fin[0:1, :].partition_broadcast(128)), writes=["GBL"], dma=True)
            S.add("sp", lambda e: e.dma_start(out=CW[:], in_=cwd[l, :, :]), writes=["CW"], dma=True)
            S.add("sp", lambda e: e.dma_start(out=CB[:], in_=cbd[l, :, :]), writes=["CB"], dma=True)
            rS0 = G.x1row(NCP)
            rS1 = G.x1row(NCH - 1) + 127
            S.add("sp", lambda e: e.dma_start(out=PKG2[0:1, :], in_=X1[rS0:rS0 + 1, :]), writes=["PKG2"], dma=True)
            S.add("sp", lambda e: e.dma_start(out=PKG2[1:2, :], in_=X1[rS1:rS1 + 1, :]), writes=["PKG2"], dma=True)
            S.add("pool", lambda e: e.collective_compute("AllGather", ALU.bypass, replica_groups=GROUPS,
                                                         ins=[PKG2_t.ap().opt()], outs=[G2_t.ap().opt()]),
                  reads=["PKG2"], writes=["G2"], cc=True)
            WF1K, WF2K = {}, {}
            for k in range(8):
                WF1K[k] = load_w(S, WF1[:, k, :], wf1[l, k * 128:(k + 1) * 128, :], ("WF1", k), maxc=1408)
            for fc in range(NFC):
                WF2K[fc] = load_w(S, WF2[:, fc, :], wf2[l, fc * 128:(fc + 1) * 128, :], ("WF2", fc), maxc=1024)
            S.add("dve", lambda e: e.memset(HALOX[:], 0.0), writes=[("HALOX", q_) for q_ in range(1, 64)])
            nbP = PL // 256
            nbS = SL // 256
            S.add("sp", lambda e: e.dma_start(out=HALOX[1:2, :], in_=X1[0:1, :]), writes=[("HALOX", 1)], dma=True)
            for b in range(1, nbP):
                S.add("sp", lambda e, b=b: e.dma_start(out=HALOX[2 * b:2 * b + 2, :], in_=X1[256 * b - 1:256 * b + 1, :]),
                      writes=[("HALOX", 10 + b)], dma=True)
            S.add("sp", lambda e: e.dma_start(out=HALOX[2 * nbP:2 * nbP + 1, :], in_=X1[PL - 1:PL, :]), writes=[("HALOX", 3)], dma=True)
            hb = 2 * (nbP + 1)
            for b in range(1, nbS):
                r = PL + 1 + 256 * b - 1
                S.add("sp", lambda e, b=b, r=r: e.dma_start(out=HALOX[hb + 2 * b:hb + 2 * b + 2, :], in_=X1[r:r + 2, :]),
                      writes=[("HALOX", 30 + b)], dma=True)
            S.add("sp", lambda e: e.dma_start(out=HALOX[hb + 1:hb + 2, :], in_=X1[PL + 1:PL + 2, :]), writes=[("HALOX", 5)], dma=True)
            S.add("sp", lambda e: e.dma_start(out=HALOX[hb + 2 * nbS:hb + 2 * nbS + 1, :], in_=X1[PL + SL:PL + SL + 1, :]),
                  writes=[("HALOX", 6)], dma=True)
            S.add("sp", lambda e: e.dma_start(out=HALOX[hb:hb + 1, :], in_=G2[1:2, :]), reads=["G2"], writes=[("HALOX", 7)], dma=True)
            S.add("sp", lambda e: e.dma_start(out=HALOX[hb + 2 * nbS + 1:hb + 2 * nbS + 2, :], in_=G2[2:3, :]),
                  reads=["G2"], writes=[("HALOX", 8)], dma=True)
            prep_chunk(S, "H", None, 128, HALOX, [("HALOX", q_) for q_ in range(1, 64)], XN[0], ("XN", 0), SQJ, SS, 15, GB, "GB", TP, "TP", ident,
                       HALOT[:], ["HALOT"], rowscale=CST[:, C_HFL:C_HFL + 1])

            S.mark("D halo done")
            tiles = []
            for b in range(nbP):
                tiles.append((b * 2, 2 * b, 2 * (b + 1) + 1))
            for b in range(nbS):
                tiles.append((NCP + b * 2, hb + 2 * b, hb + 2 * (b + 1) + 1))
            cctr = [0]

            def prep_tile(ti):
                n0, hl, hr = tiles[ti]
                hs = ti % 2
                for c in range(2):
                    n = n0 + c
                    xs = (ti % 2) * 2 + c
                    i = cctr[0] % 2
                    cctr[0] += 1
                    r1 = G.x1row(n)
                    prep_chunk(S, "D", X1[r1:r1 + 128, :], 128, XT[xs], ("XT", xs), XN[i], ("XN", i), SQJ, SS, xs,
                               GB, "GB", TP, "TP", ident, H2T[hs][:, :, 1 + c * 128:1 + (c + 1) * 128], [("H2T", hs, c)])
                S.add("dve", lambda e: e.tensor_copy(out=H2T[hs][:, :, 0:1], in_=HALOT[:, :, hl:hl + 1]),
                      reads=["HALOT"], writes=[("H2T", hs, "l")])
                S.add("dve", lambda e: e.tensor_copy(out=H2T[hs][:, :, 257:258], in_=HALOT[:, :, hr:hr + 1]),
                      reads=["HALOT"], writes=[("H2T", hs, "r")])

            fcc = [0]

            def ffn_out(ti, fc, g):
                for sbk in range(2):
                    for ct in range(2):
                        S.add("pe", lambda e, sbk=sbk, ct=ct, fc=fc, g=g: e.matmul(
                            PY[sbk * 2 + ct][:], lhsT=GU[g][:, sbk * 128:(sbk + 1) * 128], rhs=WF2[:, fc, ct * 512:(ct + 1) * 512],
                            start=(fc == 0), stop=(fc == NFC - 1)),
                              reads=[("GU", g)] + WF2K[fc], writes=[("PY", sbk * 2 + ct)])

            def main_tile(ti, mid):
                n0, hl, hr = tiles[ti]
                hs = ti % 2
                hk = [("H2T", hs, 0), ("H2T", hs, 1), ("H2T", hs, "l"), ("H2T", hs, "r")]
                def st1(fc):
                    p = fc % 2
                    for k in range(8):
                        S.add("pe", lambda e, k=k: e.matmul(PA[p][:, 0:258], lhsT=WF1[:, k, fc * 128:(fc + 1) * 128],
                                                            rhs=H2T[hs][:, k, :], start=(k == 0), stop=(k == 7)),
                              reads=WF1K[k] + hk, writes=[("PA", p)])
                    for k in range(8):
                        S.add("pe", lambda e, k=k: e.matmul(PU[:, 0:256], lhsT=WF1[:, k, DFF + fc * 128:DFF + (fc + 1) * 128],
                                                            rhs=H2T[hs][:, k, 1:257], start=(k == 0), stop=(k == 7)),
                              reads=WF1K[k] + hk, writes=["PU"])
                    S.add("act", lambda e: e.activation(out=AS[p][:], in_=PA[p][:, 0:258], func=AF.Copy),
                          reads=[("PA", p)], writes=[("AS", p)])
                    S.add("act", lambda e: e.activation(out=US[fc % 3][:], in_=PU[:, 0:256], func=AF.Copy),
                          reads=["PU"], writes=[("US", fc % 3)])

                def st2(fc):
                    p = fc % 2
                    S.add("act", lambda e: e.activation(out=C1[p][:], in_=AS[p][:, 1:257], func=AF.Identity,
                                                        scale=CW[:, fc * 3 + 1:fc * 3 + 2], bias=CB[:, fc:fc + 1]),
                          reads=[("AS", p), "CW", "CB"], writes=[("C1", p)])
                    S.add("dve", lambda e: e.scalar_tensor_tensor(out=C2[p][:], in0=AS[p][:, 0:256], scalar=CW[:, fc * 3:fc * 3 + 1],
                                                                  in1=C1[p][:], op0=ALU.mult, op1=ALU.add),
                          reads=[("AS", p), ("C1", p), "CW"], writes=[("C2", p)])
                    S.add("dve", lambda e: e.scalar_tensor_tensor(out=C3[p][:], in0=AS[p][:, 2:258], scalar=CW[:, fc * 3 + 2:fc * 3 + 3],
                                                                  in1=C2[p][:], op0=ALU.mult, op1=ALU.add),
                          reads=[("AS", p), ("C2", p), "CW"], writes=[("C3", p)])

                def st3(fc):
                    p = fc % 2
                    g = fc % 3
                    S.add("act", lambda e: e.activation(out=GE[p][:], in_=C3[p][:], func=AF.Gelu),
                          reads=[("C3", p)], writes=[("GE", p)])
                    S.add("pool", lambda e: e.tensor_tensor(out=GU[g][:], in0=US[g][:], in1=GE[p][:], op=ALU.mult),
                          reads=[("US", g), ("GE", p)], writes=[("GU", g)])

                for t in range(NFC + 3):
                    if t == NFC // 2 and mid is not None:
                        mid()
                    if t < NFC:
                        S.mark(f"D tile {ti} fc {t}")
                        st1(t)
                    if 0 <= t - 3 < NFC:
                        ffn_out(ti, t - 3, (t - 3) % 3)
                    if 0 <= t - 1 < NFC:
                        st2(t - 1)
                    if 0 <= t - 2 < NFC:
                        st3(t - 2)
                S.mark(f"D tile {ti} ffn done")
                for sbk in range(2):
                    xs = (ti % 2) * 2 + sbk
                    n = n0 + sbk
                    for ct in range(2):
                        cs = slice(ct * 512, (ct + 1) * 512)
                        S.add("dve", lambda e, xs=xs, cs=cs, sbk=sbk, ct=ct: e.tensor_tensor(
                            out=XT[xs][:, cs], in0=PY[sbk * 2 + ct][:], in1=XT[xs][:, cs], op=ALU.add),
                              reads=[("PY", sbk * 2 + ct), ("XT", xs)], writes=[("XT", xs)])
                    r0 = G.xrow(n)
                    if not last:
                        S.add("sp", lambda e, xs=xs, r0=r0: e.dma_start(out=X2[r0:r0 + 128, :], in_=XT[xs][:]),
                              reads=[("XT", xs)], writes=[("X2", n)], dma=True)
                    else:
                        sc = SS[:, 8 + xs:9 + xs]
                        sk = ("SSF", xs)
                        S.add("act", lambda e, xs=xs, sc=sc: e.activation(out=SQJ[:], in_=XT[xs][:], func=AF.Square, accum_out=sc),
                              reads=[("XT", xs)], writes=[("SQJ", "D"), sk])
                        S.add("dve", lambda e, sc=sc: e.tensor_scalar(out=sc, in0=sc, scalar1=1.0 / D, scalar2=EPS, op0=ALU.mult, op1=ALU.add),
                              reads=[sk], writes=[sk])
                        S.add("act", lambda e, sc=sc: e.activation(out=sc, in_=sc, func=AF.Sqrt), reads=[sk], writes=[sk])
                        S.add("dve", lambda e, sc=sc: e.reciprocal(out=sc, in_=sc), reads=[sk], writes=[sk])
                        S.add("dve", lambda e, xs=xs, sc=sc: e.scalar_tensor_tensor(out=XT[xs][:], in0=XT[xs][:], scalar=sc, in1=GBL[:],
                                                                                   op0=ALU.mult, op1=ALU.mult),
                              reads=[("XT", xs), sk, "GBL"], writes=[("XT", xs)])
                        S.add("sp", lambda e, xs=xs, r0=r0: e.dma_start(out=y_out[r0:r0 + 128, :], in_=XT[xs][:]),
                              reads=[("XT", xs)], writes=[("Y", n)], dma=True)

            prep_tile(0)
            for ti in range(len(tiles)):
                mid = (lambda t=ti + 1: prep_tile(t)) if ti + 1 < len(tiles) else None
                main_tile(ti, mid)
            S.emit(nc, st)

    Sched.GLOBAL.clear()
    Sched.NINST[0] = 0
    gstack = ExitStack()
    Sched.GLOBAL["stack"] = gstack
    plist = []
    for l in range(DEPTH):
        xsrc = x_in if l == 0 else X2
        plist.append(lambda l=l, xsrc=xsrc: pass_A(l, xsrc))
        plist.append(lambda l=l, xsrc=xsrc: pass_B(l, xsrc))
        plist.append(lambda l=l: pass_D(l, l == DEPTH - 1))
    with gstack:
        for f in plist[:npasses]:
            f()
    return nc, G


def _consts(G, core):
    PL, SL, T, NCS = G.PL, G.SL, G.T, G.NCS
    h = core % 2
    pos = np.concatenate([np.arange(PL), h * SL + np.arange(SL)]).astype(np.float32)

    def tabs(half):
        fr = (np.float32(10000.0) ** (-np.arange(half, dtype=np.float32) / np.float32(half))).astype(np.float32)
        ang = (pos[:, None] * fr[None, :]).astype(np.float32)
        return np.cos(ang).astype(np.float32), np.sin(ang).astype(np.float32)

    ca, sa = tabs(32)
    cr, sr = tabs(64)
    fmtab = np.zeros((128, 4, T), np.float32)
    p = np.arange(128)
    da = p % 64
    fmtab[:, 0, :] = ca[:, da % 32].T
    fmtab[:, 1, :] = sa[:, da % 32].T
    fmtab[:, 2, :] = cr[:, p % 64].T
    fmtab[:, 3, :] = sr[:, p % 64].T
    rot = np.zeros((128, 256), np.float32)
    for m in range(128):
        d = m % 64
        if d < 32:
            rot[m + 32, m] = -1.0
        else:
            rot[m - 32, m] = 1.0
        if m < 64:
            rot[m + 64, 128 + m] = -1.0
        else:
            rot[m - 64, 128 + m] = 1.0
    tmtab = np.concatenate([cr, -sr, sr], axis=1).astype(np.float32)
    NCST = 528 + NCS
    cst = np.zeros((128, NCST), np.float32)
    k = np.arange(128)[:, None].astype(np.float32)
    q = np.arange(128)[None, :].astype(np.float32)
    cst[:, 0:128] = np.maximum(q - k, 0)
    cst[:, 128:256] = np.maximum(k - q, 0)
    cst[:, 256:384] = q + 1
    cst[:, 384:512] = 128 - q
    cst[:, 512] = k[:, 0]
    cst[:, 513] = 127 - k[:, 0]
    cst[:, 514] = float(h)
    cst[:, 515] = float(1 - h)
    hfl = np.ones(128, np.float32)
    hb = 2 * (PL // 256 + 1)
    hfl[hb] = float(h)
    hfl[hb + 2 * (SL // 256) + 1] = float(1 - h)
    cst[:, 516] = hfl
    cst[:, 520:524] = 1.0
    cst[:, 528:528 + NCS] = (128.0 * (NCS - 1 - np.arange(NCS)))[None, :]
    mk = np.zeros((128, 2, 512), np.float32)
    kk = np.arange(128)[:, None]
    qq = np.arange(128)[None, :]
    mk[:, 0, :] = np.tile((kk >= qq).astype(np.float32), (1, 4))
    mk[:, 1, :] = np.tile((kk <= qq).astype(np.float32), (1, 4))
    return dict(fmtab=fmtab, tmtab=tmtab, cst=cst, masks=mk.reshape(128, 1024), ident=np.eye(128, dtype=np.float32), rot=rot)


def _perm_w_in(w_in):
    cols = []
    for g in range(4):
        cols += list(range(g * 64, (g + 1) * 64)) + list(range((4 + g) * 64, (5 + g) * 64))
    cols += list(range(512, 640))
    cols += list(range(768, 1280))
    cols += list(range(1280, 1792))
    cols += list(range(1280, 1792)) + list(range(1792, 2304)) + list(range(2304, 2816)) + list(range(2816, 4864)) + list(range(640, 768))
    return np.ascontiguousarray(w_in[:, :, np.array(cols)])


def _shared_inputs(inp):
    f = lambda a: np.ascontiguousarray(np.asarray(a, dtype=np.float32))
    cw = f(inp["conv_w"])
    cwl = np.ascontiguousarray(cw.reshape(DEPTH, 3, NFC, 128).transpose(0, 3, 2, 1).reshape(DEPTH, 128, NFC * 3))
    cb = f(inp["conv_b"])
    cbl = np.ascontiguousarray(cb.reshape(DEPTH, NFC, 128).transpose(0, 2, 1))
    return dict(
        w_in=_perm_w_in(f(inp["w_in"])), wba=f(inp["w_branch_attn"]), wbr=f(inp["w_branch_ret"]), wo=f(inp["w_out"]),
        wf1=f(inp["w_ffn_in"]), wf2=f(inp["w_ffn_out"]), g_mix=f(inp["norm_mix_g"]), g_ffn=f(inp["norm_ffn_g"]),
        g_fin=f(inp["final_norm_g"]).reshape(1, D), g_ret=f(inp["ret_norm_g"]), sink=f(inp["attn_sink"]),
        ldf=f(inp["ret_log_decay_f"]), ldb=f(inp["ret_log_decay_b"]), cw=cwl, cb=cbl)


_CACHE = {}


def run(inp, PL, SL, debug=False, npasses=6, trace=False):
    key = (PL, SL, debug, npasses)
    if key not in _CACHE:
        _CACHE[key] = build(PL, SL, debug, npasses)
    nc, G = _CACHE[key]
    shared = _shared_inputs(inp)
    xp = np.asarray(inp["x_prompt"], dtype=np.float32)
    xs = np.asarray(inp["x_sample"], dtype=np.float32)
    in_maps = []
    for c in range(8):
        m = dict(shared)
        m.update(_consts(G, c))
        h = c % 2
        m["x"] = np.ascontiguousarray(np.concatenate([xp[c], xs[c // 2, h * SL:(h + 1) * SL]], axis=0))
        in_maps.append(m)
    if trace:
        res = run_bass_kernel_spmd(nc, in_maps, core_ids=list(range(8)), trace=True)
    else:
        res = run_bass_kernel_spmd(nc, in_maps, core_ids=list(range(8)))
    yp = np.stack([res.results[c]["y"][:PL] for c in range(8)], axis=0)
    ys = np.stack([np.concatenate([res.results[2 * s]["y"][PL:], res.results[2 * s + 1]["y"][PL:]], axis=0) for s in range(4)], axis=0)
    return (yp.astype(np.float32), ys.astype(np.float32)), res


def kernel(**inputs):
    out, _ = run(inputs, 2048, 4096)
    return out
```

```python
from contextlib import ExitStack
import numpy as np
import concourse.bass as bass
import concourse.mybir as mybir
from concourse.bass_utils import run_bass_kernel_spmd

F32 = mybir.dt.float32
BF16 = mybir.dt.bfloat16
AF = mybir.ActivationFunctionType
ALU = mybir.AluOpType

D = 1024
DEPTH = 2
DFF = 2816
NFC = DFF // 128
INW = 5376
NFM = 13
TMW = 3712
KR0, VR0, GR0, GT0, VA0 = 0, 512, 1024, 1536, 3584
EPS = 1e-6
GROUPS = [[0, 1], [2, 3], [4, 5], [6, 7]]
ENGS = ("pe", "act", "dve", "pool", "sp")


class Sched:
    NDS = 14
    EPOCH = 12000
    UID = [0]

    NINST = [0]
    GLOBAL = {}

    def __init__(self):
        self.ops = []
        self.lastw = {}
        self.readers = {}
        self.inst = Sched.NINST[0]
        Sched.NINST[0] += 1

    PSUM_NAMES = {"PF", "PT", "TP", "PR", "SP", "OP", "IP", "ORp", "TPB", "APs", "RPs", "PA", "PU", "PY"}

    cap = None

    def interleave(self, streams):
        pos = [0] * len(streams)
        total = sum(len(x) for x in streams)
        for _ in range(total):
            best, bf = None, None
            for si, st_ in enumerate(streams):
                if pos[si] < len(st_):
                    f = pos[si] / len(st_)
                    if bf is None or f < bf:
                        best, bf = si, f
            a = streams[best][pos[best]]
            pos[best] += 1
            self.add(*a[0], **a[1])

    def add(self, eng, fn, reads=(), writes=(), dma=False, cc=False):
        if self.cap is not None:
            self.cap.append(((eng, fn), dict(reads=list(reads), writes=list(writes), dma=dma, cc=cc)))
            return None
        def _ps(k):
            return (k if isinstance(k, str) else k[0]) in self.PSUM_NAMES
        ps_reads = [k for k in reads if _ps(k) and k not in writes]
        reads = [k for k in reads if not _ps(k)]
        writes = list(writes)
        import os
        cut = int(os.environ.get("KCUT", "0"))
        if cut and len(self.ops) >= cut and self.inst == int(os.environ.get("KCUTPASS", "0")):
            return None
        idx = len(self.ops)
        deps = {}

        def dep(i, raw):
            if i is None or i == idx:
                return
            deps[i] = deps.get(i, False) or raw

        for k in list(reads) + ps_reads:
            dep(self.lastw.get(k), True)
        for k in ps_reads:
            for r in self.readers.get(k, ()):
                dep(r, False)
        for k in writes:
            dep(self.lastw.get(k), False)
            for r in self.readers.get(k, ()):
                dep(r, False)
        for k in writes + ps_reads:
            self.lastw[k] = idx
            self.readers[k] = []
        for k in reads:
            if k not in writes:
                self.readers.setdefault(k, []).append(idx)
        self.ops.append(dict(eng=eng, fn=fn, deps=sorted(deps), raw=deps, dma=dma, cc=cc, marked=False))
        return idx

    @staticmethod
    def _skip(op, dop, d):
        if dop["dma"] or dop["cc"] or op["dma"] or op["cc"] or dop["eng"] != op["eng"]:
            return False
        if op["eng"] == "pe":
            return True
        return not op["raw"][d]

    def mark(self, name):
        import os
        if os.environ.get("KMARK"):
            print("MARK", self.inst, name, len(self.ops))

    def emit(self, nc, st):
        ops = self.ops
        for op in ops:
            for d in op["deps"]:
                dop = ops[d]
                if dop["dma"] or dop["cc"]:
                    continue
                if self._skip(op, dop, d):
                    continue
                dop["marked"] = True
        last = {}
        for i, op in enumerate(ops):
            if not op["dma"] and not op["cc"]:
                last[op["eng"]] = i
        for e, i in last.items():
            ops[i]["marked"] = True
        gs = self.GLOBAL
        cnt = gs.setdefault("cnt", {e: 0 for e in ENGS})
        dcount = gs.setdefault("dcount", {"sp": 0, "pool": 0})
        duse = gs.setdefault("duse", {"sp": [0] * self.NDS, "pool": [0] * self.NDS})
        SEM = gs.setdefault("SEM", {})
        gst = gs["stack"]
        ncc0 = gs.get("ncc", 0)
        ncc = ncc0
        semkeys = set()
        for op in ops:
            if op["cc"]:
                op["sem"] = ("cc", ncc)
                op["val"] = 1
                ncc += 1
            elif op["dma"]:
                q = op["eng"]
                j = dcount[q] % self.NDS
                dcount[q] += 1
                duse[q][j] += 1
                op["sem"] = ("d", q, j)
                op["val"] = 16 * duse[q][j]
            elif op["marked"]:
                e = op["eng"]
                ep = cnt[e] // self.EPOCH
                cnt[e] += 1
                op["sem"] = ("c", e, ep)
                op["val"] = cnt[e] - ep * self.EPOCH
            else:
                continue
            semkeys.add(op["sem"])
        gs["ncc"] = ncc
        for k in sorted(semkeys, key=str):
            if k not in SEM:
                Sched.UID[0] += 1
                SEM[k] = gst.enter_context(nc.semaphore(f"s{Sched.UID[0]}_" + "_".join(str(x) for x in k)))
        finals = []
        for e, i in last.items():
            finals.append((ops[i]["sem"], ops[i]["val"]))
        for q in ("sp", "pool"):
            for j in range(self.NDS):
                if duse[q][j]:
                    finals.append((("d", q, j), 16 * duse[q][j]))
        for c in range(ncc0, ncc):
            finals.append((("cc", c), 1))

        def run(engname, eng):
            waited = {}

            def wait(sk, val):
                if waited.get(sk, 0) >= val:
                    return
                eng.wait_ge(SEM[sk], val)
                waited[sk] = val

            for op in ops:
                if op["eng"] != engname:
                    continue
                for d in op["deps"]:
                    dop = ops[d]
                    if self._skip(op, dop, d):
                        continue
                    wait(dop["sem"], dop["val"])
                if op["dma"] and op["val"] > 16:
                    wait(op["sem"], op["val"] - 16)
                ins = op["fn"](eng)
                if op["cc"]:
                    ins.then_inc(SEM[op["sem"]])
                elif op["dma"]:
                    ins.then_inc(SEM[op["sem"]], 16)
                elif op["marked"]:
                    ins.then_inc(SEM[op["sem"]], 1)
            for sk, val in finals:
                wait(sk, val)

        block = st.enter_context(nc.Block())

        @block.tensor
        def _(e):
            run("pe", e)

        @block.scalar
        def _(e):
            run("act", e)

        @block.vector
        def _(e):
            run("dve", e)

        @block.gpsimd
        def _(e):
            run("pool", e)

        @block.sync
        def _(e):
            run("sp", e)


class Geo:
    def __init__(self, PL, SL):
        self.PL, self.SL = PL, SL
        self.NCP, self.NCS = PL // 128, SL // 128
        self.NCH = self.NCP + self.NCS
        self.T = PL + SL
        self.TC = self.T + 256
        self.NT = self.T // 512
        self.NDT = self.T // 256
        self.NHB = (PL // 256 + 1) + (SL // 256 + 1)

    def seg(self, n):
        return 0 if n < self.NCP else 1

    def xrow(self, n):
        return n * 128

    def x1row(self, n):
        return n * 128 + (1 if n >= self.NCP else 0)

    def fmcol(self, n):
        return n * 128 + (128 if n >= self.NCP else 0)


def build(PL=2048, SL=4096, debug=False, npasses=6):
    G = Geo(PL, SL)
    T, TC, NCP, NCS, NCH = G.T, G.TC, G.NCP, G.NCS, G.NCH
    nc = bass.Bass("TRN2", target_bir_lowering=False)

    def din(name, shape, dt=F32):
        return nc.dram_tensor(name, list(shape), dt, kind="ExternalInput").ap()

    x_in = din("x", [T, D])
    w_in = din("w_in", [DEPTH, D, INW])
    wba = din("wba", [DEPTH, 512, D])
    wbr = din("wbr", [DEPTH, 512, D])
    wo = din("wo", [DEPTH, D, D])
    wf1 = din("wf1", [DEPTH, D, 2 * DFF])
    wf2 = din("wf2", [DEPTH, DFF, D])
    g_mix = din("g_mix", [DEPTH, D])
    g_ffn = din("g_ffn", [DEPTH, D])
    g_fin = din("g_fin", [1, D])
    g_ret = din("g_ret", [DEPTH, 512])
    sinkd = din("sink", [DEPTH, 8])
    ldf = din("ldf", [DEPTH, 4])
    ldb = din("ldb", [DEPTH, 4])
    cwd = din("cw", [DEPTH, 128, NFC * 3])
    cbd = din("cb", [DEPTH, 128, NFC])
    fmtab = din("fmtab", [128, 4, T])
    tmtab = din("tmtab", [T, 3 * 64])
    NCST = 528 + NCS
    cst = din("cst", [128, NCST])
    masks = din("masks", [128, 2 * 512])
    identd = din("ident", [128, 128])
    rotd = din("rot", [128, 256])

    okind = "ExternalOutput"
    y_out = nc.dram_tensor("y", [T, D], F32, kind=okind).ap()

    def scratch(name, shape, dt):
        if debug:
            return nc.dram_tensor(name, list(shape), dt, kind=okind)
        return nc.dram_tensor(name, list(shape), dt)

    FM_t = scratch("FM", [NFM, 128, TC], BF16)
    TM_t = scratch("TMs", [TC, TMW], BF16)
    RB_t = scratch("RB", [NCH, 128, 512], F32)
    X1_t = scratch("X1", [T + 2, D], F32)
    X2_t = scratch("X2", [T, D], F32)
    FM, TMs, RB, X1, X2 = FM_t.ap(), TM_t.ap(), RB_t.ap(), X1_t.ap(), X2_t.ap()
    PKG1_t = nc.dram_tensor("PKG1", [128, 1024], F32)
    G1_t = nc.dram_tensor("G1", [256, 1024], F32)
    PKGK_t = nc.dram_tensor("PKGK", [128, 512], BF16)
    GK_t = nc.dram_tensor("GK", [256, 512], BF16)
    PKG2_t = nc.dram_tensor("PKG2", [2, D], F32)
    G2_t = nc.dram_tensor("G2", [4, D], F32)
    PKG1, G1, PKGK, GK, PKG2, G2 = (t.ap() for t in (PKG1_t, G1_t, PKGK_t, GK_t, PKG2_t, G2_t))

    C_RELQK, C_RELKQ, C_QP1, C_Q128M = 0, 128, 256, 384
    C_IDXK, C_I127, C_FL, C_FR, C_HFL, C_ONE, C_NW = 512, 513, 514, 515, 516, 520, 528

    uid = [0]

    def sb(st, name, shape, dt):
        uid[0] += 1
        return st.enter_context(nc.sbuf_tensor(f"{name}_u{uid[0]}", list(shape), dt))

    def ps(st, name, shape, dt=F32):
        uid[0] += 1
        return st.enter_context(nc.psum_tensor(f"{name}_u{uid[0]}", list(shape), dt))

    def common(st, S):
        identf = sb(st, "identf", [128, 128], F32)
        ident = sb(st, "ident", [128, 128], BF16)
        CST = sb(st, "CST", [128, NCST], F32)
        S.add("sp", lambda e: e.dma_start(out=identf[:], in_=identd[:, :]), writes=["identf"], dma=True)
        S.add("sp", lambda e: e.dma_start(out=CST[:], in_=cst[:, :]), writes=["CST"], dma=True)
        S.add("dve", lambda e: e.tensor_copy(out=ident[:], in_=identf[:]), reads=["identf"], writes=["ident"])
        return ident, CST

    def load_w(S, dst, src, key, maxc=2048):
        ncols = src.shape[-1]
        c0 = 0
        keys = []
        while c0 < ncols:
            c1 = min(ncols, c0 + maxc)
            S.add("pool", lambda e, a=dst[:, c0:c1], b=src[:, c0:c1]: e.dma_start(out=a, in_=b),
                  writes=[(key, c0)], dma=True)
            keys.append((key, c0))
            c0 = c1
        return keys

    def prep_chunk(S, tag, src_ap, nrows, XTs, xkey, XNs, xnkey, SQJ, SS, ss_col, GB, gkey, TP, tpkey,
                   ident, dst_ap, dkeys, rowscale=None, phase="all"):
        xkeys = xkey if isinstance(xkey, list) else [xkey]
        if phase == "post":
            for k in range(8):
                S.add("pe", lambda e, k=k: e.transpose(out=TP[:, k, 0:nrows], in_=XNs[0:nrows, k * 128:(k + 1) * 128],
                                                       identity=ident[0:nrows, 0:nrows]),
                      reads=[xnkey, "ident"], writes=[tpkey])
            S.add("act", lambda e: e.activation(out=dst_ap, in_=TP[:, :, 0:nrows], func=AF.Copy),
                  reads=[tpkey], writes=dkeys)
            return
        if src_ap is not None:
            S.add("sp", lambda e: e.dma_start(out=XTs[0:nrows, :], in_=src_ap), writes=xkeys, dma=True)
        if rowscale is not None:
            S.add("dve", lambda e: e.tensor_scalar(out=XTs[:, :], in0=XTs[:, :], scalar1=rowscale, scalar2=None,
                                                   op0=ALU.mult), reads=["CST"], writes=xkeys)
        sc = SS[:, ss_col:ss_col + 1]
        sk = ("SS", tag, ss_col)
        S.add("act", lambda e: e.activation(out=SQJ[0:nrows, :], in_=XTs[0:nrows, :], func=AF.Square,
                                            accum_out=sc[0:nrows, :]),
              reads=xkeys, writes=[("SQJ", tag), sk])
        S.add("dve", lambda e: e.tensor_scalar(out=sc[0:nrows, :], in0=sc[0:nrows, :], scalar1=1.0 / D, scalar2=EPS,
                                               op0=ALU.mult, op1=ALU.add), reads=[sk], writes=[sk])
        S.add("act", lambda e: e.activation(out=sc[0:nrows, :], in_=sc[0:nrows, :], func=AF.Sqrt), reads=[sk], writes=[sk])
        S.add("dve", lambda e: e.reciprocal(out=sc[0:nrows, :], in_=sc[0:nrows, :]), reads=[sk], writes=[sk])
        S.add("dve", lambda e: e.scalar_tensor_tensor(out=XNs[0:nrows, :], in0=XTs[0:nrows, :], scalar=sc[0:nrows, :],
                                                      in1=GB[0:nrows, :], op0=ALU.mult, op1=ALU.mult),
              reads=xkeys + [sk, gkey], writes=[xnkey])
        if phase == "pre":
            return
        for k in range(8):
            S.add("pe", lambda e, k=k: e.transpose(out=TP[:, k, 0:nrows], in_=XNs[0:nrows, k * 128:(k + 1) * 128],
                                                   identity=ident[0:nrows, 0:nrows]),
                  reads=[xnkey, "ident"], writes=[tpkey])
        S.add("act", lambda e: e.activation(out=dst_ap, in_=TP[:, :, 0:nrows], func=AF.Copy),
              reads=[tpkey], writes=dkeys)

    def pass_A(l, xsrc):
        with ExitStack() as st:
            S = Sched()
            ident, CST = common(st, S)
            WIN = sb(st, "WIN", [128, 8, INW], BF16)
            GB = sb(st, "GB", [128, D], F32)
            XT = [sb(st, f"XT{i}", [128, D], F32) for i in range(4)]
            XN = [sb(st, f"XN{i}", [128, D], BF16) for i in range(4)]
            SQJ = sb(st, "SQJ", [128, D], BF16)
            SS = sb(st, "SS", [128, 8], F32)
            HT = [sb(st, f"HT{i}", [128, 8, 512], BF16) for i in range(2)]
            TAB = [sb(st, f"TAB{i}", [128, 4, 512], F32) for i in range(2)]
            TMT = [sb(st, f"TMT{i}", [128, 3, 64], F32) for i in range(2)]
            T1 = [sb(st, f"T1_{i}", [128, 512], F32) for i in range(2)]
            T2 = [sb(st, f"T2_{i}", [128, 512], F32) for i in range(2)]
            T1T = [sb(st, f"T1T_{i}", [128, 512], F32) for i in range(2)]
            T2T = [sb(st, f"T2T_{i}", [128, 512], F32) for i in range(2)]
            FMO = [sb(st, f"FMO{i}", [128, 512], BF16) for i in range(2)]
            TMO = [sb(st, f"TMO{i}", [128, TMW], BF16) for i in range(2)]
            TP = [ps(st, "TP0", [128, 8, 128], BF16)] * 2
            PF = [ps(st, f"PF{i}", [128, 512]) for i in range(3)]
            PT = [ps(st, f"PT{i}", [128, 512]) for i in range(3)]
            PR = ps(st, "PR", [128, 512])
            XB = [sb(st, f"XB{i}", [128, 512], BF16) for i in range(2)]
            ROTF = sb(st, "ROTF", [128, 256], F32)
            ROT = sb(st, "ROT", [128, 256], BF16)
            S.add("sp", lambda e: e.dma_start(out=ROTF[:], in_=rotd[:, :]), writes=["ROTF"], dma=True)
            S.add("dve", lambda e: e.tensor_copy(out=ROT[:], in_=ROTF[:]), reads=["ROTF"], writes=["ROT"])

            S.add("sp", lambda e: e.dma_start(out=GB[:], in_=g_mix[l:l + 1, :].partition_broadcast(128)),
                  writes=["GB"], dma=True)
            WINK = {}
            for k in range(8):
                WINK[k] = load_w(S, WIN[:, k, :], w_in[l, k * 128:(k + 1) * 128, :], ("WIN", k), maxc=1792)

            cctr = [0]
            fctr = [0]
            tctr = [0]

            def prep_tile(tt, phase="all"):
                hs = tt % 2
                for c in range(4):
                    n = tt * 4 + c
                    i = c
                    r0 = G.xrow(n)
                    prep_chunk(S, "A", xsrc[r0:r0 + 128, :], 128, XT[i], ("XT", i), XN[i], ("XN", i), SQJ, SS, i,
                               GB, "GB", TP[i % 2], ("TP", 0), ident, HT[hs][:, :, c * 128:(c + 1) * 128],
                               [("HT", hs, c)], phase=phase)

            def main_tile(tt, mid=None):
                hs = tt % 2
                hkeys = [("HT", hs, c) for c in range(4)]
                n0 = tt * 4
                c0 = G.fmcol(n0)
                r0 = G.xrow(n0)
                S.add("sp", lambda e: e.dma_start(out=TAB[hs][:], in_=fmtab[:, :, r0:r0 + 512]),
                      writes=[("TAB", hs)], dma=True)
                fm_pend = []
                for f in range(NFM):
                    pf = fctr[0] % 3
                    p = fctr[0] % 2
                    fctr[0] += 1
                    for k in range(8):
                        S.add("pe", lambda e, k=k, f=f, pf=pf: e.matmul(PF[pf][:], lhsT=WIN[:, k, f * 128:(f + 1) * 128],
                                                                       rhs=HT[hs][:, k, :], start=(k == 0), stop=(k == 7)),
                              reads=WINK[k] + hkeys, writes=[("PF", pf)])
                    att = f < 5
                    ci, si = (0, 1) if att else (2, 3)
                    ro = 0 if att else 128
                    S.add("act", lambda e, p=p, pf=pf: e.activation(out=XB[p][:], in_=PF[pf][:], func=AF.Copy),
                          reads=[("PF", pf)], writes=[("XB", p)])
                    S.add("dve", lambda e, p=p, pf=pf, ci=ci: e.tensor_tensor(out=T1[p][:], in0=PF[pf][:], in1=TAB[hs][:, ci, :],
                                                                             op=ALU.mult),
                          reads=[("PF", pf), ("TAB", hs)], writes=[("T1", p)])

                    def stage2(p=p, ro=ro, si=si, f=f):
                        S.add("pe", lambda e: e.matmul(PR[:], lhsT=ROT[:, ro:ro + 128], rhs=XB[p][:], start=True, stop=True),
                              reads=[("XB", p), "ROT"], writes=["PR"])
                        S.add("dve", lambda e: e.tensor_tensor(out=T2[p][:], in0=PR[:], in1=TAB[hs][:, si, :], op=ALU.mult),
                              reads=["PR", ("TAB", hs)], writes=[("T2", p)])
                        S.add("pool", lambda e: e.tensor_tensor(out=FMO[p][:], in0=T1[p][:], in1=T2[p][:], op=ALU.add),
                              reads=[("T1", p), ("T2", p)], writes=[("FMO", p)])
                        S.add("sp", lambda e: e.dma_start(out=FM[f, :, c0:c0 + 512], in_=FMO[p][:]),
                              reads=[("FMO", p)], writes=[("FMd", f, tt)], dma=True)
                    if fm_pend:
                        fm_pend.pop(0)()
                    fm_pend.append(stage2)
                while fm_pend:
                    fm_pend.pop(0)()
                if mid is not None:
                    mid()
                S.mark(f"A tile {tt} FM done")
                for c in range(4):
                    S.mark(f"A tile {tt} TM chunk {c}")
                    n = n0 + c
                    o_ = n % 2
                    rr = G.xrow(n)
                    S.add("sp", lambda e, o_=o_, rr=rr: e.dma_start(
                        out=TMT[o_][:], in_=tmtab[rr:rr + 128, :].rearrange("p (a b) -> p a b", a=3)),
                          writes=[("TMT", o_)], dma=True)
                    for ct in range(8):
                        p = tctr[0] % 3
                        tctr[0] += 1
                        w = 128 if ct == 7 else 512
                        wc0 = 1664 + ct * 512
                        for k in range(8):
                            S.add("pe", lambda e, k=k, p=p, w=w, wc0=wc0, c=c: e.matmul(
                                PT[p][:, 0:w], lhsT=HT[hs][:, k, c * 128:(c + 1) * 128], rhs=WIN[:, k, wc0:wc0 + w],
                                start=(k == 0), stop=(k == 7)),
                                  reads=WINK[k] + [("HT", hs, c)], writes=[("PT", p)])
                        okey = ("TMO", o_, ct)
                        if ct == 0:
                            q = p % 2
                            pv = PT[p][:].rearrange("p (h t d) -> p h t d", h=4, t=2)
                            t1v = T1T[q][:].rearrange("p (g d) -> p g d", d=64)
                            t2v = T2T[q][:].rearrange("p (h t d) -> p h t d", h=4, t=2)
                            S.add("dve", lambda e, p=p, o_=o_, t1v=t1v: e.tensor_tensor(
                                out=t1v, in0=PT[p][:].rearrange("p (g d) -> p g d", d=64),
                                in1=TMT[o_][:, 0:1, :].broadcast_to([128, 8, 64]), op=ALU.mult),
                                  reads=[("PT", p), ("TMT", o_)], writes=[("T1T", q)])
                            S.add("dve", lambda e, pv=pv, t2v=t2v, o_=o_: e.tensor_tensor(
                                out=t2v[:, :, 0, :], in0=pv[:, :, 1, :],
                                in1=TMT[o_][:, 1:2, :].broadcast_to([128, 4, 64]), op=ALU.mult),
                                  reads=[("PT", p), ("TMT", o_)], writes=[("T2T", q, "a")])
                            S.add("dve", lambda e, pv=pv, t2v=t2v, o_=o_: e.tensor_tensor(
                                out=t2v[:, :, 1, :], in0=pv[:, :, 0, :],
                                in1=TMT[o_][:, 2:3, :].broadcast_to([128, 4, 64]), op=ALU.mult),
                                  reads=[("PT", p), ("TMT", o_)], writes=[("T2T", q, "b")])
                            S.add("pool", lambda e, q=q, o_=o_: e.tensor_tensor(
                                out=TMO[o_][:, KR0:KR0 + 512], in0=T1T[q][:], in1=T2T[q][:], op=ALU.add),
                                  reads=[("T1T", q), ("T2T", q, "a"), ("T2T", q, "b")], writes=[okey])
                        else:
                            func = AF.Copy if ct in (1, 7) else (AF.Silu if ct == 2 else AF.Sigmoid)
                            dc0 = {1: VR0, 2: GR0, 7: VA0}.get(ct, GT0 + (ct - 3) * 512)
                            S.add("act", lambda e, p=p, w=w, dc0=dc0, func=func, o_=o_: e.activation(
                                out=TMO[o_][:, dc0:dc0 + w], in_=PT[p][:, 0:w], func=func),
                                  reads=[("PT", p)], writes=[okey])
                    rc = G.fmcol(n)
                    S.add("sp", lambda e, o_=o_, rc=rc: e.dma_start(out=TMs[rc:rc + 128, :], in_=TMO[o_][:]),
                          reads=[("TMO", o_, ct) for ct in range(8)], writes=[("TMd", n)], dma=True)

            prep_tile(0)
            for tt in range(G.NT):
                mid = None
                if tt + 1 < G.NT:
                    prep_tile(tt + 1, "pre")
                    mid = lambda t=tt + 1: prep_tile(t, "post")
                main_tile(tt, mid)
            S.emit(nc, st)

    def pass_B(l, xsrc):
        with ExitStack() as st:
            S = Sched()
            ident, CST = common(st, S)
            WBA = sb(st, "WBA", [64, 8, D], BF16)
            WBR = sb(st, "WBR", [128, 4, D], BF16)
            WO = sb(st, "WO", [128, 8, D], BF16)
            MK = sb(st, "MK", [128, 2, 512], BF16)
            MKF = sb(st, "MKF", [128, 2, 512], BF16)
            LFB = sb(st, "LFB", [128, 8], F32)
            ZET = sb(st, "ZET", [128, 8], F32)
            CDC = sb(st, "CDC", [128, 8], F32)
            ESK = sb(st, "ESK", [128, 8], F32)
            DTOT = sb(st, "DTOT", [128, 4, 128], BF16)
            DTMP = sb(st, "DTMP", [128, 128], F32)
            DTMP2 = sb(st, "DTMP2", [128, 128], F32)
            XIF = sb(st, "XIF", [128, 4, 128], F32)
            XIB = sb(st, "XIB", [128, 4, 128], F32)
            WBW = sb(st, "WBW", [128, NCS, 4], F32)
            NG = sb(st, "NG", [128, 512], F32)
            SINF = sb(st, "SINF", [128, 512], F32)
            SINB = sb(st, "SINB", [128, 512], F32)
            RBST = [sb(st, f"RBST{i}", [128, 512], F32) for i in range(2)]
            RFST = [sb(st, f"RFST{i}", [128, 512], F32) for i in range(2)]
            KVL = [sb(st, f"KVL{i}", [128, 1024], BF16) for i in range(2)]
            VZ = [sb(st, f"VZ{i}", [128, 512], BF16) for i in range(2)]
            QA = [sb(st, f"QA{i}", [128, 4, 128], BF16) for i in range(2)]
            KA = [sb(st, f"KA{i}", [128, 384], BF16) for i in range(2)]
            VA1 = [sb(st, f"VA1_{i}", [128, 3, 2, 128], BF16) for i in range(2)]
            QR = [sb(st, f"QR{i}", [128, 4, 128], BF16) for i in range(2)]
            KR = [sb(st, f"KR{i}", [128, 4, 128], BF16) for i in range(2)]
            GG = [sb(st, f"GG{i}", [128, 2560], BF16) for i in range(3)]
            RBL = [sb(st, f"RBL{i}", [128, 512], F32) for i in range(2)]
            XR = [sb(st, f"XR{i}", [128, D], F32) for i in range(3)]
            PTs = [sb(st, f"PTs{i}", [128, 512], BF16) for i in range(4)]
            def two(name, shape, dt):
                return [sb(st, f"{name}{i}", shape, dt) for i in range(2)]
            DENs = two("DEN", [64, 512], F32)
            ATTs = [[sb(st, f"ATT{i}_{k}", [64, 4, 128], BF16) for k in range(2)] for i in range(2)]
            INMs = two("INM", [128, 4, 128], BF16)
            QXFs = two("QXF", [128, 4, 128], BF16)
            QXBs = two("QXB", [128, 4, 128], BF16)
            RFBs = two("RFB", [128, 512], BF16)
            RBBs = two("RBB", [128, 512], BF16)
            RBTs = two("RBT", [128, 512], F32)
            SQs = two("SQ", [128, 512], F32)
            SSRs = two("SSR", [128, 4], F32)
            TRs = two("TR", [128, 512], F32)
            T2Rs = two("T2R", [128, 512], F32)
            RETs = two("RET", [128, 512], BF16)
            RETTs = two("RETT", [128, 4, 128], BF16)
            M1s = two("M1", [128, 512], F32)
            M2s = two("M2", [128, 512], F32)
            MERs = two("MER", [128, D], BF16)
            MTs = two("MT", [128, 8, 128], BF16)
            SP = [ps(st, f"SP{i}", [128, 512]) for i in range(2)]
            OP = ps(st, "OP", [128, 512])
            IP = ps(st, "IP", [128, 4, 128])
            ORp = ps(st, "ORp", [128, 4, 128])
            TPB = ps(st, "TPB", [128, 8, 128], BF16)
            APs = ps(st, "APs", [128, 512])
            RPs = ps(st, "RPs", [128, 512])

            WBAK, WBRK, WOK = {}, {}, {}
            for h in range(8):
                WBAK[h] = load_w(S, WBA[:, h, :], wba[l, h * 64:(h + 1) * 64, :], ("WBA", h), maxc=1024)
            for h in range(4):
                WBRK[h] = load_w(S, WBR[:, h, :], wbr[l, h * 128:(h + 1) * 128, :], ("WBR", h), maxc=1024)
            for k in range(8):
                WOK[k] = load_w(S, WO[:, k, :], wo[l, k * 128:(k + 1) * 128, :], ("WO", k), maxc=1024)
            S.add("pool", lambda e: e.dma_start(out=MK[:].rearrange("p a b -> p (a b)"), in_=masks[:, :]),
                  writes=["MK"], dma=True)
            S.add("sp", lambda e: e.dma_start(out=LFB[:, 0:4], in_=ldf[l:l + 1, :].partition_broadcast(128)),
                  writes=["LFB"], dma=True)
            S.add("sp", lambda e: e.dma_start(out=LFB[:, 4:8], in_=ldb[l:l + 1, :].partition_broadcast(128)),
                  writes=["LFB"], dma=True)
            S.add("sp", lambda e: e.dma_start(out=ESK[:], in_=sinkd[l:l + 1, :].partition_broadcast(128)),
                  writes=["ESK"], dma=True)
            S.add("sp", lambda e: e.dma_start(out=NG[:], in_=g_ret[l:l + 1, :].partition_broadcast(128)),
                  writes=["NG"], dma=True)
            S.add("act", lambda e: e.activation(out=ESK[:], in_=ESK[:], func=AF.Exp), reads=["ESK"], writes=["ESK"])
            for j, cf in ((0, C_FL), (1, C_FR)):
                S.add("dve", lambda e, j=j, cf=cf: e.tensor_scalar(out=MKF[:, j, :], in0=MK[:, j, :],
                                                                  scalar1=CST[:, cf:cf + 1], scalar2=None, op0=ALU.mult),
                      reads=["MK", "CST"], writes=[("MKF", j)])
            KSC = 128.0 ** -0.5
            S.add("act", lambda e: e.activation(out=ZET[:, 0:4], in_=LFB[:, 0:4], func=AF.Exp,
                                                scale=CST[:, C_I127:C_I127 + 1]), reads=["LFB", "CST"], writes=["ZETa"])
            S.add("act", lambda e: e.activation(out=ZET[:, 4:8], in_=LFB[:, 4:8], func=AF.Exp,
                                                scale=CST[:, C_IDXK:C_IDXK + 1]), reads=["LFB", "CST"], writes=["ZETb"])
            S.add("dve", lambda e: e.tensor_scalar(out=ZET[:], in0=ZET[:], scalar1=KSC, scalar2=None, op0=ALU.mult),
                  reads=["ZETa", "ZETb"], writes=["ZET"])
            S.add("act", lambda e: e.activation(out=CDC[:], in_=LFB[:], func=AF.Exp, scale=128.0),
                  reads=["LFB"], writes=["CDC"])
            for h in range(4):
                S.add("dve", lambda e, h=h: e.tensor_scalar(out=DTMP[:], in0=CST[:, C_RELQK:C_RELQK + 128],
                                                           scalar1=LFB[:, h:h + 1], scalar2=None, op0=ALU.mult),
                      reads=["CST", "LFB"], writes=["DTMP"])
                S.add("dve", lambda e, h=h: e.scalar_tensor_tensor(out=DTMP2[:], in0=CST[:, C_RELKQ:C_RELKQ + 128],
                                                                  scalar=LFB[:, 4 + h:5 + h], in1=DTMP[:],
                                                                  op0=ALU.mult, op1=ALU.add),
                      reads=["CST", "LFB", "DTMP"], writes=["DTMP2"])
                S.add("act", lambda e: e.activation(out=DTMP[:], in_=DTMP2[:], func=AF.Exp),
                      reads=["DTMP2"], writes=["DTMP"])
                S.add("dve", lambda e, h=h: e.tensor_scalar(out=DTOT[:, h, :], in0=DTMP[:], scalar1=KSC, scalar2=None,
                                                           op0=ALU.mult), reads=["DTMP"], writes=[("DTOT", h)])
                S.add("act", lambda e, h=h: e.activation(out=XIF[:, h, :], in_=CST[:, C_QP1:C_QP1 + 128], func=AF.Exp,
                                                        scale=LFB[:, h:h + 1]), reads=["CST", "LFB"], writes=[("XIF", h)])
                S.add("act", lambda e, h=h: e.activation(out=XIB[:, h, :], in_=CST[:, C_Q128M:C_Q128M + 128], func=AF.Exp,
                                                        scale=LFB[:, 4 + h:5 + h]), reads=["CST", "LFB"], writes=[("XIB", h)])
            DTK = [("DTOT", h) for h in range(4)]
            XIFK = [("XIF", h) for h in range(4)]
            XIBK = [("XIB", h) for h in range(4)]
            S.add("dve", lambda e: e.tensor_tensor(out=WBW[:], in0=CST[:, C_NW:C_NW + NCS].unsqueeze(2).broadcast_to([128, NCS, 4]),
                                                   in1=LFB[:, 4:8].unsqueeze(1).broadcast_to([128, NCS, 4]), op=ALU.mult),
                  reads=["CST", "LFB"], writes=["WBW"])
            S.add("act", lambda e: e.activation(out=WBW[:], in_=WBW[:], func=AF.Exp), reads=["WBW"], writes=["WBW"])
            for i in range(2):
                S.add("dve", lambda e, i=i: e.memset(VA1[i][:, :, :, 64:128], 1.0), writes=[("VA1o", i)])

            def v4(t):
                return t[:].rearrange("p (h e) -> p h e", h=4)

            def bc4(col_ap):
                return col_ap.unsqueeze(2).broadcast_to([128, 4, 128])

            ldc = [0]

            def scan(chunks, direction, ST, skey, store_rb):
                cur = 0
                first = True
                for n in chunks:
                    if first:
                        S.add("dve", lambda e, cur=cur: e.memset(ST[cur][:], 0.0), writes=[(skey, cur)])
                        first = False
                    if store_rb:
                        S.add("sp", lambda e, n=n, cur=cur: e.dma_start(out=RB[n, :, :], in_=ST[cur][:]),
                              reads=[(skey, cur)], writes=[("RB", n)], dma=True)
                    i = ldc[0] % 2
                    ldc[0] += 1
                    rc = G.fmcol(n)
                    S.add("sp", lambda e, i=i, rc=rc: e.dma_start(out=KVL[i][:], in_=TMs[rc:rc + 128, 0:1024]),
                          writes=[("KVL", i)], dma=True)
                    zc = 0 if direction == "f" else 4
                    S.add("dve", lambda e, i=i, zc=zc: e.tensor_tensor(
                        out=v4(VZ[i]), in0=KVL[i][:, 512:1024].rearrange("p (h e) -> p h e", h=4),
                        in1=bc4(ZET[:, zc:zc + 4]), op=ALU.mult),
                          reads=[("KVL", i), "ZET"], writes=[("VZ", i)])
                    for h in range(4):
                        S.add("pe", lambda e, i=i, h=h: e.matmul(IP[:, h, :], lhsT=KVL[i][:, h * 128:(h + 1) * 128],
                                                                rhs=VZ[i][:, h * 128:(h + 1) * 128], start=True, stop=True),
                              reads=[("KVL", i), ("VZ", i)], writes=["IP"])
                    nxt = 1 - cur
                    S.add("dve", lambda e, cur=cur, nxt=nxt, zc=zc: e.tensor_tensor(
                        out=v4(ST[nxt]), in0=v4(ST[cur]), in1=bc4(CDC[:, zc:zc + 4]), op=ALU.mult),
                          reads=[(skey, cur), "CDC"], writes=[(skey, nxt)])
                    S.add("dve", lambda e, nxt=nxt: e.tensor_tensor(
                        out=ST[nxt][:], in0=ST[nxt][:], in1=IP[:].rearrange("p h e -> p (h e)"), op=ALU.add),
                          reads=["IP", (skey, nxt)], writes=[(skey, nxt)])
                    cur = nxt
                return cur

            cur = scan(list(range(NCH - 1, NCP - 1, -1)), "b", RBST, "RBST", True)
            S.add("sp", lambda e, cur=cur: e.dma_start(out=PKG1[:, 512:1024], in_=RBST[cur][:]),
                  reads=[("RBST", cur)], writes=["PKG1"], dma=True)
            cur = scan(list(range(NCP, NCH)), "f", RFST, "RFST", False)
            S.add("sp", lambda e, cur=cur: e.dma_start(out=PKG1[:, 0:512], in_=RFST[cur][:]),
                  reads=[("RFST", cur)], writes=["PKG1"], dma=True)
            cS0 = G.fmcol(NCP)
            cS1 = G.fmcol(NCH - 1)
            S.add("sp", lambda e: e.dma_start(out=PKGK[:, 0:128], in_=FM[4, :, cS0:cS0 + 128]), writes=["PKGK"], dma=True)
            S.add("sp", lambda e: e.dma_start(out=PKGK[:, 128:256], in_=FM[4, :, cS1:cS1 + 128]), writes=["PKGK"], dma=True)
            S.add("sp", lambda e: e.dma_start(out=PKGK[:, 256:384], in_=TMs[cS0:cS0 + 128, VA0:VA0 + 128]), writes=["PKGK"], dma=True)
            S.add("sp", lambda e: e.dma_start(out=PKGK[:, 384:512], in_=TMs[cS1:cS1 + 128, VA0:VA0 + 128]), writes=["PKGK"], dma=True)
            S.add("pool", lambda e: e.collective_compute("AllGather", ALU.bypass, replica_groups=GROUPS,
                                                         ins=[PKG1_t.ap().opt()], outs=[G1_t.ap().opt()]),
                  reads=["PKG1"], writes=["G1"], cc=True)
            S.add("pool", lambda e: e.collective_compute("AllGather", ALU.bypass, replica_groups=GROUPS,
                                                         ins=[PKGK_t.ap().opt()], outs=[GK_t.ap().opt()]),
                  reads=["PKGK"], writes=["GK"], cc=True)
            cur = scan(list(range(NCP - 1, -1, -1)), "b", RBST, "RBST", True)

            def unpack():
                cL = PL
                cR = PL + 128 + SL
                S.add("sp", lambda e: e.dma_start(out=FM[4, :, cL:cL + 128], in_=GK[0:128, 128:256]),
                      reads=["GK"], writes=[("FMh", "L")], dma=True)
                S.add("sp", lambda e: e.dma_start(out=FM[4, :, cR:cR + 128], in_=GK[128:256, 0:128]),
                      reads=["GK"], writes=[("FMh", "R")], dma=True)
                S.add("sp", lambda e: e.dma_start(out=TMs[cL:cL + 128, VA0:VA0 + 128], in_=GK[0:128, 384:512]),
                      reads=["GK"], writes=[("TMh", "L")], dma=True)
                S.add("sp", lambda e: e.dma_start(out=TMs[cR:cR + 128, VA0:VA0 + 128], in_=GK[128:256, 256:384]),
                      reads=["GK"], writes=[("TMh", "R")], dma=True)
                S.add("sp", lambda e: e.dma_start(out=SINF[:], in_=G1[0:128, 0:512]), reads=["G1"], writes=["SINF"], dma=True)
                S.add("sp", lambda e: e.dma_start(out=SINB[:], in_=G1[128:256, 512:1024]), reads=["G1"], writes=["SINB"], dma=True)
                S.add("dve", lambda e: e.tensor_scalar(out=SINF[:], in0=SINF[:], scalar1=CST[:, C_FL:C_FL + 1], scalar2=None,
                                                       op0=ALU.mult), reads=["SINF", "CST"], writes=["SINF"])
                S.add("dve", lambda e: e.tensor_scalar(out=SINB[:], in0=SINB[:], scalar1=CST[:, C_FR:C_FR + 1], scalar2=None,
                                                       op0=ALU.mult), reads=["SINB", "CST"], writes=["SINB"])

            def loads(n):
                i = n % 2
                sg = G.seg(n)
                c0 = G.fmcol(n)
                lo = NCP if sg else 0
                hi = NCH if sg else NCP
                jl = 0 if (n > lo or sg == 1) else 1
                jh = 2 if (n < hi - 1 or sg == 1) else 1
                extra = []
                if sg == 1 and n == lo:
                    extra = [("FMh", "L"), ("TMh", "L")]
                if sg == 1 and n == hi - 1:
                    extra = extra + [("FMh", "R"), ("TMh", "R")]
                S.add("sp", lambda e: e.dma_start(out=QA[i][:], in_=FM[0:4, :, c0:c0 + 128].rearrange("f p t -> p f t")),
                      writes=[("QA", i)], dma=True)
                ka0 = c0 + (jl - 1) * 128
                nk = (jh - jl + 1) * 128
                S.add("sp", lambda e: e.dma_start(out=KA[i][:, jl * 128:jl * 128 + nk], in_=FM[4, :, ka0:ka0 + nk]),
                      reads=extra, writes=[("KA", i)], dma=True)
                for j in range(jl, jh + 1):
                    rj = c0 + (j - 1) * 128
                    S.add("sp", lambda e, j=j, rj=rj: e.dma_start(
                        out=VA1[i][:, j, :, 0:64], in_=TMs[rj:rj + 128, VA0:VA0 + 128].rearrange("p (k d) -> p k d", k=2)),
                          reads=extra, writes=[("VA1", i, j)], dma=True)
                S.add("sp", lambda e: e.dma_start(out=QR[i][:], in_=FM[5:9, :, c0:c0 + 128].rearrange("f p t -> p f t")),
                      writes=[("QR", i)], dma=True)
                S.add("sp", lambda e: e.dma_start(out=KR[i][:], in_=FM[9:13, :, c0:c0 + 128].rearrange("f p t -> p f t")),
                      writes=[("KR", i)], dma=True)
                S.add("sp", lambda e: e.dma_start(out=KVL[i][:], in_=TMs[c0:c0 + 128, 0:1024]), writes=[("KVL", i)], dma=True)
                i3 = n % 3
                S.add("sp", lambda e: e.dma_start(out=GG[i3][:], in_=TMs[c0:c0 + 128, GR0:GR0 + 2560]), writes=[("GG", i3)], dma=True)
                S.add("sp", lambda e: e.dma_start(out=RBL[i][:], in_=RB[n, :, :]), reads=[("RB", n)], writes=[("RBL", i)], dma=True)
                r0 = G.xrow(n)
                S.add("sp", lambda e: e.dma_start(out=XR[i3][:], in_=xsrc[r0:r0 + 128, :]), writes=[("XR", i3)], dma=True)
                return jl, jh

            rf = [0]
            ptc = [0]
            spc = [0]

            def phase1a(n, jl, jh):
                i = n % 2
                sg = G.seg(n)
                lo = NCP if sg else 0
                hi = NCH if sg else NCP
                DEN, ATT = DENs[i], ATTs[i]
                for kvh in range(2):
                    pts = []
                    for j in range(jl, jh + 1):
                        sp_ = spc[0] % 2
                        spc[0] += 1
                        pt = ptc[0] % 4
                        ptc[0] += 1
                        pts.append((j, pt))
                        S.add("pe", lambda e, j=j, sp_=sp_, kvh=kvh: e.matmul(
                            SP[sp_][:], lhsT=KA[i][kvh * 64:(kvh + 1) * 64, j * 128:(j + 1) * 128],
                            rhs=QA[i][kvh * 64:(kvh + 1) * 64, :, :].rearrange("p g q -> p (g q)"), start=True, stop=True),
                              reads=[("KA", i), ("QA", i)], writes=[("SP", sp_)])
                        S.add("act", lambda e, sp_=sp_, pt=pt: e.activation(out=PTs[pt][:], in_=SP[sp_][:], func=AF.Exp, scale=0.125),
                              reads=[("SP", sp_)], writes=[("PTs", pt)])
                        if j != 1:
                            mj = 0 if j == 0 else 1
                            edge = sg == 1 and ((j == 0 and n == lo) or (j == 2 and n == hi - 1))
                            msk = MKF if edge else MK
                            mkey = ("MKF", mj) if edge else "MK"
                            S.add("pool", lambda e, pt=pt, msk=msk, mj=mj: e.tensor_tensor(
                                out=PTs[pt][:], in0=PTs[pt][:], in1=msk[:, mj, :], op=ALU.mult),
                                  reads=[("PTs", pt), mkey], writes=[("PTs", pt)])
                    for idx, (j, pt) in enumerate(pts):
                        S.add("pe", lambda e, j=j, pt=pt, kvh=kvh, idx=idx, np_=len(pts): e.matmul(
                            OP[:], lhsT=VA1[i][:, j, kvh, :], rhs=PTs[pt][:], start=(idx == 0), stop=(idx == np_ - 1)),
                              reads=[("VA1", i, j), ("VA1o", i), ("PTs", pt)], writes=["OP"])
                    S.add("dve", lambda e, kvh=kvh: e.tensor_tensor(
                        out=DEN[:].rearrange("p (g q) -> p g q", g=4), in0=OP[64:128, :].rearrange("p (g q) -> p g q", g=4),
                        in1=ESK[0:64, kvh * 4:(kvh + 1) * 4].unsqueeze(2).broadcast_to([64, 4, 128]), op=ALU.add),
                          reads=["OP", "ESK"], writes=[("DEN", i)])
                    S.add("dve", lambda e: e.reciprocal(out=DEN[:], in_=DEN[:]), reads=[("DEN", i)], writes=[("DEN", i)])
                    S.add("dve", lambda e, kvh=kvh: e.tensor_tensor(
                        out=ATT[kvh][:].rearrange("p g q -> p (g q)"), in0=OP[0:64, :], in1=DEN[:], op=ALU.mult),
                          reads=["OP", ("DEN", i)], writes=[("ATT", i, kvh)])
            def phase1r(n):
                i = n % 2
                sg = G.seg(n)
                lo = NCP if sg else 0
                hi = NCH if sg else NCP
                INM, QXF, QXB, RFB, RBB, RBT = INMs[i], QXFs[i], QXBs[i], RFBs[i], RBBs[i], RBTs[i]
                SQ, SSR, TR, T2R, RET = SQs[i], SSRs[i], TRs[i], T2Rs[i], RETs[i]
                if n == lo:
                    if sg == 0:
                        S.add("dve", lambda e, c=rf[0]: e.memset(RFST[c][:], 0.0), writes=[("RFST", rf[0])])
                    else:
                        S.add("dve", lambda e, c=rf[0]: e.tensor_copy(out=RFST[c][:], in_=SINF[:]),
                              reads=["SINF"], writes=[("RFST", rf[0])])
                cur = rf[0]
                for h in range(4):
                    S.add("pe", lambda e, h=h: e.matmul(IP[:, h, :], lhsT=KR[i][:, h, :], rhs=QR[i][:, h, :], start=True, stop=True),
                          reads=[("KR", i), ("QR", i)], writes=["IP"])
                S.add("dve", lambda e: e.tensor_tensor(out=INM[:], in0=IP[:], in1=DTOT[:], op=ALU.mult),
                      reads=["IP"] + DTK, writes=[("INM", i)])
                S.add("pool", lambda e: e.tensor_tensor(out=QXF[:], in0=QR[i][:], in1=XIF[:], op=ALU.mult),
                      reads=[("QR", i)] + XIFK, writes=[("QXF", i)])
                S.add("pool", lambda e: e.tensor_tensor(out=QXB[:], in0=QR[i][:], in1=XIB[:], op=ALU.mult),
                      reads=[("QR", i)] + XIBK, writes=[("QXB", i)])
                S.add("act", lambda e, cur=cur: e.activation(out=RFB[:], in_=RFST[cur][:], func=AF.Copy),
                      reads=[("RFST", cur)], writes=[("RFB", i)])
                if sg == 0:
                    S.add("act", lambda e: e.activation(out=RBB[:], in_=RBL[i][:], func=AF.Copy),
                          reads=[("RBL", i)], writes=[("RBB", i)])
                else:
                    jloc = n - lo
                    S.add("pool", lambda e, jloc=jloc: e.tensor_tensor(out=v4(RBT), in0=v4(SINB), in1=bc4(WBW[:, jloc, :]), op=ALU.mult),
                          reads=["SINB", "WBW"], writes=[("RBT", i)])
                    S.add("pool", lambda e: e.tensor_tensor(out=RBB[:], in0=RBT[:], in1=RBL[i][:], op=ALU.add),
                          reads=[("RBT", i), ("RBL", i)], writes=[("RBB", i)])
                for h in range(4):
                    hs_ = slice(h * 128, (h + 1) * 128)
                    S.add("pe", lambda e, h=h, hs_=hs_: e.matmul(ORp[:, h, :], lhsT=INM[:, h, :], rhs=KVL[i][:, 512 + h * 128:512 + (h + 1) * 128],
                                                               start=True, stop=False),
                          reads=[("INM", i), ("KVL", i)], writes=["ORp"])
                    S.add("pe", lambda e, h=h, hs_=hs_: e.matmul(ORp[:, h, :], lhsT=QXF[:, h, :], rhs=RFB[:, hs_], start=False, stop=False),
                          reads=[("QXF", i), ("RFB", i)], writes=["ORp"])
                    S.add("pe", lambda e, h=h, hs_=hs_: e.matmul(ORp[:, h, :], lhsT=QXB[:, h, :], rhs=RBB[:, hs_], start=False, stop=True),
                          reads=[("QXB", i), ("RBB", i)], writes=["ORp"])
                vz = i
                S.add("pool", lambda e: e.tensor_tensor(out=v4(VZ[vz]), in0=KVL[i][:, 512:1024].rearrange("p (h e) -> p h e", h=4),
                                                        in1=bc4(ZET[:, 0:4]), op=ALU.mult),
                      reads=[("KVL", i), "ZET"], writes=[("VZ", vz)])
                S.add("act", lambda e: e.activation(out=SQ[:], in_=ORp[:].rearrange("p h e -> p (h e)"), func=AF.Square),
                      reads=["ORp"], writes=[("SQ", i)])
                S.add("dve", lambda e: e.tensor_tensor(out=v4(TR), in0=ORp[:], in1=bc4(CST[:, C_ONE:C_ONE + 4]), op=ALU.mult),
                      reads=["ORp", "CST"], writes=[("TR", i)])
                for h in range(4):
                    S.add("pe", lambda e, h=h: e.matmul(IP[:, h, :], lhsT=KVL[i][:, h * 128:(h + 1) * 128],
                                                        rhs=VZ[vz][:, h * 128:(h + 1) * 128], start=True, stop=True),
                          reads=[("KVL", i), ("VZ", vz)], writes=["IP"])
                nxt = 1 - cur
                S.add("pool", lambda e, cur=cur, nxt=nxt: e.tensor_tensor(out=v4(RFST[nxt]), in0=v4(RFST[cur]), in1=bc4(CDC[:, 0:4]), op=ALU.mult),
                      reads=[("RFST", cur), "CDC"], writes=[("RFST", nxt)])
                S.add("dve", lambda e, nxt=nxt: e.tensor_tensor(out=RFST[nxt][:], in0=RFST[nxt][:], in1=IP[:].rearrange("p h e -> p (h e)"), op=ALU.add),
                      reads=["IP", ("RFST", nxt)], writes=[("RFST", nxt)])
                rf[0] = nxt
                S.add("dve", lambda e: e.tensor_reduce(out=SSR[:], in_=v4(SQ), axis=mybir.AxisListType.X, op=ALU.add),
                      reads=[("SQ", i)], writes=[("SSR", i)])
                S.add("dve", lambda e: e.tensor_scalar(out=SSR[:], in0=SSR[:], scalar1=1.0 / 128, scalar2=EPS, op0=ALU.mult, op1=ALU.add),
                      reads=[("SSR", i)], writes=[("SSR", i)])
                S.add("act", lambda e: e.activation(out=SSR[:], in_=SSR[:], func=AF.Sqrt), reads=[("SSR", i)], writes=[("SSR", i)])
                S.add("dve", lambda e: e.reciprocal(out=SSR[:], in_=SSR[:]), reads=[("SSR", i)], writes=[("SSR", i)])
                S.add("pool", lambda e: e.tensor_tensor(out=T2R[:], in0=NG[:], in1=GG[n % 3][:, 0:512], op=ALU.mult),
                      reads=["NG", ("GG", n % 3)], writes=[("T2R", i)])
                S.add("pool", lambda e: e.tensor_tensor(out=v4(TR), in0=v4(TR), in1=bc4(SSR[:, 0:4]), op=ALU.mult),
                      reads=[("TR", i), ("SSR", i)], writes=[("TR", i)])
                S.add("pool", lambda e: e.tensor_tensor(out=RET[:], in0=TR[:], in1=T2R[:], op=ALU.mult),
                      reads=[("TR", i), ("T2R", i)], writes=[("RET", i)])

            def phase2(n):
                i = n % 2
                i3 = n % 3
                ATT, RET, RETT, M1, M2, MER, MT = ATTs[i], RETs[i], RETTs[i], M1s[i], M2s[i], MERs[i], MTs[i]
                for h in range(4):
                    S.add("pe", lambda e, h=h: e.transpose(out=TPB[:, h, :], in_=RET[:, h * 128:(h + 1) * 128], identity=ident[:]),
                          reads=[("RET", i), "ident"], writes=["TPB"])
                S.add("act", lambda e: e.activation(out=RETT[:], in_=TPB[:, 0:4, :], func=AF.Copy), reads=["TPB"], writes=[("RETT", i)])
                for ct in range(2):
                    cs = slice(ct * 512, (ct + 1) * 512)
                    for hh in range(8):
                        kvh, g = hh // 4, hh % 4
                        S.add("pe", lambda e, hh=hh, kvh=kvh, g=g, cs=cs: e.matmul(APs[:], lhsT=ATT[kvh][:, g, :], rhs=WBA[:, hh, cs],
                                                                                 start=(hh == 0), stop=(hh == 7)),
                              reads=[("ATT", i, kvh)] + WBAK[hh], writes=["APs"])
                    for h in range(4):
                        S.add("pe", lambda e, h=h, cs=cs: e.matmul(RPs[:], lhsT=RETT[:, h, :], rhs=WBR[:, h, cs], start=(h == 0), stop=(h == 3)),
                              reads=[("RETT", i)] + WBRK[h], writes=["RPs"])
                    S.add("dve", lambda e, ct=ct: e.tensor_tensor(out=M1[:], in0=APs[:], in1=GG[i3][:, 512 + ct * 512:1024 + ct * 512], op=ALU.mult),
                          reads=["APs", ("GG", i3)], writes=[("M1", i)])
                    S.add("dve", lambda e, ct=ct: e.tensor_tensor(out=M2[:], in0=RPs[:], in1=GG[i3][:, 1536 + ct * 512:2048 + ct * 512], op=ALU.mult),
                          reads=["RPs", ("GG", i3)], writes=[("M2", i)])
                    S.add("pool", lambda e, cs=cs: e.tensor_tensor(out=MER[:, cs], in0=M1[:], in1=M2[:], op=ALU.add),
                          reads=[("M1", i), ("M2", i)], writes=[("MER", i, ct)])
                for k in range(8):
                    S.add("pe", lambda e, k=k: e.transpose(out=TPB[:, k, :], in_=MER[:, k * 128:(k + 1) * 128], identity=ident[:]),
                          reads=[("MER", i, k // 4), "ident"], writes=["TPB"])
                S.add("act", lambda e: e.activation(out=MT[:], in_=TPB[:], func=AF.Copy), reads=["TPB"], writes=[("MT", i)])
                for ct in range(2):
                    cs = slice(ct * 512, (ct + 1) * 512)
                    yp, ypk = (APs, "APs") if ct == 0 else (RPs, "RPs")
                    for k in range(8):
                        S.add("pe", lambda e, k=k, cs=cs, yp=yp: e.matmul(yp[:], lhsT=MT[:, k, :], rhs=WO[:, k, cs], start=(k == 0), stop=(k == 7)),
                              reads=[("MT", i)] + WOK[k], writes=[ypk])
                    S.add("dve", lambda e, cs=cs, yp=yp: e.tensor_tensor(out=XR[i3][:, cs], in0=yp[:], in1=XR[i3][:, cs], op=ALU.add),
                          reads=[ypk, ("XR", i3)], writes=[("XR", i3)])
                r1 = G.x1row(n)
                S.add("sp", lambda e: e.dma_start(out=X1[r1:r1 + 128, :], in_=XR[i3][:]), reads=[("XR", i3)], writes=[("X1", n)], dma=True)

            jj = {}
            jj[0] = loads(0)
            for n in range(NCH):
                if n + 1 < NCH:
                    if n + 1 == NCP:
                        unpack()
                    jj[n + 1] = loads(n + 1)
                streams = []
                for f_ in ((lambda: phase1a(n, *jj[n])), (lambda: phase1r(n)), ((lambda: phase2(n - 1)) if n >= 1 else None)):
                    if f_ is None:
                        continue
                    S.cap = []
                    f_()
                    streams.append(S.cap)
                    S.cap = None
                S.interleave(streams)
            phase2(NCH - 1)
            S.emit(nc, st)

    def pass_D(l, last):
        with ExitStack() as st:
            S = Sched()
            ident, CST = common(st, S)
            WF1 = sb(st, "WF1", [128, 8, 2 * DFF], BF16)
            WF2 = sb(st, "WF2", [128, NFC, D], BF16)
            GB = sb(st, "GBF", [128, D], F32)
            GBL = sb(st, "GBL", [128, D], F32)
            CW = sb(st, "CW", [128, NFC * 3], F32)
            CB = sb(st, "CB", [128, NFC], F32)
            XT = [sb(st, f"XT{i}", [128, D], F32) for i in range(4)]
            XN = [sb(st, f"XN{i}", [128, D], BF16) for i in range(2)]
            SQJ = sb(st, "SQJ", [128, D], BF16)
            SS = sb(st, "SS", [128, 16], F32)
            H2T = [sb(st, f"H2T{i}", [128, 8, 258], BF16) for i in range(2)]
            HALOX = sb(st, "HALOX", [128, D], F32)
            HALOT = sb(st, "HALOT", [128, 8, 128], BF16)
            GU = [sb(st, f"GU{i}", [128, 256], BF16) for i in range(3)]
            C1 = [sb(st, f"C1_{i}", [128, 256], F32) for i in range(2)]
            C2 = [sb(st, f"C2_{i}", [128, 256], F32) for i in range(2)]
            C3 = [sb(st, f"C3_{i}", [128, 256], F32) for i in range(2)]
            GE = [sb(st, f"GE{i}", [128, 256], F32) for i in range(2)]
            AS = [sb(st, f"AS{i}", [128, 258], F32) for i in range(2)]
            US = [sb(st, f"US{i}", [128, 256], F32) for i in range(3)]
            TP = ps(st, "TP", [128, 8, 128], BF16)
            PA = [ps(st, f"PA{i}", [128, 512]) for i in range(2)]
            PU = ps(st, "PU", [128, 512])
            PY = [ps(st, f"PY{i}", [128, 512]) for i in range(4)]

            S.add("sp", lambda e: e.dma_start(out=GB[:], in_=g_ffn[l:l + 1, :].partition_broadcast(128)), writes=["GB"], dma=True)
            S.add("sp", lambda e: e.dma_start(out=GBL[:], in_=g_fin[0:1, :].partition_broadcast(128)), writes=["GBL"], dma=True)
            S.add("sp", lambda e: e.dma_start(out=CW[:], in_=cwd[l, :, :]), writes=["CW"], dma=True)
            S.add("sp", lambda e: e.dma_start(out=CB[:], in_=cbd[l, :, :]), writes=["CB"], dma=True)
            rS0 = G.x1row(NCP)
            rS1 = G.x1row(NCH - 1) + 127
            S.add("sp", lambda e: e.dma_start(out=PKG2[0:1, :], in_=X1[rS0:rS0 + 1, :]), writes=["PKG2"], dma=True)
            S.add("sp", lambda e: e.dma_start(out=PKG2[1:2, :], in_=X1[rS1:rS1 + 1, :]), writes=["PKG2"], dma=True)
            S.add("pool", lambda e: e.collective_compute("AllGather", ALU.bypass, replica_groups=GROUPS,
                                                         ins=[PKG2_t.ap().opt()], outs=[G2_t.ap().opt()]),
                  reads=["PKG2"], writes=["G2"], cc=True)
            WF1K, WF2K = {}, {}
            for k in range(8):
                WF1K[k] = load_w(S, WF1[:, k, :], wf1[l, k * 128:(k + 1) * 128, :], ("WF1", k), maxc=1408)
            for fc in range(NFC):
                WF2K[fc] = load_w(S, WF2[:, fc, :], wf2[l, fc * 128:(fc + 1) * 128, :], ("WF2", fc), maxc=1024)
            S.add("dve", lambda e: e.memset(HALOX[:], 0.0), writes=[("HALOX", q_) for q_ in range(1, 64)])
            nbP = PL // 256
            nbS = SL // 256
            S.add("sp", lambda e: e.dma_start(out=HALOX[1:2, :], in_=X1[0:1, :]), writes=[("HALOX", 1)], dma=True)
            for b in range(1, nbP):
                S.add("sp", lambda e, b=b: e.dma_start(out=HALOX[2 * b:2 * b + 2, :], in_=X1[256 * b - 1:256 * b + 1, :]),
                      writes=[("HALOX", 10 + b)], dma=True)
            S.add("sp", lambda e: e.dma_start(out=HALOX[2 * nbP:2 * nbP + 1, :], in_=X1[PL - 1:PL, :]), writes=[("HALOX", 3)], dma=True)
            hb = 2 * (nbP + 1)
            for b in range(1, nbS):
                r = PL + 1 + 256 * b - 1
                S.add("sp", lambda e, b=b, r=r: e.dma_start(out=HALOX[hb + 2 * b:hb + 2 * b + 2, :], in_=X1[r:r + 2, :]),
                      writes=[("HALOX", 30 + b)], dma=True)
            S.add("sp", lambda e: e.dma_start(out=HALOX[hb + 1:hb + 2, :], in_=X1[PL + 1:PL + 2, :]), writes=[("HALOX", 5)], dma=True)
            S.add("sp", lambda e: e.dma_start(out=HALOX[hb + 2 * nbS:hb + 2 * nbS + 1, :], in_=X1[PL + SL:PL + SL + 1, :]),
                  writes=[("HALOX", 6)], dma=True)
            S.add("sp", lambda e: e.dma_start(out=HALOX[hb:hb + 1, :], in_=G2[1:2, :]), reads=["G2"], writes=[("HALOX", 7)], dma=True)
            S.add("sp", lambda e: e.dma_start(out=HALOX[hb + 2 * nbS + 1:hb + 2 * nbS + 2, :], in_=G2[2:3, :]),
                  reads=["G2"], writes=[("HALOX", 8)], dma=True)
            prep_chunk(S, "H", None, 128, HALOX, [("HALOX", q_) for q_ in range(1, 64)], XN[0], ("XN", 0), SQJ, SS, 15, GB, "GB", TP, "TP", ident,
                       HALOT[:], ["HALOT"], rowscale=CST[:, C_HFL:C_HFL + 1])

            S.mark("D halo done")
            tiles = []
            for b in range(nbP):
                tiles.append((b * 2, 2 * b, 2 * (b + 1) + 1))
            for b in range(nbS):
                tiles.append((NCP + b * 2, hb + 2 * b, hb + 2 * (b + 1) + 1))
            cctr = [0]

            def prep_tile(ti, phase="all"):
                n0, hl, hr = tiles[ti]
                hs = ti % 2
                for c in range(2):
                    n = n0 + c
                    xs = (ti % 2) * 2 + c
                    i = c
                    r1 = G.x1row(n)
                    prep_chunk(S, "D", X1[r1:r1 + 128, :], 128, XT[xs], ("XT", xs), XN[i], ("XN", i), SQJ, SS, xs,
                               GB, "GB", TP, "TP", ident, H2T[hs][:, :, 1 + c * 128:1 + (c + 1) * 128], [("H2T", hs, c)],
                               phase=phase)
                if phase == "pre":
                    return
                S.add("dve", lambda e: e.tensor_copy(out=H2T[hs][:, :, 0:1], in_=HALOT[:, :, hl:hl + 1]),
                      reads=["HALOT"], writes=[("H2T", hs, "l")])
                S.add("dve", lambda e: e.tensor_copy(out=H2T[hs][:, :, 257:258], in_=HALOT[:, :, hr:hr + 1]),
                      reads=["HALOT"], writes=[("H2T", hs, "r")])

            fcc = [0]

            def ffn_out(ti, fc, g):
                for sbk in range(2):
                    for ct in range(2):
                        S.add("pe", lambda e, sbk=sbk, ct=ct, fc=fc, g=g: e.matmul(
                            PY[sbk * 2 + ct][:], lhsT=GU[g][:, sbk * 128:(sbk + 1) * 128], rhs=WF2[:, fc, ct * 512:(ct + 1) * 512],
                            start=(fc == 0), stop=(fc == NFC - 1)),
                              reads=[("GU", g)] + WF2K[fc], writes=[("PY", sbk * 2 + ct)])

            def main_tile(ti, mid, early=None):
                n0, hl, hr = tiles[ti]
                hs = ti % 2
                hk = [("H2T", hs, 0), ("H2T", hs, 1), ("H2T", hs, "l"), ("H2T", hs, "r")]
                def st1(fc):
                    p = fc % 2
                    for k in range(8):
                        S.add("pe", lambda e, k=k: e.matmul(PA[p][:, 0:258], lhsT=WF1[:, k, fc * 128:(fc + 1) * 128],
                                                            rhs=H2T[hs][:, k, :], start=(k == 0), stop=(k == 7)),
                              reads=WF1K[k] + hk, writes=[("PA", p)])
                    for k in range(8):
                        S.add("pe", lambda e, k=k: e.matmul(PU[:, 0:256], lhsT=WF1[:, k, DFF + fc * 128:DFF + (fc + 1) * 128],
                                                            rhs=H2T[hs][:, k, 1:257], start=(k == 0), stop=(k == 7)),
                              reads=WF1K[k] + hk, writes=["PU"])
                    S.add("act", lambda e: e.activation(out=AS[p][:], in_=PA[p][:, 0:258], func=AF.Copy),
                          reads=[("PA", p)], writes=[("AS", p)])
                    S.add("act", lambda e: e.activation(out=US[fc % 3][:], in_=PU[:, 0:256], func=AF.Copy),
                          reads=["PU"], writes=[("US", fc % 3)])

                def st2(fc):
                    p = fc % 2
                    S.add("act", lambda e: e.activation(out=C1[p][:], in_=AS[p][:, 1:257], func=AF.Identity,
                                                        scale=CW[:, fc * 3 + 1:fc * 3 + 2], bias=CB[:, fc:fc + 1]),
                          reads=[("AS", p), "CW", "CB"], writes=[("C1", p)])
                    S.add("dve", lambda e: e.scalar_tensor_tensor(out=C2[p][:], in0=AS[p][:, 0:256], scalar=CW[:, fc * 3:fc * 3 + 1],
                                                                  in1=C1[p][:], op0=ALU.mult, op1=ALU.add),
                          reads=[("AS", p), ("C1", p), "CW"], writes=[("C2", p)])
                    S.add("dve", lambda e: e.scalar_tensor_tensor(out=C3[p][:], in0=AS[p][:, 2:258], scalar=CW[:, fc * 3 + 2:fc * 3 + 3],
                                                                  in1=C2[p][:], op0=ALU.mult, op1=ALU.add),
                          reads=[("AS", p), ("C2", p), "CW"], writes=[("C3", p)])

                def st3(fc):
                    p = fc % 2
                    g = fc % 3
                    S.add("act", lambda e: e.activation(out=GE[p][:], in_=C3[p][:], func=AF.Gelu),
                          reads=[("C3", p)], writes=[("GE", p)])
                    S.add("pool", lambda e: e.tensor_tensor(out=GU[g][:], in0=US[g][:], in1=GE[p][:], op=ALU.mult),
                          reads=[("US", g), ("GE", p)], writes=[("GU", g)])

                for t in range(NFC + 3):
                    if t == 1 and early is not None:
                        early()
                    if t == NFC // 2 and mid is not None:
                        mid()
                    if t < NFC:
                        S.mark(f"D tile {ti} fc {t}")
                        st1(t)
                    if 0 <= t - 3 < NFC:
                        ffn_out(ti, t - 3, (t - 3) % 3)
                    if 0 <= t - 1 < NFC:
                        st2(t - 1)
                    if 0 <= t - 2 < NFC:
                        st3(t - 2)
                S.mark(f"D tile {ti} ffn done")
                for sbk in range(2):
                    xs = (ti % 2) * 2 + sbk
                    n = n0 + sbk
                    for ct in range(2):
                        cs = slice(ct * 512, (ct + 1) * 512)
                        S.add("dve", lambda e, xs=xs, cs=cs, sbk=sbk, ct=ct: e.tensor_tensor(
                            out=XT[xs][:, cs], in0=PY[sbk * 2 + ct][:], in1=XT[xs][:, cs], op=ALU.add),
                              reads=[("PY", sbk * 2 + ct), ("XT", xs)], writes=[("XT", xs)])
                    r0 = G.xrow(n)
                    if not last:
                        S.add("sp", lambda e, xs=xs, r0=r0: e.dma_start(out=X2[r0:r0 + 128, :], in_=XT[xs][:]),
                              reads=[("XT", xs)], writes=[("X2", n)], dma=True)
                    else:
                        sc = SS[:, 8 + xs:9 + xs]
                        sk = ("SSF", xs)
                        S.add("act", lambda e, xs=xs, sc=sc: e.activation(out=SQJ[:], in_=XT[xs][:], func=AF.Square, accum_out=sc),
                              reads=[("XT", xs)], writes=[("SQJ", "D"), sk])
                        S.add("dve", lambda e, sc=sc: e.tensor_scalar(out=sc, in0=sc, scalar1=1.0 / D, scalar2=EPS, op0=ALU.mult, op1=ALU.add),
                              reads=[sk], writes=[sk])
                        S.add("act", lambda e, sc=sc: e.activation(out=sc, in_=sc, func=AF.Sqrt), reads=[sk], writes=[sk])
                        S.add("dve", lambda e, sc=sc: e.reciprocal(out=sc, in_=sc), reads=[sk], writes=[sk])
                        S.add("dve", lambda e, xs=xs, sc=sc: e.scalar_tensor_tensor(out=XT[xs][:], in0=XT[xs][:], scalar=sc, in1=GBL[:],
                                                                                   op0=ALU.mult, op1=ALU.mult),
                              reads=[("XT", xs), sk, "GBL"], writes=[("XT", xs)])
                        S.add("sp", lambda e, xs=xs, r0=r0: e.dma_start(out=y_out[r0:r0 + 128, :], in_=XT[xs][:]),
                              reads=[("XT", xs)], writes=[("Y", n)], dma=True)

            prep_tile(0)
            for ti in range(len(tiles)):
                mid = (lambda t=ti + 1: prep_tile(t, "post")) if ti + 1 < len(tiles) else None
                early = (lambda t=ti + 1: prep_tile(t, "pre")) if ti + 1 < len(tiles) else None
                main_tile(ti, mid, early)
            S.emit(nc, st)

    Sched.GLOBAL.clear()
    Sched.NINST[0] = 0
    gstack = ExitStack()
    Sched.GLOBAL["stack"] = gstack
    plist = []
    for l in range(DEPTH):
        xsrc = x_in if l == 0 else X2
        plist.append(lambda l=l, xsrc=xsrc: pass_A(l, xsrc))
        plist.append(lambda l=l, xsrc=xsrc: pass_B(l, xsrc))
        plist.append(lambda l=l: pass_D(l, l == DEPTH - 1))
    with gstack:
        for f in plist[:npasses]:
            f()
    return nc, G


def _consts(G, core):
    PL, SL, T, NCS = G.PL, G.SL, G.T, G.NCS
    h = core % 2
    pos = np.concatenate([np.arange(PL), h * SL + np.arange(SL)]).astype(np.float32)

    def tabs(half):
        fr = (np.float32(10000.0) ** (-np.arange(half, dtype=np.float32) / np.float32(half))).astype(np.float32)
        ang = (pos[:, None] * fr[None, :]).astype(np.float32)
        return np.cos(ang).astype(np.float32), np.sin(ang).astype(np.float32)

    ca, sa = tabs(32)
    cr, sr = tabs(64)
    fmtab = np.zeros((128, 4, T), np.float32)
    p = np.arange(128)
    da = p % 64
    fmtab[:, 0, :] = ca[:, da % 32].T
    fmtab[:, 1, :] = sa[:, da % 32].T
    fmtab[:, 2, :] = cr[:, p % 64].T
    fmtab[:, 3, :] = sr[:, p % 64].T
    rot = np.zeros((128, 256), np.float32)
    for m in range(128):
        d = m % 64
        if d < 32:
            rot[m + 32, m] = -1.0
        else:
            rot[m - 32, m] = 1.0
        if m < 64:
            rot[m + 64, 128 + m] = -1.0
        else:
            rot[m - 64, 128 + m] = 1.0
    tmtab = np.concatenate([cr, -sr, sr], axis=1).astype(np.float32)
    NCST = 528 + NCS
    cst = np.zeros((128, NCST), np.float32)
    k = np.arange(128)[:, None].astype(np.float32)
    q = np.arange(128)[None, :].astype(np.float32)
    cst[:, 0:128] = np.maximum(q - k, 0)
    cst[:, 128:256] = np.maximum(k - q, 0)
    cst[:, 256:384] = q + 1
    cst[:, 384:512] = 128 - q
    cst[:, 512] = k[:, 0]
    cst[:, 513] = 127 - k[:, 0]
    cst[:, 514] = float(h)
    cst[:, 515] = float(1 - h)
    hfl = np.ones(128, np.float32)
    hb = 2 * (PL // 256 + 1)
    hfl[hb] = float(h)
    hfl[hb + 2 * (SL // 256) + 1] = float(1 - h)
    cst[:, 516] = hfl
    cst[:, 520:524] = 1.0
    cst[:, 528:528 + NCS] = (128.0 * (NCS - 1 - np.arange(NCS)))[None, :]
    mk = np.zeros((128, 2, 512), np.float32)
    kk = np.arange(128)[:, None]
    qq = np.arange(128)[None, :]
    mk[:, 0, :] = np.tile((kk >= qq).astype(np.float32), (1, 4))
    mk[:, 1, :] = np.tile((kk <= qq).astype(np.float32), (1, 4))
    return dict(fmtab=fmtab, tmtab=tmtab, cst=cst, masks=mk.reshape(128, 1024), ident=np.eye(128, dtype=np.float32), rot=rot)


def _perm_w_in(w_in):
    cols = []
    for g in range(4):
        cols += list(range(g * 64, (g + 1) * 64)) + list(range((4 + g) * 64, (5 + g) * 64))
    cols += list(range(512, 640))
    cols += list(range(768, 1280))
    cols += list(range(1280, 1792))
    cols += list(range(1280, 1792)) + list(range(1792, 2304)) + list(range(2304, 2816)) + list(range(2816, 4864)) + list(range(640, 768))
    return np.ascontiguousarray(w_in[:, :, np.array(cols)])


def _shared_inputs(inp):
    f = lambda a: np.ascontiguousarray(np.asarray(a, dtype=np.float32))
    cw = f(inp["conv_w"])
    cwl = np.ascontiguousarray(cw.reshape(DEPTH, 3, NFC, 128).transpose(0, 3, 2, 1).reshape(DEPTH, 128, NFC * 3))
    cb = f(inp["conv_b"])
    cbl = np.ascontiguousarray(cb.reshape(DEPTH, NFC, 128).transpose(0, 2, 1))
    return dict(
        w_in=_perm_w_in(f(inp["w_in"])), wba=f(inp["w_branch_attn"]), wbr=f(inp["w_branch_ret"]), wo=f(inp["w_out"]),
        wf1=f(inp["w_ffn_in"]), wf2=f(inp["w_ffn_out"]), g_mix=f(inp["norm_mix_g"]), g_ffn=f(inp["norm_ffn_g"]),
        g_fin=f(inp["final_norm_g"]).reshape(1, D), g_ret=f(inp["ret_norm_g"]), sink=f(inp["attn_sink"]),
        ldf=f(inp["ret_log_decay_f"]), ldb=f(inp["ret_log_decay_b"]), cw=cwl, cb=cbl)


_CACHE = {}


def run(inp, PL, SL, debug=False, npasses=6, trace=False):
    key = (PL, SL, debug, npasses)
    if key not in _CACHE:
        _CACHE[key] = build(PL, SL, debug, npasses)
    nc, G = _CACHE[key]
    shared = _shared_inputs(inp)
    xp = np.asarray(inp["x_prompt"], dtype=np.float32)
    xs = np.asarray(inp["x_sample"], dtype=np.float32)
    in_maps = []
    for c in range(8):
        m = dict(shared)
        m.update(_consts(G, c))
        h = c % 2
        m["x"] = np.ascontiguousarray(np.concatenate([xp[c], xs[c // 2, h * SL:(h + 1) * SL]], axis=0))
        in_maps.append(m)
    if trace:
        res = run_bass_kernel_spmd(nc, in_maps, core_ids=list(range(8)), trace=True)
    else:
        res = run_bass_kernel_spmd(nc, in_maps, core_ids=list(range(8)))
    yp = np.stack([res.results[c]["y"][:PL] for c in range(8)], axis=0)
    ys = np.stack([np.concatenate([res.results[2 * s]["y"][PL:], res.results[2 * s + 1]["y"][PL:]], axis=0) for s in range(4)], axis=0)
    return (yp.astype(np.float32), ys.astype(np.float32)), res


def kernel(**inputs):
    out, _ = run(inputs, 2048, 4096)
    return out
```

```python
from contextlib import ExitStack
import numpy as np
import concourse.bass as bass
import concourse.mybir as mybir
from concourse.bass_utils import run_bass_kernel_spmd

F32 = mybir.dt.float32
BF16 = mybir.dt.bfloat16
AF = mybir.ActivationFunctionType
ALU = mybir.AluOpType

D = 1024
DEPTH = 2
DFF = 2816
NFC = DFF // 128
INW = 5376
NFM = 13
TMW = 3712
KR0, VR0, GR0, GT0, VA0 = 0, 512, 1024, 1536, 3584
EPS = 1e-6
GROUPS = [[0, 1], [2, 3], [4, 5], [6, 7]]
ENGS = ("pe", "act", "dve", "pool", "sp")


class Sched:
    NDS = 14
    EPOCH = 12000
    UID = [0]

    NINST = [0]
    GLOBAL = {}

    def __init__(self):
        self.ops = []
        self.lastw = {}
        self.readers = {}
        self.inst = Sched.NINST[0]
        Sched.NINST[0] += 1

    PSUM_NAMES = {"PF", "PT", "TP", "PR", "SP", "OP", "IP", "ORp", "TPB", "APs", "RPs", "PA", "PU", "PY"}

    cap = None

    def interleave(self, streams):
        pos = [0] * len(streams)
        total = sum(len(x) for x in streams)
        for _ in range(total):
            best, bf = None, None
            for si, st_ in enumerate(streams):
                if pos[si] < len(st_):
                    f = pos[si] / len(st_)
                    if bf is None or f < bf:
                        best, bf = si, f
            a = streams[best][pos[best]]
            pos[best] += 1
            self.add(*a[0], **a[1])

    def add(self, eng, fn, reads=(), writes=(), dma=False, cc=False):
        if self.cap is not None:
            self.cap.append(((eng, fn), dict(reads=list(reads), writes=list(writes), dma=dma, cc=cc)))
            return None
        def _ps(k):
            return (k if isinstance(k, str) else k[0]) in self.PSUM_NAMES
        ps_reads = [k for k in reads if _ps(k) and k not in writes]
        reads = [k for k in reads if not _ps(k)]
        writes = list(writes)
        import os
        cut = int(os.environ.get("KCUT", "0"))
        if cut and len(self.ops) >= cut and self.inst == int(os.environ.get("KCUTPASS", "0")):
            return None
        idx = len(self.ops)
        deps = {}

        def dep(i, raw):
            if i is None or i == idx:
                return
            deps[i] = deps.get(i, False) or raw

        for k in list(reads) + ps_reads:
            dep(self.lastw.get(k), True)
        for k in ps_reads:
            for r in self.readers.get(k, ()):
                dep(r, False)
        for k in writes:
            dep(self.lastw.get(k), False)
            for r in self.readers.get(k, ()):
                dep(r, False)
        for k in writes + ps_reads:
            self.lastw[k] = idx
            self.readers[k] = []
        for k in reads:
            if k not in writes:
                self.readers.setdefault(k, []).append(idx)
        self.ops.append(dict(eng=eng, fn=fn, deps=sorted(deps), raw=deps, dma=dma, cc=cc, marked=False))
        return idx

    @staticmethod
    def _skip(op, dop, d):
        if dop["dma"] or dop["cc"] or op["dma"] or op["cc"] or dop["eng"] != op["eng"]:
            return False
        if op["eng"] == "pe":
            return True
        return not op["raw"][d]

    def mark(self, name):
        import os
        if os.environ.get("KMARK"):
            print("MARK", self.inst, name, len(self.ops))

    def emit(self, nc, st):
        ops = self.ops
        for op in ops:
            for d in op["deps"]:
                dop = ops[d]
                if dop["dma"] or dop["cc"]:
                    continue
                if self._skip(op, dop, d):
                    continue
                dop["marked"] = True
        last = {}
        for i, op in enumerate(ops):
            if not op["dma"] and not op["cc"]:
                last[op["eng"]] = i
        for e, i in last.items():
            ops[i]["marked"] = True
        gs = self.GLOBAL
        cnt = gs.setdefault("cnt", {e: 0 for e in ENGS})
        dcount = gs.setdefault("dcount", {"sp": 0, "pool": 0})
        duse = gs.setdefault("duse", {"sp": [0] * self.NDS, "pool": [0] * self.NDS})
        SEM = gs.setdefault("SEM", {})
        gst = gs["stack"]
        ncc0 = gs.get("ncc", 0)
        ncc = ncc0
        semkeys = set()
        for op in ops:
            if op["cc"]:
                op["sem"] = ("cc", ncc)
                op["val"] = 1
                ncc += 1
            elif op["dma"]:
                q = op["eng"]
                j = dcount[q] % self.NDS
                dcount[q] += 1
                duse[q][j] += 1
                op["sem"] = ("d", q, j)
                op["val"] = 16 * duse[q][j]
            elif op["marked"]:
                e = op["eng"]
                ep = cnt[e] // self.EPOCH
                cnt[e] += 1
                op["sem"] = ("c", e, ep)
                op["val"] = cnt[e] - ep * self.EPOCH
            else:
                continue
            semkeys.add(op["sem"])
        gs["ncc"] = ncc
        for k in sorted(semkeys, key=str):
            if k not in SEM:
                Sched.UID[0] += 1
                SEM[k] = gst.enter_context(nc.semaphore(f"s{Sched.UID[0]}_" + "_".join(str(x) for x in k)))
        finals = []
        for e, i in last.items():
            finals.append((ops[i]["sem"], ops[i]["val"]))
        for q in ("sp", "pool"):
            for j in range(self.NDS):
                if duse[q][j]:
                    finals.append((("d", q, j), 16 * duse[q][j]))
        for c in range(ncc0, ncc):
            finals.append((("cc", c), 1))

        def run(engname, eng):
            waited = {}

            def wait(sk, val):
                if waited.get(sk, 0) >= val:
                    return
                eng.wait_ge(SEM[sk], val)
                waited[sk] = val

            for op in ops:
                if op["eng"] != engname:
                    continue
                for d in op["deps"]:
                    dop = ops[d]
                    if self._skip(op, dop, d):
                        continue
                    wait(dop["sem"], dop["val"])
                if op["dma"] and op["val"] > 16:
                    wait(op["sem"], op["val"] - 16)
                ins = op["fn"](eng)
                if op["cc"]:
                    ins.then_inc(SEM[op["sem"]])
                elif op["dma"]:
                    ins.then_inc(SEM[op["sem"]], 16)
                elif op["marked"]:
                    ins.then_inc(SEM[op["sem"]], 1)
            for sk, val in finals:
                wait(sk, val)

        block = st.enter_context(nc.Block())

        @block.tensor
        def _(e):
            run("pe", e)

        @block.scalar
        def _(e):
            run("act", e)

        @block.vector
        def _(e):
            run("dve", e)

        @block.gpsimd
        def _(e):
            run("pool", e)

        @block.sync
        def _(e):
            run("sp", e)


class Geo:
    def __init__(self, PL, SL):
        self.PL, self.SL = PL, SL
        self.NCP, self.NCS = PL // 128, SL // 128
        self.NCH = self.NCP + self.NCS
        self.T = PL + SL
        self.TC = self.T + 256
        self.NT = self.T // 512
        self.NDT = self.T // 256
        self.NHB = (PL // 256 + 1) + (SL // 256 + 1)

    def seg(self, n):
        return 0 if n < self.NCP else 1

    def xrow(self, n):
        return n * 128

    def x1row(self, n):
        return n * 128 + (1 if n >= self.NCP else 0)

    def fmcol(self, n):
        return n * 128 + (128 if n >= self.NCP else 0)


def build(PL=2048, SL=4096, debug=False, npasses=6):
    G = Geo(PL, SL)
    T, TC, NCP, NCS, NCH = G.T, G.TC, G.NCP, G.NCS, G.NCH
    nc = bass.Bass("TRN2", target_bir_lowering=False)

    def din(name, shape, dt=F32):
        return nc.dram_tensor(name, list(shape), dt, kind="ExternalInput").ap()

    x_in = din("x", [T, D])
    w_in = din("w_in", [DEPTH, D, INW])
    wba = din("wba", [DEPTH, 512, D])
    wbr = din("wbr", [DEPTH, 512, D])
    wo = din("wo", [DEPTH, D, D])
    wf1 = din("wf1", [DEPTH, D, 2 * DFF])
    wf2 = din("wf2", [DEPTH, DFF, D])
    g_mix = din("g_mix", [DEPTH, D])
    g_ffn = din("g_ffn", [DEPTH, D])
    g_fin = din("g_fin", [1, D])
    g_ret = din("g_ret", [DEPTH, 512])
    sinkd = din("sink", [DEPTH, 8])
    ldf = din("ldf", [DEPTH, 4])
    ldb = din("ldb", [DEPTH, 4])
    cwd = din("cw", [DEPTH, 128, NFC * 3])
    cbd = din("cb", [DEPTH, 128, NFC])
    fmtab = din("fmtab", [128, 4, T])
    tmtab = din("tmtab", [T, 3 * 64])
    NCST = 528 + NCS
    cst = din("cst", [128, NCST])
    masks = din("masks", [128, 2 * 512])
    identd = din("ident", [128, 128])
    rotd = din("rot", [128, 256])

    okind = "ExternalOutput"
    y_out = nc.dram_tensor("y", [T, D], F32, kind=okind).ap()

    def scratch(name, shape, dt):
        if debug:
            return nc.dram_tensor(name, list(shape), dt, kind=okind)
        return nc.dram_tensor(name, list(shape), dt)

    FM_t = scratch("FM", [NFM, 128, TC], BF16)
    TM_t = scratch("TMs", [TC, TMW], BF16)
    RB_t = scratch("RB", [NCH, 128, 512], F32)
    X1_t = scratch("X1", [T + 2, D], F32)
    X2_t = scratch("X2", [T, D], F32)
    FM, TMs, RB, X1, X2 = FM_t.ap(), TM_t.ap(), RB_t.ap(), X1_t.ap(), X2_t.ap()
    PKG1_t = nc.dram_tensor("PKG1", [128, 1024], F32)
    G1_t = nc.dram_tensor("G1", [256, 1024], F32)
    PKGK_t = nc.dram_tensor("PKGK", [128, 512], BF16)
    GK_t = nc.dram_tensor("GK", [256, 512], BF16)
    PKG2_t = nc.dram_tensor("PKG2", [2, D], F32)
    G2_t = nc.dram_tensor("G2", [4, D], F32)
    PKG1, G1, PKGK, GK, PKG2, G2 = (t.ap() for t in (PKG1_t, G1_t, PKGK_t, GK_t, PKG2_t, G2_t))

    C_RELQK, C_RELKQ, C_QP1, C_Q128M = 0, 128, 256, 384
    C_IDXK, C_I127, C_FL, C_FR, C_HFL, C_ONE, C_NW = 512, 513, 514, 515, 516, 520, 528

    uid = [0]

    def sb(st, name, shape, dt):
        uid[0] += 1
        return st.enter_context(nc.sbuf_tensor(f"{name}_u{uid[0]}", list(shape), dt))

    def ps(st, name, shape, dt=F32):
        uid[0] += 1
        return st.enter_context(nc.psum_tensor(f"{name}_u{uid[0]}", list(shape), dt))

    def common(st, S):
        identf = sb(st, "identf", [128, 128], F32)
        ident = sb(st, "ident", [128, 128], BF16)
        CST = sb(st, "CST", [128, NCST], F32)
        S.add("sp", lambda e: e.dma_start(out=identf[:], in_=identd[:, :]), writes=["identf"], dma=True)
        S.add("sp", lambda e: e.dma_start(out=CST[:], in_=cst[:, :]), writes=["CST"], dma=True)
        S.add("dve", lambda e: e.tensor_copy(out=ident[:], in_=identf[:]), reads=["identf"], writes=["ident"])
        return ident, CST

    def load_w(S, dst, src, key, maxc=2048):
        ncols = src.shape[-1]
        c0 = 0
        keys = []
        while c0 < ncols:
            c1 = min(ncols, c0 + maxc)
            S.add("pool", lambda e, a=dst[:, c0:c1], b=src[:, c0:c1]: e.dma_start(out=a, in_=b),
                  writes=[(key, c0)], dma=True)
            keys.append((key, c0))
            c0 = c1
        return keys

    def prep_chunk(S, tag, src_ap, nrows, XTs, xkey, XNs, xnkey, SQJ, SS, ss_col, GB, gkey, TP, tpkey,
                   ident, dst_ap, dkeys, rowscale=None, phase="all"):
        xkeys = xkey if isinstance(xkey, list) else [xkey]
        if phase == "post":
            for k in range(8):
                S.add("pe", lambda e, k=k: e.transpose(out=TP[:, k, 0:nrows], in_=XNs[0:nrows, k * 128:(k + 1) * 128],
                                                       identity=ident[0:nrows, 0:nrows]),
                      reads=[xnkey, "ident"], writes=[tpkey])
            S.add("act", lambda e: e.activation(out=dst_ap, in_=TP[:, :, 0:nrows], func=AF.Copy),
                  reads=[tpkey], writes=dkeys)
            return
        if src_ap is not None:
            S.add("sp", lambda e: e.dma_start(out=XTs[0:nrows, :], in_=src_ap), writes=xkeys, dma=True)
        if rowscale is not None:
            S.add("dve", lambda e: e.tensor_scalar(out=XTs[:, :], in0=XTs[:, :], scalar1=rowscale, scalar2=None,
                                                   op0=ALU.mult), reads=["CST"], writes=xkeys)
        sc = SS[:, ss_col:ss_col + 1]
        sk = ("SS", tag, ss_col)
        S.add("act", lambda e: e.activation(out=SQJ[0:nrows, :], in_=XTs[0:nrows, :], func=AF.Square,
                                            accum_out=sc[0:nrows, :]),
              reads=xkeys, writes=[("SQJ", tag), sk])
        S.add("dve", lambda e: e.tensor_scalar(out=sc[0:nrows, :], in0=sc[0:nrows, :], scalar1=1.0 / D, scalar2=EPS,
                                               op0=ALU.mult, op1=ALU.add), reads=[sk], writes=[sk])
        S.add("act", lambda e: e.activation(out=sc[0:nrows, :], in_=sc[0:nrows, :], func=AF.Sqrt), reads=[sk], writes=[sk])
        S.add("dve", lambda e: e.reciprocal(out=sc[0:nrows, :], in_=sc[0:nrows, :]), reads=[sk], writes=[sk])
        S.add("dve", lambda e: e.scalar_tensor_tensor(out=XNs[0:nrows, :], in0=XTs[0:nrows, :], scalar=sc[0:nrows, :],
                                                      in1=GB[0:nrows, :], op0=ALU.mult, op1=ALU.mult),
              reads=xkeys + [sk, gkey], writes=[xnkey])
        if phase == "pre":
            return
        for k in range(8):
            S.add("pe", lambda e, k=k: e.transpose(out=TP[:, k, 0:nrows], in_=XNs[0:nrows, k * 128:(k + 1) * 128],
                                                   identity=ident[0:nrows, 0:nrows]),
                  reads=[xnkey, "ident"], writes=[tpkey])
        S.add("act", lambda e: e.activation(out=dst_ap, in_=TP[:, :, 0:nrows], func=AF.Copy),
              reads=[tpkey], writes=dkeys)

    def pass_A(l, xsrc):
        with ExitStack() as st:
            S = Sched()
            ident, CST = common(st, S)
            WIN = sb(st, "WIN", [128, 8, INW], BF16)
            GB = sb(st, "GB", [128, D], F32)
            XT = [sb(st, f"XT{i}", [128, D], F32) for i in range(4)]
            XN = [sb(st, f"XN{i}", [128, D], BF16) for i in range(4)]
            SQJ = sb(st, "SQJ", [128, D], BF16)
            SS = sb(st, "SS", [128, 8], F32)
            HT = [sb(st, f"HT{i}", [128, 8, 512], BF16) for i in range(2)]
            TAB = [sb(st, f"TAB{i}", [128, 4, 512], F32) for i in range(2)]
            TMT = [sb(st, f"TMT{i}", [128, 3, 64], F32) for i in range(2)]
            T1 = [sb(st, f"T1_{i}", [128, 512], F32) for i in range(2)]
            T2 = [sb(st, f"T2_{i}", [128, 512], F32) for i in range(2)]
            T1T = [sb(st, f"T1T_{i}", [128, 512], F32) for i in range(2)]
            T2T = [sb(st, f"T2T_{i}", [128, 512], F32) for i in range(2)]
            FMO = [sb(st, f"FMO{i}", [128, 512], BF16) for i in range(2)]
            TMO = [sb(st, f"TMO{i}", [128, TMW], BF16) for i in range(2)]
            TP = [ps(st, f"TP{i}", [128, 8, 128], BF16) for i in range(2)]
            PF = [ps(st, f"PF{i}", [128, 512]) for i in range(2)]
            PT = [ps(st, f"PT{i}", [128, 512]) for i in range(3)]
            PR = ps(st, "PR", [128, 512])
            XB = [sb(st, f"XB{i}", [128, 512], BF16) for i in range(2)]
            ROTF = sb(st, "ROTF", [128, 256], F32)
            ROT = sb(st, "ROT", [128, 256], BF16)
            S.add("sp", lambda e: e.dma_start(out=ROTF[:], in_=rotd[:, :]), writes=["ROTF"], dma=True)
            S.add("dve", lambda e: e.tensor_copy(out=ROT[:], in_=ROTF[:]), reads=["ROTF"], writes=["ROT"])

            S.add("sp", lambda e: e.dma_start(out=GB[:], in_=g_mix[l:l + 1, :].partition_broadcast(128)),
                  writes=["GB"], dma=True)
            WINK = {}
            for k in range(8):
                WINK[k] = load_w(S, WIN[:, k, :], w_in[l, k * 128:(k + 1) * 128, :], ("WIN", k), maxc=1792)

            cctr = [0]
            fctr = [0]
            tctr = [0]

            def prep_tile(tt, phase="all"):
                hs = tt % 2
                for c in range(4):
                    n = tt * 4 + c
                    i = c
                    r0 = G.xrow(n)
                    prep_chunk(S, "A", xsrc[r0:r0 + 128, :], 128, XT[i], ("XT", i), XN[i], ("XN", i), SQJ, SS, i,
                               GB, "GB", TP[i % 2], ("TP", i % 2), ident, HT[hs][:, :, c * 128:(c + 1) * 128],
                               [("HT", hs, c)], phase=phase)

            def main_tile(tt, mid=None):
                hs = tt % 2
                hkeys = [("HT", hs, c) for c in range(4)]
                n0 = tt * 4
                c0 = G.fmcol(n0)
                r0 = G.xrow(n0)
                S.add("sp", lambda e: e.dma_start(out=TAB[hs][:], in_=fmtab[:, :, r0:r0 + 512]),
                      writes=[("TAB", hs)], dma=True)
                fm_pend = []
                for f in range(NFM):
                    pf = fctr[0] % 2
                    p = fctr[0] % 2
                    fctr[0] += 1
                    for k in range(8):
                        S.add("pe", lambda e, k=k, f=f, pf=pf: e.matmul(PF[pf][:], lhsT=WIN[:, k, f * 128:(f + 1) * 128],
                                                                       rhs=HT[hs][:, k, :], start=(k == 0), stop=(k == 7)),
                              reads=WINK[k] + hkeys, writes=[("PF", pf)])
                    att = f < 5
                    ci, si = (0, 1) if att else (2, 3)
                    ro = 0 if att else 128
                    S.add("act", lambda e, p=p, pf=pf: e.activation(out=XB[p][:], in_=PF[pf][:], func=AF.Copy),
                          reads=[("PF", pf)], writes=[("XB", p)])
                    S.add("dve", lambda e, p=p, pf=pf, ci=ci: e.tensor_tensor(out=T1[p][:], in0=PF[pf][:], in1=TAB[hs][:, ci, :],
                                                                             op=ALU.mult),
                          reads=[("PF", pf), ("TAB", hs)], writes=[("T1", p)])

                    def stage2(p=p, ro=ro, si=si, f=f):
                        S.add("pe", lambda e: e.matmul(PR[:], lhsT=ROT[:, ro:ro + 128], rhs=XB[p][:], start=True, stop=True),
                              reads=[("XB", p), "ROT"], writes=["PR"])
                        S.add("dve", lambda e: e.tensor_tensor(out=T2[p][:], in0=PR[:], in1=TAB[hs][:, si, :], op=ALU.mult),
                              reads=["PR", ("TAB", hs)], writes=[("T2", p)])
                        S.add("pool", lambda e: e.tensor_tensor(out=FMO[p][:], in0=T1[p][:], in1=T2[p][:], op=ALU.add),
                              reads=[("T1", p), ("T2", p)], writes=[("FMO", p)])
                        S.add("sp", lambda e: e.dma_start(out=FM[f, :, c0:c0 + 512], in_=FMO[p][:]),
                              reads=[("FMO", p)], writes=[("FMd", f, tt)], dma=True)
                    if fm_pend:
                        fm_pend.pop(0)()
                    fm_pend.append(stage2)
                while fm_pend:
                    fm_pend.pop(0)()
                if mid is not None:
                    mid()
                S.mark(f"A tile {tt} FM done")
                for c in range(4):
                    S.mark(f"A tile {tt} TM chunk {c}")
                    n = n0 + c
                    o_ = n % 2
                    rr = G.xrow(n)
                    S.add("sp", lambda e, o_=o_, rr=rr: e.dma_start(
                        out=TMT[o_][:], in_=tmtab[rr:rr + 128, :].rearrange("p (a b) -> p a b", a=3)),
                          writes=[("TMT", o_)], dma=True)
                    for ct in range(8):
                        p = tctr[0] % 3
                        tctr[0] += 1
                        w = 128 if ct == 7 else 512
                        wc0 = 1664 + ct * 512
                        for k in range(8):
                            S.add("pe", lambda e, k=k, p=p, w=w, wc0=wc0, c=c: e.matmul(
                                PT[p][:, 0:w], lhsT=HT[hs][:, k, c * 128:(c + 1) * 128], rhs=WIN[:, k, wc0:wc0 + w],
                                start=(k == 0), stop=(k == 7)),
                                  reads=WINK[k] + [("HT", hs, c)], writes=[("PT", p)])
                        okey = ("TMO", o_, ct)
                        if ct == 0:
                            q = p % 2
                            pv = PT[p][:].rearrange("p (h t d) -> p h t d", h=4, t=2)
                            t1v = T1T[q][:].rearrange("p (g d) -> p g d", d=64)
                            t2v = T2T[q][:].rearrange("p (h t d) -> p h t d", h=4, t=2)
                            S.add("dve", lambda e, p=p, o_=o_, t1v=t1v: e.tensor_tensor(
                                out=t1v, in0=PT[p][:].rearrange("p (g d) -> p g d", d=64),
                                in1=TMT[o_][:, 0:1, :].broadcast_to([128, 8, 64]), op=ALU.mult),
                                  reads=[("PT", p), ("TMT", o_)], writes=[("T1T", q)])
                            S.add("dve", lambda e, pv=pv, t2v=t2v, o_=o_: e.tensor_tensor(
                                out=t2v[:, :, 0, :], in0=pv[:, :, 1, :],
                                in1=TMT[o_][:, 1:2, :].broadcast_to([128, 4, 64]), op=ALU.mult),
                                  reads=[("PT", p), ("TMT", o_)], writes=[("T2T", q, "a")])
                            S.add("dve", lambda e, pv=pv, t2v=t2v, o_=o_: e.tensor_tensor(
                                out=t2v[:, :, 1, :], in0=pv[:, :, 0, :],
                                in1=TMT[o_][:, 2:3, :].broadcast_to([128, 4, 64]), op=ALU.mult),
                                  reads=[("PT", p), ("TMT", o_)], writes=[("T2T", q, "b")])
                            S.add("pool", lambda e, q=q, o_=o_: e.tensor_tensor(
                                out=TMO[o_][:, KR0:KR0 + 512], in0=T1T[q][:], in1=T2T[q][:], op=ALU.add),
                                  reads=[("T1T", q), ("T2T", q, "a"), ("T2T", q, "b")], writes=[okey])
                        else:
                            func = AF.Copy if ct in (1, 7) else (AF.Silu if ct == 2 else AF.Sigmoid)
                            dc0 = {1: VR0, 2: GR0, 7: VA0}.get(ct, GT0 + (ct - 3) * 512)
                            S.add("act", lambda e, p=p, w=w, dc0=dc0, func=func, o_=o_: e.activation(
                                out=TMO[o_][:, dc0:dc0 + w], in_=PT[p][:, 0:w], func=func),
                                  reads=[("PT", p)], writes=[okey])
                    rc = G.fmcol(n)
                    S.add("sp", lambda e, o_=o_, rc=rc: e.dma_start(out=TMs[rc:rc + 128, :], in_=TMO[o_][:]),
                          reads=[("TMO", o_, ct) for ct in range(8)], writes=[("TMd", n)], dma=True)

            prep_tile(0)
            for tt in range(G.NT):
                mid = None
                if tt + 1 < G.NT:
                    prep_tile(tt + 1, "pre")
                    mid = lambda t=tt + 1: prep_tile(t, "post")
                main_tile(tt, mid)
            S.emit(nc, st)

    def pass_B(l, xsrc):
        with ExitStack() as st:
            S = Sched()
            ident, CST = common(st, S)
            WBA = sb(st, "WBA", [64, 8, D], BF16)
            WBR = sb(st, "WBR", [128, 4, D], BF16)
            WO = sb(st, "WO", [128, 8, D], BF16)
            MK = sb(st, "MK", [128, 2, 512], BF16)
            MKF = sb(st, "MKF", [128, 2, 512], BF16)
            LFB = sb(st, "LFB", [128, 8], F32)
            ZET = sb(st, "ZET", [128, 8], F32)
            CDC = sb(st, "CDC", [128, 8], F32)
            ESK = sb(st, "ESK", [128, 8], F32)
            DTOT = sb(st, "DTOT", [128, 4, 128], BF16)
            DTMP = sb(st, "DTMP", [128, 128], F32)
            DTMP2 = sb(st, "DTMP2", [128, 128], F32)
            XIF = sb(st, "XIF", [128, 4, 128], F32)
            XIB = sb(st, "XIB", [128, 4, 128], F32)
            WBW = sb(st, "WBW", [128, NCS, 4], F32)
            NG = sb(st, "NG", [128, 512], F32)
            SINF = sb(st, "SINF", [128, 512], F32)
            SINB = sb(st, "SINB", [128, 512], F32)
            RBST = [sb(st, f"RBST{i}", [128, 512], F32) for i in range(2)]
            RFST = [sb(st, f"RFST{i}", [128, 512], F32) for i in range(2)]
            KVL = [sb(st, f"KVL{i}", [128, 1024], BF16) for i in range(2)]
            VZ = [sb(st, f"VZ{i}", [128, 512], BF16) for i in range(2)]
            QA = [sb(st, f"QA{i}", [128, 4, 128], BF16) for i in range(2)]
            KA = [sb(st, f"KA{i}", [128, 384], BF16) for i in range(2)]
            VA1 = [sb(st, f"VA1_{i}", [128, 3, 2, 128], BF16) for i in range(2)]
            QR = [sb(st, f"QR{i}", [128, 4, 128], BF16) for i in range(2)]
            KR = [sb(st, f"KR{i}", [128, 4, 128], BF16) for i in range(2)]
            GG = [sb(st, f"GG{i}", [128, 2560], BF16) for i in range(3)]
            RBL = [sb(st, f"RBL{i}", [128, 512], F32) for i in range(2)]
            XR = [sb(st, f"XR{i}", [128, D], F32) for i in range(3)]
            PTs = [sb(st, f"PTs{i}", [128, 512], BF16) for i in range(4)]
            def two(name, shape, dt):
                return [sb(st, f"{name}{i}", shape, dt) for i in range(2)]
            DENs = two("DEN", [64, 512], F32)
            ATTs = [[sb(st, f"ATT{i}_{k}", [64, 4, 128], BF16) for k in range(2)] for i in range(2)]
            INMs = two("INM", [128, 4, 128], BF16)
            QXFs = two("QXF", [128, 4, 128], BF16)
            QXBs = two("QXB", [128, 4, 128], BF16)
            RFBs = two("RFB", [128, 512], BF16)
            RBBs = two("RBB", [128, 512], BF16)
            RBTs = two("RBT", [128, 512], F32)
            SQs = two("SQ", [128, 512], F32)
            SSRs = two("SSR", [128, 4], F32)
            TRs = two("TR", [128, 512], F32)
            T2Rs = two("T2R", [128, 512], F32)
            RETs = two("RET", [128, 512], BF16)
            RETTs = two("RETT", [128, 4, 128], BF16)
            M1s = two("M1", [128, 512], F32)
            M2s = two("M2", [128, 512], F32)
            MERs = two("MER", [128, D], BF16)
            MTs = two("MT", [128, 8, 128], BF16)
            SP = [ps(st, f"SP{i}", [128, 512]) for i in range(2)]
            OP = ps(st, "OP", [128, 512])
            IP = ps(st, "IP", [128, 4, 128])
            ORp = ps(st, "ORp", [128, 4, 128])
            TPB = ps(st, "TPB", [128, 8, 128], BF16)
            APs = ps(st, "APs", [128, 512])
            RPs = ps(st, "RPs", [128, 512])

            WBAK, WBRK, WOK = {}, {}, {}
            for h in range(8):
                WBAK[h] = load_w(S, WBA[:, h, :], wba[l, h * 64:(h + 1) * 64, :], ("WBA", h), maxc=1024)
            for h in range(4):
                WBRK[h] = load_w(S, WBR[:, h, :], wbr[l, h * 128:(h + 1) * 128, :], ("WBR", h), maxc=1024)
            for k in range(8):
                WOK[k] = load_w(S, WO[:, k, :], wo[l, k * 128:(k + 1) * 128, :], ("WO", k), maxc=1024)
            S.add("pool", lambda e: e.dma_start(out=MK[:].rearrange("p a b -> p (a b)"), in_=masks[:, :]),
                  writes=["MK"], dma=True)
            S.add("sp", lambda e: e.dma_start(out=LFB[:, 0:4], in_=ldf[l:l + 1, :].partition_broadcast(128)),
                  writes=["LFB"], dma=True)
            S.add("sp", lambda e: e.dma_start(out=LFB[:, 4:8], in_=ldb[l:l + 1, :].partition_broadcast(128)),
                  writes=["LFB"], dma=True)
            S.add("sp", lambda e: e.dma_start(out=ESK[:], in_=sinkd[l:l + 1, :].partition_broadcast(128)),
                  writes=["ESK"], dma=True)
            S.add("sp", lambda e: e.dma_start(out=NG[:], in_=g_ret[l:l + 1, :].partition_broadcast(128)),
                  writes=["NG"], dma=True)
            S.add("act", lambda e: e.activation(out=ESK[:], in_=ESK[:], func=AF.Exp), reads=["ESK"], writes=["ESK"])
            for j, cf in ((0, C_FL), (1, C_FR)):
                S.add("dve", lambda e, j=j, cf=cf: e.tensor_scalar(out=MKF[:, j, :], in0=MK[:, j, :],
                                                                  scalar1=CST[:, cf:cf + 1], scalar2=None, op0=ALU.mult),
                      reads=["MK", "CST"], writes=[("MKF", j)])
            KSC = 128.0 ** -0.5
            S.add("act", lambda e: e.activation(out=ZET[:, 0:4], in_=LFB[:, 0:4], func=AF.Exp,
                                                scale=CST[:, C_I127:C_I127 + 1]), reads=["LFB", "CST"], writes=["ZETa"])
            S.add("act", lambda e: e.activation(out=ZET[:, 4:8], in_=LFB[:, 4:8], func=AF.Exp,
                                                scale=CST[:, C_IDXK:C_IDXK + 1]), reads=["LFB", "CST"], writes=["ZETb"])
            S.add("dve", lambda e: e.tensor_scalar(out=ZET[:], in0=ZET[:], scalar1=KSC, scalar2=None, op0=ALU.mult),
                  reads=["ZETa", "ZETb"], writes=["ZET"])
            S.add("act", lambda e: e.activation(out=CDC[:], in_=LFB[:], func=AF.Exp, scale=128.0),
                  reads=["LFB"], writes=["CDC"])
            for h in range(4):
                S.add("dve", lambda e, h=h: e.tensor_scalar(out=DTMP[:], in0=CST[:, C_RELQK:C_RELQK + 128],
                                                           scalar1=LFB[:, h:h + 1], scalar2=None, op0=ALU.mult),
                      reads=["CST", "LFB"], writes=["DTMP"])
                S.add("dve", lambda e, h=h: e.scalar_tensor_tensor(out=DTMP2[:], in0=CST[:, C_RELKQ:C_RELKQ + 128],
                                                                  scalar=LFB[:, 4 + h:5 + h], in1=DTMP[:],
                                                                  op0=ALU.mult, op1=ALU.add),
                      reads=["CST", "LFB", "DTMP"], writes=["DTMP2"])
                S.add("act", lambda e: e.activation(out=DTMP[:], in_=DTMP2[:], func=AF.Exp),
                      reads=["DTMP2"], writes=["DTMP"])
                S.add("dve", lambda e, h=h: e.tensor_scalar(out=DTOT[:, h, :], in0=DTMP[:], scalar1=KSC, scalar2=None,
                                                           op0=ALU.mult), reads=["DTMP"], writes=[("DTOT", h)])
                S.add("act", lambda e, h=h: e.activation(out=XIF[:, h, :], in_=CST[:, C_QP1:C_QP1 + 128], func=AF.Exp,
                                                        scale=LFB[:, h:h + 1]), reads=["CST", "LFB"], writes=[("XIF", h)])
                S.add("act", lambda e, h=h: e.activation(out=XIB[:, h, :], in_=CST[:, C_Q128M:C_Q128M + 128], func=AF.Exp,
                                                        scale=LFB[:, 4 + h:5 + h]), reads=["CST", "LFB"], writes=[("XIB", h)])
            DTK = [("DTOT", h) for h in range(4)]
            XIFK = [("XIF", h) for h in range(4)]
            XIBK = [("XIB", h) for h in range(4)]
            S.add("dve", lambda e: e.tensor_tensor(out=WBW[:], in0=CST[:, C_NW:C_NW + NCS].unsqueeze(2).broadcast_to([128, NCS, 4]),
                                                   in1=LFB[:, 4:8].unsqueeze(1).broadcast_to([128, NCS, 4]), op=ALU.mult),
                  reads=["CST", "LFB"], writes=["WBW"])
            S.add("act", lambda e: e.activation(out=WBW[:], in_=WBW[:], func=AF.Exp), reads=["WBW"], writes=["WBW"])
            for i in range(2):
                S.add("dve", lambda e, i=i: e.memset(VA1[i][:, :, :, 64:128], 1.0), writes=[("VA1o", i)])

            def v4(t):
                return t[:].rearrange("p (h e) -> p h e", h=4)

            def bc4(col_ap):
                return col_ap.unsqueeze(2).broadcast_to([128, 4, 128])

            ldc = [0]

            RBSTP = [sb(st, f"RBSTP{i}", [128, 512], F32) for i in range(2)]
            KVS = [[sb(st, f"KVS{a_}_{i}", [128, 1024], BF16) for i in range(2)] for a_ in range(3)]
            VZS = [[sb(st, f"VZS{a_}_{i}", [128, 512], BF16) for i in range(2)] for a_ in range(3)]

            def scan(chunks, direction, ST, skey, store_rb, sid, PSB, pkey):
                KVL = KVS[sid]
                VZ = VZS[sid]
                ldc = [0]
                IP = PSB
                cur = 0
                first = True
                for n in chunks:
                    if first:
                        S.add("dve", lambda e, cur=cur: e.memset(ST[cur][:], 0.0), writes=[(skey, cur)])
                        first = False
                    if store_rb:
                        S.add("sp", lambda e, n=n, cur=cur: e.dma_start(out=RB[n, :, :], in_=ST[cur][:]),
                              reads=[(skey, cur)], writes=[("RB", n)], dma=True)
                    i = ldc[0] % 2
                    ldc[0] += 1
                    rc = G.fmcol(n)
                    S.add("sp", lambda e, i=i, rc=rc: e.dma_start(out=KVL[i][:], in_=TMs[rc:rc + 128, 0:1024]),
                          writes=[("KVS", sid, i)], dma=True)
                    zc = 0 if direction == "f" else 4
                    S.add("dve", lambda e, i=i, zc=zc: e.tensor_tensor(
                        out=v4(VZ[i]), in0=KVL[i][:, 512:1024].rearrange("p (h e) -> p h e", h=4),
                        in1=bc4(ZET[:, zc:zc + 4]), op=ALU.mult),
                          reads=[("KVS", sid, i), "ZET"], writes=[("VZS", sid, i)])
                    for h in range(4):
                        S.add("pe", lambda e, i=i, h=h: e.matmul(IP[:, h, :], lhsT=KVL[i][:, h * 128:(h + 1) * 128],
                                                                rhs=VZ[i][:, h * 128:(h + 1) * 128], start=True, stop=True),
                              reads=[("KVS", sid, i), ("VZS", sid, i)], writes=[pkey])
                    nxt = 1 - cur
                    S.add("dve", lambda e, cur=cur, nxt=nxt, zc=zc: e.tensor_tensor(
                        out=v4(ST[nxt]), in0=v4(ST[cur]), in1=bc4(CDC[:, zc:zc + 4]), op=ALU.mult),
                          reads=[(skey, cur), "CDC"], writes=[(skey, nxt)])
                    S.add("dve", lambda e, nxt=nxt: e.tensor_tensor(
                        out=ST[nxt][:], in0=ST[nxt][:], in1=IP[:].rearrange("p h e -> p (h e)"), op=ALU.add),
                          reads=[pkey, (skey, nxt)], writes=[(skey, nxt)])
                    cur = nxt
                return cur

            OPv = OP[:].rearrange("p (h e) -> p h e", h=4)
            streams = []
            S.cap = []
            cur = scan(list(range(NCH - 1, NCP - 1, -1)), "b", RBST, "RBST", True, 0, IP, "IP")
            S.add("sp", lambda e, cur=cur: e.dma_start(out=PKG1[:, 512:1024], in_=RBST[cur][:]),
                  reads=[("RBST", cur)], writes=["PKG1b"], dma=True)
            streams.append(S.cap)
            S.cap = []
            cur = scan(list(range(NCP, NCH)), "f", RFST, "RFST", False, 1, ORp, "ORp")
            S.add("sp", lambda e, cur=cur: e.dma_start(out=PKG1[:, 0:512], in_=RFST[cur][:]),
                  reads=[("RFST", cur)], writes=["PKG1a"], dma=True)
            streams.append(S.cap)
            S.cap = []
            scan(list(range(NCP - 1, -1, -1)), "b", RBSTP, "RBSTP", True, 2, OPv, "OP")
            streams.append(S.cap)
            S.cap = None
            S.interleave(streams)
            cS0 = G.fmcol(NCP)
            cS1 = G.fmcol(NCH - 1)
            S.add("sp", lambda e: e.dma_start(out=PKGK[:, 0:128], in_=FM[4, :, cS0:cS0 + 128]), writes=["PKGK"], dma=True)
            S.add("sp", lambda e: e.dma_start(out=PKGK[:, 128:256], in_=FM[4, :, cS1:cS1 + 128]), writes=["PKGK"], dma=True)
            S.add("sp", lambda e: e.dma_start(out=PKGK[:, 256:384], in_=TMs[cS0:cS0 + 128, VA0:VA0 + 128]), writes=["PKGK"], dma=True)
            S.add("sp", lambda e: e.dma_start(out=PKGK[:, 384:512], in_=TMs[cS1:cS1 + 128, VA0:VA0 + 128]), writes=["PKGK"], dma=True)
            S.add("pool", lambda e: e.collective_compute("AllGather", ALU.bypass, replica_groups=GROUPS,
                                                         ins=[PKG1_t.ap().opt()], outs=[G1_t.ap().opt()]),
                  reads=["PKG1a", "PKG1b"], writes=["G1"], cc=True)
            S.add("pool", lambda e: e.collective_compute("AllGather", ALU.bypass, replica_groups=GROUPS,
                                                         ins=[PKGK_t.ap().opt()], outs=[GK_t.ap().opt()]),
                  reads=["PKGK"], writes=["GK"], cc=True)

            def unpack():
                cL = PL
                cR = PL + 128 + SL
                S.add("sp", lambda e: e.dma_start(out=FM[4, :, cL:cL + 128], in_=GK[0:128, 128:256]),
                      reads=["GK"], writes=[("FMh", "L")], dma=True)
                S.add("sp", lambda e: e.dma_start(out=FM[4, :, cR:cR + 128], in_=GK[128:256, 0:128]),
                      reads=["GK"], writes=[("FMh", "R")], dma=True)
                S.add("sp", lambda e: e.dma_start(out=TMs[cL:cL + 128, VA0:VA0 + 128], in_=GK[0:128, 384:512]),
                      reads=["GK"], writes=[("TMh", "L")], dma=True)
                S.add("sp", lambda e: e.dma_start(out=TMs[cR:cR + 128, VA0:VA0 + 128], in_=GK[128:256, 256:384]),
                      reads=["GK"], writes=[("TMh", "R")], dma=True)
                S.add("sp", lambda e: e.dma_start(out=SINF[:], in_=G1[0:128, 0:512]), reads=["G1"], writes=["SINF"], dma=True)
                S.add("sp", lambda e: e.dma_start(out=SINB[:], in_=G1[128:256, 512:1024]), reads=["G1"], writes=["SINB"], dma=True)
                S.add("dve", lambda e: e.tensor_scalar(out=SINF[:], in0=SINF[:], scalar1=CST[:, C_FL:C_FL + 1], scalar2=None,
                                                       op0=ALU.mult), reads=["SINF", "CST"], writes=["SINF"])
                S.add("dve", lambda e: e.tensor_scalar(out=SINB[:], in0=SINB[:], scalar1=CST[:, C_FR:C_FR + 1], scalar2=None,
                                                       op0=ALU.mult), reads=["SINB", "CST"], writes=["SINB"])

            def loads(n):
                i = n % 2
                sg = G.seg(n)
                c0 = G.fmcol(n)
                lo = NCP if sg else 0
                hi = NCH if sg else NCP
                jl = 0 if (n > lo or sg == 1) else 1
                jh = 2 if (n < hi - 1 or sg == 1) else 1
                extra = []
                if sg == 1 and n == lo:
                    extra = [("FMh", "L"), ("TMh", "L")]
                if sg == 1 and n == hi - 1:
                    extra = extra + [("FMh", "R"), ("TMh", "R")]
                S.add("sp", lambda e: e.dma_start(out=QA[i][:], in_=FM[0:4, :, c0:c0 + 128].rearrange("f p t -> p f t")),
                      writes=[("QA", i)], dma=True)
                ka0 = c0 + (jl - 1) * 128
                nk = (jh - jl + 1) * 128
                S.add("sp", lambda e: e.dma_start(out=KA[i][:, jl * 128:jl * 128 + nk], in_=FM[4, :, ka0:ka0 + nk]),
                      reads=extra, writes=[("KA", i)], dma=True)
                for j in range(jl, jh + 1):
                    rj = c0 + (j - 1) * 128
                    S.add("sp", lambda e, j=j, rj=rj: e.dma_start(
                        out=VA1[i][:, j, :, 0:64], in_=TMs[rj:rj + 128, VA0:VA0 + 128].rearrange("p (k d) -> p k d", k=2)),
                          reads=extra, writes=[("VA1", i, j)], dma=True)
                S.add("sp", lambda e: e.dma_start(out=QR[i][:], in_=FM[5:9, :, c0:c0 + 128].rearrange("f p t -> p f t")),
                      writes=[("QR", i)], dma=True)
                S.add("sp", lambda e: e.dma_start(out=KR[i][:], in_=FM[9:13, :, c0:c0 + 128].rearrange("f p t -> p f t")),
                      writes=[("KR", i)], dma=True)
                S.add("sp", lambda e: e.dma_start(out=KVL[i][:], in_=TMs[c0:c0 + 128, 0:1024]), writes=[("KVL", i)], dma=True)
                i3 = n % 3
                S.add("sp", lambda e: e.dma_start(out=GG[i3][:], in_=TMs[c0:c0 + 128, GR0:GR0 + 2560]), writes=[("GG", i3)], dma=True)
                S.add("sp", lambda e: e.dma_start(out=RBL[i][:], in_=RB[n, :, :]), reads=[("RB", n)], writes=[("RBL", i)], dma=True)
                r0 = G.xrow(n)
                S.add("sp", lambda e: e.dma_start(out=XR[i3][:], in_=xsrc[r0:r0 + 128, :]), writes=[("XR", i3)], dma=True)
                return jl, jh

            rf = [0]
            ptc = [0]
            spc = [0]

            def phase1a(n, jl, jh):
                i = n % 2
                sg = G.seg(n)
                lo = NCP if sg else 0
                hi = NCH if sg else NCP
                DEN, ATT = DENs[i], ATTs[i]
                for kvh in range(2):
                    pts = []
                    for j in range(jl, jh + 1):
                        sp_ = spc[0] % 2
                        spc[0] += 1
                        pt = ptc[0] % 4
                        ptc[0] += 1
                        pts.append((j, pt))
                        S.add("pe", lambda e, j=j, sp_=sp_, kvh=kvh: e.matmul(
                            SP[sp_][:], lhsT=KA[i][kvh * 64:(kvh + 1) * 64, j * 128:(j + 1) * 128],
                            rhs=QA[i][kvh * 64:(kvh + 1) * 64, :, :].rearrange("p g q -> p (g q)"), start=True, stop=True),
                              reads=[("KA", i), ("QA", i)], writes=[("SP", sp_)])
                        S.add("act", lambda e, sp_=sp_, pt=pt: e.activation(out=PTs[pt][:], in_=SP[sp_][:], func=AF.Exp, scale=0.125),
                              reads=[("SP", sp_)], writes=[("PTs", pt)])
                        if j != 1:
                            mj = 0 if j == 0 else 1
                            edge = sg == 1 and ((j == 0 and n == lo) or (j == 2 and n == hi - 1))
                            msk = MKF if edge else MK
                            mkey = ("MKF", mj) if edge else "MK"
                            S.add("pool", lambda e, pt=pt, msk=msk, mj=mj: e.tensor_tensor(
                                out=PTs[pt][:], in0=PTs[pt][:], in1=msk[:, mj, :], op=ALU.mult),
                                  reads=[("PTs", pt), mkey], writes=[("PTs", pt)])
                    for idx, (j, pt) in enumerate(pts):
                        S.add("pe", lambda e, j=j, pt=pt, kvh=kvh, idx=idx, np_=len(pts): e.matmul(
                            OP[:], lhsT=VA1[i][:, j, kvh, :], rhs=PTs[pt][:], start=(idx == 0), stop=(idx == np_ - 1)),
                              reads=[("VA1", i, j), ("VA1o", i), ("PTs", pt)], writes=["OP"])
                    S.add("dve", lambda e, kvh=kvh: e.tensor_tensor(
                        out=DEN[:].rearrange("p (g q) -> p g q", g=4), in0=OP[64:128, :].rearrange("p (g q) -> p g q", g=4),
                        in1=ESK[0:64, kvh * 4:(kvh + 1) * 4].unsqueeze(2).broadcast_to([64, 4, 128]), op=ALU.add),
                          reads=["OP", "ESK"], writes=[("DEN", i)])
                    S.add("dve", lambda e: e.reciprocal(out=DEN[:], in_=DEN[:]), reads=[("DEN", i)], writes=[("DEN", i)])
                    S.add("dve", lambda e, kvh=kvh: e.tensor_tensor(
                        out=ATT[kvh][:].rearrange("p g q -> p (g q)"), in0=OP[0:64, :], in1=DEN[:], op=ALU.mult),
                          reads=["OP", ("DEN", i)], writes=[("ATT", i, kvh)])
            def phase1r(n):
                i = n % 2
                sg = G.seg(n)
                lo = NCP if sg else 0
                hi = NCH if sg else NCP
                INM, QXF, QXB, RFB, RBB, RBT = INMs[i], QXFs[i], QXBs[i], RFBs[i], RBBs[i], RBTs[i]
                SQ, SSR, TR, T2R, RET = SQs[i], SSRs[i], TRs[i], T2Rs[i], RETs[i]
                if n == lo:
                    if sg == 0:
                        S.add("dve", lambda e, c=rf[0]: e.memset(RFST[c][:], 0.0), writes=[("RFST", rf[0])])
                    else:
                        S.add("dve", lambda e, c=rf[0]: e.tensor_copy(out=RFST[c][:], in_=SINF[:]),
                              reads=["SINF"], writes=[("RFST", rf[0])])
                cur = rf[0]
                for h in range(4):
                    S.add("pe", lambda e, h=h: e.matmul(IP[:, h, :], lhsT=KR[i][:, h, :], rhs=QR[i][:, h, :], start=True, stop=True),
                          reads=[("KR", i), ("QR", i)], writes=["IP"])
                S.add("dve", lambda e: e.tensor_tensor(out=INM[:], in0=IP[:], in1=DTOT[:], op=ALU.mult),
                      reads=["IP"] + DTK, writes=[("INM", i)])
                S.add("pool", lambda e: e.tensor_tensor(out=QXF[:], in0=QR[i][:], in1=XIF[:], op=ALU.mult),
                      reads=[("QR", i)] + XIFK, writes=[("QXF", i)])
                S.add("pool", lambda e: e.tensor_tensor(out=QXB[:], in0=QR[i][:], in1=XIB[:], op=ALU.mult),
                      reads=[("QR", i)] + XIBK, writes=[("QXB", i)])
                S.add("act", lambda e, cur=cur: e.activation(out=RFB[:], in_=RFST[cur][:], func=AF.Copy),
                      reads=[("RFST", cur)], writes=[("RFB", i)])
                if sg == 0:
                    S.add("act", lambda e: e.activation(out=RBB[:], in_=RBL[i][:], func=AF.Copy),
                          reads=[("RBL", i)], writes=[("RBB", i)])
                else:
                    jloc = n - lo
                    S.add("pool", lambda e, jloc=jloc: e.tensor_tensor(out=v4(RBT), in0=v4(SINB), in1=bc4(WBW[:, jloc, :]), op=ALU.mult),
                          reads=["SINB", "WBW"], writes=[("RBT", i)])
                    S.add("pool", lambda e: e.tensor_tensor(out=RBB[:], in0=RBT[:], in1=RBL[i][:], op=ALU.add),
                          reads=[("RBT", i), ("RBL", i)], writes=[("RBB", i)])
                for h in range(4):
                    hs_ = slice(h * 128, (h + 1) * 128)
                    S.add("pe", lambda e, h=h, hs_=hs_: e.matmul(ORp[:, h, :], lhsT=INM[:, h, :], rhs=KVL[i][:, 512 + h * 128:512 + (h + 1) * 128],
                                                               start=True, stop=False),
                          reads=[("INM", i), ("KVL", i)], writes=["ORp"])
                    S.add("pe", lambda e, h=h, hs_=hs_: e.matmul(ORp[:, h, :], lhsT=QXF[:, h, :], rhs=RFB[:, hs_], start=False, stop=False),
                          reads=[("QXF", i), ("RFB", i)], writes=["ORp"])
                    S.add("pe", lambda e, h=h, hs_=hs_: e.matmul(ORp[:, h, :], lhsT=QXB[:, h, :], rhs=RBB[:, hs_], start=False, stop=True),
                          reads=[("QXB", i), ("RBB", i)], writes=["ORp"])
                vz = i
                S.add("pool", lambda e: e.tensor_tensor(out=v4(VZ[vz]), in0=KVL[i][:, 512:1024].rearrange("p (h e) -> p h e", h=4),
                                                        in1=bc4(ZET[:, 0:4]), op=ALU.mult),
                      reads=[("KVL", i), "ZET"], writes=[("VZ", vz)])
                S.add("act", lambda e: e.activation(out=SQ[:], in_=ORp[:].rearrange("p h e -> p (h e)"), func=AF.Square),
                      reads=["ORp"], writes=[("SQ", i)])
                S.add("dve", lambda e: e.tensor_tensor(out=v4(TR), in0=ORp[:], in1=bc4(CST[:, C_ONE:C_ONE + 4]), op=ALU.mult),
                      reads=["ORp", "CST"], writes=[("TR", i)])
                for h in range(4):
                    S.add("pe", lambda e, h=h: e.matmul(IP[:, h, :], lhsT=KVL[i][:, h * 128:(h + 1) * 128],
                                                        rhs=VZ[vz][:, h * 128:(h + 1) * 128], start=True, stop=True),
                          reads=[("KVL", i), ("VZ", vz)], writes=["IP"])
                nxt = 1 - cur
                S.add("pool", lambda e, cur=cur, nxt=nxt: e.tensor_tensor(out=v4(RFST[nxt]), in0=v4(RFST[cur]), in1=bc4(CDC[:, 0:4]), op=ALU.mult),
                      reads=[("RFST", cur), "CDC"], writes=[("RFST", nxt)])
                S.add("dve", lambda e, nxt=nxt: e.tensor_tensor(out=RFST[nxt][:], in0=RFST[nxt][:], in1=IP[:].rearrange("p h e -> p (h e)"), op=ALU.add),
                      reads=["IP", ("RFST", nxt)], writes=[("RFST", nxt)])
                rf[0] = nxt
                S.add("dve", lambda e: e.tensor_reduce(out=SSR[:], in_=v4(SQ), axis=mybir.AxisListType.X, op=ALU.add),
                      reads=[("SQ", i)], writes=[("SSR", i)])
                S.add("dve", lambda e: e.tensor_scalar(out=SSR[:], in0=SSR[:], scalar1=1.0 / 128, scalar2=EPS, op0=ALU.mult, op1=ALU.add),
                      reads=[("SSR", i)], writes=[("SSR", i)])
                S.add("act", lambda e: e.activation(out=SSR[:], in_=SSR[:], func=AF.Sqrt), reads=[("SSR", i)], writes=[("SSR", i)])
                S.add("dve", lambda e: e.reciprocal(out=SSR[:], in_=SSR[:]), reads=[("SSR", i)], writes=[("SSR", i)])
                S.add("pool", lambda e: e.tensor_tensor(out=T2R[:], in0=NG[:], in1=GG[n % 3][:, 0:512], op=ALU.mult),
                      reads=["NG", ("GG", n % 3)], writes=[("T2R", i)])
                S.add("pool", lambda e: e.tensor_tensor(out=v4(TR), in0=v4(TR), in1=bc4(SSR[:, 0:4]), op=ALU.mult),
                      reads=[("TR", i), ("SSR", i)], writes=[("TR", i)])
                S.add("pool", lambda e: e.tensor_tensor(out=RET[:], in0=TR[:], in1=T2R[:], op=ALU.mult),
                      reads=[("TR", i), ("T2R", i)], writes=[("RET", i)])

            def phase2(n):
                i = n % 2
                i3 = n % 3
                ATT, RET, RETT, M1, M2, MER, MT = ATTs[i], RETs[i], RETTs[i], M1s[i], M2s[i], MERs[i], MTs[i]
                for h in range(4):
                    S.add("pe", lambda e, h=h: e.transpose(out=TPB[:, h, :], in_=RET[:, h * 128:(h + 1) * 128], identity=ident[:]),
                          reads=[("RET", i), "ident"], writes=["TPB"])
                S.add("act", lambda e: e.activation(out=RETT[:], in_=TPB[:, 0:4, :], func=AF.Copy), reads=["TPB"], writes=[("RETT", i)])
                for ct in range(2):
                    cs = slice(ct * 512, (ct + 1) * 512)
                    for hh in range(8):
                        kvh, g = hh // 4, hh % 4
                        S.add("pe", lambda e, hh=hh, kvh=kvh, g=g, cs=cs: e.matmul(APs[:], lhsT=ATT[kvh][:, g, :], rhs=WBA[:, hh, cs],
                                                                                 start=(hh == 0), stop=(hh == 7)),
                              reads=[("ATT", i, kvh)] + WBAK[hh], writes=["APs"])
                    for h in range(4):
                        S.add("pe", lambda e, h=h, cs=cs: e.matmul(RPs[:], lhsT=RETT[:, h, :], rhs=WBR[:, h, cs], start=(h == 0), stop=(h == 3)),
                              reads=[("RETT", i)] + WBRK[h], writes=["RPs"])
                    S.add("dve", lambda e, ct=ct: e.tensor_tensor(out=M1[:], in0=APs[:], in1=GG[i3][:, 512 + ct * 512:1024 + ct * 512], op=ALU.mult),
                          reads=["APs", ("GG", i3)], writes=[("M1", i)])
                    S.add("dve", lambda e, ct=ct: e.tensor_tensor(out=M2[:], in0=RPs[:], in1=GG[i3][:, 1536 + ct * 512:2048 + ct * 512], op=ALU.mult),
                          reads=["RPs", ("GG", i3)], writes=[("M2", i)])
                    S.add("pool", lambda e, cs=cs: e.tensor_tensor(out=MER[:, cs], in0=M1[:], in1=M2[:], op=ALU.add),
                          reads=[("M1", i), ("M2", i)], writes=[("MER", i, ct)])
                for k in range(8):
                    S.add("pe", lambda e, k=k: e.transpose(out=TPB[:, k, :], in_=MER[:, k * 128:(k + 1) * 128], identity=ident[:]),
                          reads=[("MER", i, k // 4), "ident"], writes=["TPB"])
                S.add("act", lambda e: e.activation(out=MT[:], in_=TPB[:], func=AF.Copy), reads=["TPB"], writes=[("MT", i)])
                for ct in range(2):
                    cs = slice(ct * 512, (ct + 1) * 512)
                    yp, ypk = (APs, "APs") if ct == 0 else (RPs, "RPs")
                    for k in range(8):
                        S.add("pe", lambda e, k=k, cs=cs, yp=yp: e.matmul(yp[:], lhsT=MT[:, k, :], rhs=WO[:, k, cs], start=(k == 0), stop=(k == 7)),
                              reads=[("MT", i)] + WOK[k], writes=[ypk])
                    S.add("dve", lambda e, cs=cs, yp=yp: e.tensor_tensor(out=XR[i3][:, cs], in0=yp[:], in1=XR[i3][:, cs], op=ALU.add),
                          reads=[ypk, ("XR", i3)], writes=[("XR", i3)])
                r1 = G.x1row(n)
                S.add("sp", lambda e: e.dma_start(out=X1[r1:r1 + 128, :], in_=XR[i3][:]), reads=[("XR", i3)], writes=[("X1", n)], dma=True)

            jj = {}
            jj[0] = loads(0)
            for n in range(NCH):
                if n + 1 < NCH:
                    if n + 1 == NCP:
                        unpack()
                    jj[n + 1] = loads(n + 1)
                streams = []
                for f_ in ((lambda: phase1a(n, *jj[n])), (lambda: phase1r(n)), ((lambda: phase2(n - 1)) if n >= 1 else None)):
                    if f_ is None:
                        continue
                    S.cap = []
                    f_()
                    streams.append(S.cap)
                    S.cap = None
                S.interleave(streams)
            phase2(NCH - 1)
            S.emit(nc, st)

    def pass_D(l, last):
        with ExitStack() as st:
            S = Sched()
            ident, CST = common(st, S)
            WF1 = sb(st, "WF1", [128, 8, 2 * DFF], BF16)
            WF2 = sb(st, "WF2", [128, NFC, D], BF16)
            GB = sb(st, "GBF", [128, D], F32)
            GBL = sb(st, "GBL", [128, D], F32)
            CW = sb(st, "CW", [128, NFC * 3], F32)
            CB = sb(st, "CB", [128, NFC], F32)
            XT = [sb(st, f"XT{i}", [128, D], F32) for i in range(4)]
            XN = [sb(st, f"XN{i}", [128, D], BF16) for i in range(2)]
            SQJ = sb(st, "SQJ", [128, D], BF16)
            SS = sb(st, "SS", [128, 16], F32)
            H2T = [sb(st, f"H2T{i}", [128, 8, 258], BF16) for i in range(2)]
            HALOX = sb(st, "HALOX", [128, D], F32)
            HALOT = sb(st, "HALOT", [128, 8, 128], BF16)
            GU = [sb(st, f"GU{i}", [128, 256], BF16) for i in range(3)]
            C1 = [sb(st, f"C1_{i}", [128, 256], F32) for i in range(2)]
            C2 = [sb(st, f"C2_{i}", [128, 256], F32) for i in range(2)]
            C3 = [sb(st, f"C3_{i}", [128, 256], F32) for i in range(2)]
            GE = [sb(st, f"GE{i}", [128, 256], F32) for i in range(2)]
            AS = [sb(st, f"AS{i}", [128, 258], F32) for i in range(2)]
            US = [sb(st, f"US{i}", [128, 256], F32) for i in range(3)]
            TP = ps(st, "TP", [128, 8, 128], BF16)
            PA = [ps(st, f"PA{i}", [128, 512]) for i in range(2)]
            PU = ps(st, "PU", [128, 512])
            PY = [ps(st, f"PY{i}", [128, 512]) for i in range(4)]

            S.add("sp", lambda e: e.dma_start(out=GB[:], in_=g_ffn[l:l + 1, :].partition_broadcast(128)), writes=["GB"], dma=True)
            S.add("sp", lambda e: e.dma_start(out=GBL[:], in_=g_fin[0:1, :].partition_broadcast(128)), writes=["GBL"], dma=True)
            S.add("sp", lambda e: e.dma_start(out=CW[:], in_=cwd[l, :, :]), writes=["CW"], dma=True)
            S.add("sp", lambda e: e.dma_start(out=CB[:], in_=cbd[l, :, :]), writes=["CB"], dma=True)
            rS0 = G.x1row(NCP)
            rS1 = G.x1row(NCH - 1) + 127
            S.add("sp", lambda e: e.dma_start(out=PKG2[0:1, :], in_=X1[rS0:rS0 + 1, :]), writes=["PKG2"], dma=True)
            S.add("sp", lambda e: e.dma_start(out=PKG2[1:2, :], in_=X1[rS1:rS1 + 1, :]), writes=["PKG2"], dma=True)
            S.add("pool", lambda e: e.collective_compute("AllGather", ALU.bypass, replica_groups=GROUPS,
                                                         ins=[PKG2_t.ap().opt()], outs=[G2_t.ap().opt()]),
                  reads=["PKG2"], writes=["G2"], cc=True)
            WF1K, WF2K = {}, {}
            for k in range(8):
                WF1K[k] = load_w(S, WF1[:, k, :], wf1[l, k * 128:(k + 1) * 128, :], ("WF1", k), maxc=1408)
            for fc in range(NFC):
                WF2K[fc] = load_w(S, WF2[:, fc, :], wf2[l, fc * 128:(fc + 1) * 128, :], ("WF2", fc), maxc=1024)
            S.add("dve", lambda e: e.memset(HALOX[:], 0.0), writes=[("HALOX", q_) for q_ in range(1, 64)])
            nbP = PL // 256
            nbS = SL // 256
            S.add("sp", lambda e: e.dma_start(out=HALOX[1:2, :], in_=X1[0:1, :]), writes=[("HALOX", 1)], dma=True)
            for b in range(1, nbP):
                S.add("sp", lambda e, b=b: e.dma_start(out=HALOX[2 * b:2 * b + 2, :], in_=X1[256 * b - 1:256 * b + 1, :]),
                      writes=[("HALOX", 10 + b)], dma=True)
            S.add("sp", lambda e: e.dma_start(out=HALOX[2 * nbP:2 * nbP + 1, :], in_=X1[PL - 1:PL, :]), writes=[("HALOX", 3)], dma=True)
            hb = 2 * (nbP + 1)
            for b in range(1, nbS):
                r = PL + 1 + 256 * b - 1
                S.add("sp", lambda e, b=b, r=r: e.dma_start(out=HALOX[hb + 2 * b:hb + 2 * b + 2, :], in_=X1[r:r + 2, :]),
                      writes=[("HALOX", 30 + b)], dma=True)
            S.add("sp", lambda e: e.dma_start(out=HALOX[hb + 1:hb + 2, :], in_=X1[PL + 1:PL + 2, :]), writes=[("HALOX", 5)], dma=True)
            S.add("sp", lambda e: e.dma_start(out=HALOX[hb + 2 * nbS:hb + 2 * nbS + 1, :], in_=X1[PL + SL:PL + SL + 1, :]),
                  writes=[("HALOX", 6)], dma=True)
            S.add("sp", lambda e: e.dma_start(out=HALOX[hb:hb + 1, :], in_=G2[1:2, :]), reads=["G2"], writes=[("HALOX", 7)], dma=True)
            S.add("sp", lambda e: e.dma_start(out=HALOX[hb + 2 * nbS + 1:hb + 2 * nbS + 2, :], in_=G2[2:3, :]),
                  reads=["G2"], writes=[("HALOX", 8)], dma=True)
            prep_chunk(S, "H", None, 128, HALOX, [("HALOX", q_) for q_ in range(1, 64)], XN[0], ("XN", 0), SQJ, SS, 15, GB, "GB", TP, "TP", ident,
                       HALOT[:], ["HALOT"], rowscale=CST[:, C_HFL:C_HFL + 1])

            S.mark("D halo done")
            tiles = []
            for b in range(nbP):
                tiles.append((b * 2, 2 * b, 2 * (b + 1) + 1))
            for b in range(nbS):
                tiles.append((NCP + b * 2, hb + 2 * b, hb + 2 * (b + 1) + 1))
            cctr = [0]

            def prep_tile(ti, phase="all"):
                n0, hl, hr = tiles[ti]
                hs = ti % 2
                for c in range(2):
                    n = n0 + c
                    xs = (ti % 2) * 2 + c
                    i = c
                    r1 = G.x1row(n)
                    prep_chunk(S, "D", X1[r1:r1 + 128, :], 128, XT[xs], ("XT", xs), XN[i], ("XN", i), SQJ, SS, xs,
                               GB, "GB", TP, "TP", ident, H2T[hs][:, :, 1 + c * 128:1 + (c + 1) * 128], [("H2T", hs, c)],
                               phase=phase)
                if phase == "pre":
                    return
                S.add("dve", lambda e: e.tensor_copy(out=H2T[hs][:, :, 0:1], in_=HALOT[:, :, hl:hl + 1]),
                      reads=["HALOT"], writes=[("H2T", hs, "l")])
                S.add("dve", lambda e: e.tensor_copy(out=H2T[hs][:, :, 257:258], in_=HALOT[:, :, hr:hr + 1]),
                      reads=["HALOT"], writes=[("H2T", hs, "r")])

            fcc = [0]

            def ffn_out(ti, fc, g):
                for sbk in range(2):
                    for ct in range(2):
                        S.add("pe", lambda e, sbk=sbk, ct=ct, fc=fc, g=g: e.matmul(
                            PY[sbk * 2 + ct][:], lhsT=GU[g][:, sbk * 128:(sbk + 1) * 128], rhs=WF2[:, fc, ct * 512:(ct + 1) * 512],
                            start=(fc == 0), stop=(fc == NFC - 1)),
                              reads=[("GU", g)] + WF2K[fc], writes=[("PY", sbk * 2 + ct)])

            def main_tile(ti, mid, early=None):
                n0, hl, hr = tiles[ti]
                hs = ti % 2
                hk = [("H2T", hs, 0), ("H2T", hs, 1), ("H2T", hs, "l"), ("H2T", hs, "r")]
                def st1(fc):
                    p = fc % 2
                    for k in range(8):
                        S.add("pe", lambda e, k=k: e.matmul(PA[p][:, 0:258], lhsT=WF1[:, k, fc * 128:(fc + 1) * 128],
                                                            rhs=H2T[hs][:, k, :], start=(k == 0), stop=(k == 7)),
                              reads=WF1K[k] + hk, writes=[("PA", p)])
                    for k in range(8):
                        S.add("pe", lambda e, k=k: e.matmul(PU[:, 0:256], lhsT=WF1[:, k, DFF + fc * 128:DFF + (fc + 1) * 128],
                                                            rhs=H2T[hs][:, k, 1:257], start=(k == 0), stop=(k == 7)),
                              reads=WF1K[k] + hk, writes=["PU"])
                    S.add("act", lambda e: e.activation(out=AS[p][:], in_=PA[p][:, 0:258], func=AF.Copy),
                          reads=[("PA", p)], writes=[("AS", p)])
                    S.add("act", lambda e: e.activation(out=US[fc % 3][:], in_=PU[:, 0:256], func=AF.Copy),
                          reads=["PU"], writes=[("US", fc % 3)])

                def st2(fc):
                    p = fc % 2
                    S.add("act", lambda e: e.activation(out=C1[p][:], in_=AS[p][:, 1:257], func=AF.Identity,
                                                        scale=CW[:, fc * 3 + 1:fc * 3 + 2], bias=CB[:, fc:fc + 1]),
                          reads=[("AS", p), "CW", "CB"], writes=[("C1", p)])
                    S.add("dve", lambda e: e.scalar_tensor_tensor(out=C2[p][:], in0=AS[p][:, 0:256], scalar=CW[:, fc * 3:fc * 3 + 1],
                                                                  in1=C1[p][:], op0=ALU.mult, op1=ALU.add),
                          reads=[("AS", p), ("C1", p), "CW"], writes=[("C2", p)])
                    S.add("dve", lambda e: e.scalar_tensor_tensor(out=C3[p][:], in0=AS[p][:, 2:258], scalar=CW[:, fc * 3 + 2:fc * 3 + 3],
                                                                  in1=C2[p][:], op0=ALU.mult, op1=ALU.add),
                          reads=[("AS", p), ("C2", p), "CW"], writes=[("C3", p)])

                def st3(fc):
                    p = fc % 2
                    g = fc % 3
                    S.add("act", lambda e: e.activation(out=GE[p][:], in_=C3[p][:], func=AF.Gelu),
                          reads=[("C3", p)], writes=[("GE", p)])
                    S.add("pool", lambda e: e.tensor_tensor(out=GU[g][:], in0=US[g][:], in1=GE[p][:], op=ALU.mult),
                          reads=[("US", g), ("GE", p)], writes=[("GU", g)])

                for t in range(NFC + 3):
                    if t == 1 and early is not None:
                        early()
                    if t == NFC // 2 and mid is not None:
                        mid()
                    if t < NFC:
                        S.mark(f"D tile {ti} fc {t}")
                        st1(t)
                    if 0 <= t - 3 < NFC:
                        ffn_out(ti, t - 3, (t - 3) % 3)
                    if 0 <= t - 1 < NFC:
                        st2(t - 1)
                    if 0 <= t - 2 < NFC:
                        st3(t - 2)
                S.mark(f"D tile {ti} ffn done")
                for sbk in range(2):
                    xs = (ti % 2) * 2 + sbk
                    n = n0 + sbk
                    for ct in range(2):
                        cs = slice(ct * 512, (ct + 1) * 512)
                        S.add("dve", lambda e, xs=xs, cs=cs, sbk=sbk, ct=ct: e.tensor_tensor(
                            out=XT[xs][:, cs], in0=PY[sbk * 2 + ct][:], in1=XT[xs][:, cs], op=ALU.add),
                              reads=[("PY", sbk * 2 + ct), ("XT", xs)], writes=[("XT", xs)])
                    r0 = G.xrow(n)
                    if not last:
                        S.add("sp", lambda e, xs=xs, r0=r0: e.dma_start(out=X2[r0:r0 + 128, :], in_=XT[xs][:]),
                              reads=[("XT", xs)], writes=[("X2", n)], dma=True)
                    else:
                        sc = SS[:, 8 + xs:9 + xs]
                        sk = ("SSF", xs)
                        S.add("act", lambda e, xs=xs, sc=sc: e.activation(out=SQJ[:], in_=XT[xs][:], func=AF.Square, accum_out=sc),
                              reads=[("XT", xs)], writes=[("SQJ", "D"), sk])
                        S.add("dve", lambda e, sc=sc: e.tensor_scalar(out=sc, in0=sc, scalar1=1.0 / D, scalar2=EPS, op0=ALU.mult, op1=ALU.add),
                              reads=[sk], writes=[sk])
                        S.add("act", lambda e, sc=sc: e.activation(out=sc, in_=sc, func=AF.Sqrt), reads=[sk], writes=[sk])
                        S.add("dve", lambda e, sc=sc: e.reciprocal(out=sc, in_=sc), reads=[sk], writes=[sk])
                        S.add("dve", lambda e, xs=xs, sc=sc: e.scalar_tensor_tensor(out=XT[xs][:], in0=XT[xs][:], scalar=sc, in1=GBL[:],
                                                                                   op0=ALU.mult, op1=ALU.mult),
                              reads=[("XT", xs), sk, "GBL"], writes=[("XT", xs)])
                        S.add("sp", lambda e, xs=xs, r0=r0: e.dma_start(out=y_out[r0:r0 + 128, :], in_=XT[xs][:]),
                              reads=[("XT", xs)], writes=[("Y", n)], dma=True)

            prep_tile(0)
            for ti in range(len(tiles)):
                mid = (lambda t=ti + 1: prep_tile(t, "post")) if ti + 1 < len(tiles) else None
                early = (lambda t=ti + 1: prep_tile(t, "pre")) if ti + 1 < len(tiles) else None
                main_tile(ti, mid, early)
            S.emit(nc, st)

    Sched.GLOBAL.clear()
    Sched.NINST[0] = 0
    gstack = ExitStack()
    Sched.GLOBAL["stack"] = gstack
    plist = []
    for l in range(DEPTH):
        xsrc = x_in if l == 0 else X2
        plist.append(lambda l=l, xsrc=xsrc: pass_A(l, xsrc))
        plist.append(lambda l=l, xsrc=xsrc: pass_B(l, xsrc))
        plist.append(lambda l=l: pass_D(l, l == DEPTH - 1))
    with gstack:
        for f in plist[:npasses]:
            f()
    return nc, G


def _consts(G, core):
    PL, SL, T, NCS = G.PL, G.SL, G.T, G.NCS
    h = core % 2
    pos = np.concatenate([np.arange(PL), h * SL + np.arange(SL)]).astype(np.float32)

    def tabs(half):
        fr = (np.float32(10000.0) ** (-np.arange(half, dtype=np.float32) / np.float32(half))).astype(np.float32)
        ang = (pos[:, None] * fr[None, :]).astype(np.float32)
        return np.cos(ang).astype(np.float32), np.sin(ang).astype(np.float32)

    ca, sa = tabs(32)
    cr, sr = tabs(64)
    fmtab = np.zeros((128, 4, T), np.float32)
    p = np.arange(128)
    da = p % 64
    fmtab[:, 0, :] = ca[:, da % 32].T
    fmtab[:, 1, :] = sa[:, da % 32].T
    fmtab[:, 2, :] = cr[:, p % 64].T
    fmtab[:, 3, :] = sr[:, p % 64].T
    rot = np.zeros((128, 256), np.float32)
    for m in range(128):
        d = m % 64
        if d < 32:
            rot[m + 32, m] = -1.0
        else:
            rot[m - 32, m] = 1.0
        if m < 64:
            rot[m + 64, 128 + m] = -1.0
        else:
            rot[m - 64, 128 + m] = 1.0
    tmtab = np.concatenate([cr, -sr, sr], axis=1).astype(np.float32)
    NCST = 528 + NCS
    cst = np.zeros((128, NCST), np.float32)
    k = np.arange(128)[:, None].astype(np.float32)
    q = np.arange(128)[None, :].astype(np.float32)
    cst[:, 0:128] = np.maximum(q - k, 0)
    cst[:, 128:256] = np.maximum(k - q, 0)
    cst[:, 256:384] = q + 1
    cst[:, 384:512] = 128 - q
    cst[:, 512] = k[:, 0]
    cst[:, 513] = 127 - k[:, 0]
    cst[:, 514] = float(h)
    cst[:, 515] = float(1 - h)
    hfl = np.ones(128, np.float32)
    hb = 2 * (PL // 256 + 1)
    hfl[hb] = float(h)
    hfl[hb + 2 * (SL // 256) + 1] = float(1 - h)
    cst[:, 516] = hfl
    cst[:, 520:524] = 1.0
    cst[:, 528:528 + NCS] = (128.0 * (NCS - 1 - np.arange(NCS)))[None, :]
    mk = np.zeros((128, 2, 512), np.float32)
    kk = np.arange(128)[:, None]
    qq = np.arange(128)[None, :]
    mk[:, 0, :] = np.tile((kk >= qq).astype(np.float32), (1, 4))
    mk[:, 1, :] = np.tile((kk <= qq).astype(np.float32), (1, 4))
    return dict(fmtab=fmtab, tmtab=tmtab, cst=cst, masks=mk.reshape(128, 1024), ident=np.eye(128, dtype=np.float32), rot=rot)


def _perm_w_in(w_in):
    cols = []
    for g in range(4):
        cols += list(range(g * 64, (g + 1) * 64)) + list(range((4 + g) * 64, (5 + g) * 64))
    cols += list(range(512, 640))
    cols += list(range(768, 1280))
    cols += list(range(1280, 1792))
    cols += list(range(1280, 1792)) + list(range(1792, 2304)) + list(range(2304, 2816)) + list(range(2816, 4864)) + list(range(640, 768))
    return np.ascontiguousarray(w_in[:, :, np.array(cols)])


def _shared_inputs(inp):
    f = lambda a: np.ascontiguousarray(np.asarray(a, dtype=np.float32))
    cw = f(inp["conv_w"])
    cwl = np.ascontiguousarray(cw.reshape(DEPTH, 3, NFC, 128).transpose(0, 3, 2, 1).reshape(DEPTH, 128, NFC * 3))
    cb = f(inp["conv_b"])
    cbl = np.ascontiguousarray(cb.reshape(DEPTH, NFC, 128).transpose(0, 2, 1))
    return dict(
        w_in=_perm_w_in(f(inp["w_in"])), wba=f(inp["w_branch_attn"]), wbr=f(inp["w_branch_ret"]), wo=f(inp["w_out"]),
        wf1=f(inp["w_ffn_in"]), wf2=f(inp["w_ffn_out"]), g_mix=f(inp["norm_mix_g"]), g_ffn=f(inp["norm_ffn_g"]),
        g_fin=f(inp["final_norm_g"]).reshape(1, D), g_ret=f(inp["ret_norm_g"]), sink=f(inp["attn_sink"]),
        ldf=f(inp["ret_log_decay_f"]), ldb=f(inp["ret_log_decay_b"]), cw=cwl, cb=cbl)


_CACHE = {}


def run(inp, PL, SL, debug=False, npasses=6, trace=False):
    key = (PL, SL, debug, npasses)
    if key not in _CACHE:
        _CACHE[key] = build(PL, SL, debug, npasses)
    nc, G = _CACHE[key]
    shared = _shared_inputs(inp)
    xp = np.asarray(inp["x_prompt"], dtype=np.float32)
    xs = np.asarray(inp["x_sample"], dtype=np.float32)
    in_maps = []
    for c in range(8):
        m = dict(shared)
        m.update(_consts(G, c))
        h = c % 2
        m["x"] = np.ascontiguousarray(np.concatenate([xp[c], xs[c // 2, h * SL:(h + 1) * SL]], axis=0))
        in_maps.append(m)
    if trace:
        res = run_bass_kernel_spmd(nc, in_maps, core_ids=list(range(8)), trace=True)
    else:
        res = run_bass_kernel_spmd(nc, in_maps, core_ids=list(range(8)))
    yp = np.stack([res.results[c]["y"][:PL] for c in range(8)], axis=0)
    ys = np.stack([np.concatenate([res.results[2 * s]["y"][PL:], res.results[2 * s + 1]["y"][PL:]], axis=0) for s in range(4)], axis=0)
    return (yp.astype(np.float32), ys.astype(np.float32)), res


def kernel(**inputs):
    out, _ = run(inputs, 2048, 4096)
    return out
```

```python
from contextlib import ExitStack
import numpy as np
import concourse.bass as bass
import concourse.mybir as mybir
from concourse.bass_utils import run_bass_kernel_spmd

F32 = mybir.dt.float32
BF16 = mybir.dt.bfloat16
AF = mybir.ActivationFunctionType
ALU = mybir.AluOpType

D = 1024
DEPTH = 2
DFF = 2816
NFC = DFF // 128
INW = 5376
NFM = 13
TMW = 3712
KR0, VR0, GR0, GT0, VA0 = 0, 512, 1024, 1536, 3584
EPS = 1e-6
GROUPS = [[0, 1], [2, 3], [4, 5], [6, 7]]
ENGS = ("pe", "act", "dve", "pool", "sp")


class Sched:
    NDS = 14
    EPOCH = 12000
    UID = [0]

    NINST = [0]
    GLOBAL = {}

    def __init__(self):
        self.ops = []
        self.lastw = {}
        self.readers = {}
        self.inst = Sched.NINST[0]
        Sched.NINST[0] += 1

    PSUM_NAMES = {"PF", "PT", "TP", "PR", "SP", "OP", "IP", "ORp", "TPB", "APs", "RPs", "PA", "PU", "PY"}

    cap = None

    def interleave(self, streams):
        pos = [0] * len(streams)
        total = sum(len(x) for x in streams)
        for _ in range(total):
            best, bf = None, None
            for si, st_ in enumerate(streams):
                if pos[si] < len(st_):
                    f = pos[si] / len(st_)
                    if bf is None or f < bf:
                        best, bf = si, f
            a = streams[best][pos[best]]
            pos[best] += 1
            self.add(*a[0], **a[1])

    def add(self, eng, fn, reads=(), writes=(), dma=False, cc=False):
        if self.cap is not None:
            self.cap.append(((eng, fn), dict(reads=list(reads), writes=list(writes), dma=dma, cc=cc)))
            return None
        def _ps(k):
            return (k if isinstance(k, str) else k[0]) in self.PSUM_NAMES
        ps_reads = [k for k in reads if _ps(k) and k not in writes]
        reads = [k for k in reads if not _ps(k)]
        writes = list(writes)
        import os
        cut = int(os.environ.get("KCUT", "0"))
        if cut and len(self.ops) >= cut and self.inst == int(os.environ.get("KCUTPASS", "0")):
            return None
        idx = len(self.ops)
        deps = {}

        def dep(i, raw):
            if i is None or i == idx:
                return
            deps[i] = deps.get(i, False) or raw

        for k in list(reads) + ps_reads:
            dep(self.lastw.get(k), True)
        for k in ps_reads:
            for r in self.readers.get(k, ()):
                dep(r, False)
        for k in writes:
            dep(self.lastw.get(k), False)
            for r in self.readers.get(k, ()):
                dep(r, False)
        for k in writes + ps_reads:
            self.lastw[k] = idx
            self.readers[k] = []
        for k in reads:
            if k not in writes:
                self.readers.setdefault(k, []).append(idx)
        self.ops.append(dict(eng=eng, fn=fn, deps=sorted(deps), raw=deps, dma=dma, cc=cc, marked=False))
        return idx

    @staticmethod
    def _skip(op, dop, d):
        if dop["dma"] or dop["cc"] or op["dma"] or op["cc"] or dop["eng"] != op["eng"]:
            return False
        if op["eng"] == "pe":
            return True
        return not op["raw"][d]

    def mark(self, name):
        import os
        if os.environ.get("KMARK"):
            print("MARK", self.inst, name, len(self.ops))

    def emit(self, nc, st):
        ops = self.ops
        for op in ops:
            for d in op["deps"]:
                dop = ops[d]
                if dop["dma"] or dop["cc"]:
                    continue
                if self._skip(op, dop, d):
                    continue
                dop["marked"] = True
        last = {}
        for i, op in enumerate(ops):
            if not op["dma"] and not op["cc"]:
                last[op["eng"]] = i
        for e, i in last.items():
            ops[i]["marked"] = True
        gs = self.GLOBAL
        cnt = gs.setdefault("cnt", {e: 0 for e in ENGS})
        dcount = gs.setdefault("dcount", {"sp": 0, "pool": 0})
        duse = gs.setdefault("duse", {"sp": [0] * self.NDS, "pool": [0] * self.NDS})
        SEM = gs.setdefault("SEM", {})
        gst = gs["stack"]
        ncc0 = gs.get("ncc", 0)
        ncc = ncc0
        semkeys = set()
        for op in ops:
            if op["cc"]:
                op["sem"] = ("cc", ncc)
                op["val"] = 1
                ncc += 1
            elif op["dma"]:
                q = op["eng"]
                j = dcount[q] % self.NDS
                dcount[q] += 1
                duse[q][j] += 1
                op["sem"] = ("d", q, j)
                op["val"] = 16 * duse[q][j]
            elif op["marked"]:
                e = op["eng"]
                ep = cnt[e] // self.EPOCH
                cnt[e] += 1
                op["sem"] = ("c", e, ep)
                op["val"] = cnt[e] - ep * self.EPOCH
            else:
                continue
            semkeys.add(op["sem"])
        gs["ncc"] = ncc
        for k in sorted(semkeys, key=str):
            if k not in SEM:
                Sched.UID[0] += 1
                SEM[k] = gst.enter_context(nc.semaphore(f"s{Sched.UID[0]}_" + "_".join(str(x) for x in k)))
        finals = []
        for e, i in last.items():
            finals.append((ops[i]["sem"], ops[i]["val"]))
        for q in ("sp", "pool"):
            for j in range(self.NDS):
                if duse[q][j]:
                    finals.append((("d", q, j), 16 * duse[q][j]))
        for c in range(ncc0, ncc):
            finals.append((("cc", c), 1))

        def run(engname, eng):
            waited = {}

            def wait(sk, val):
                if waited.get(sk, 0) >= val:
                    return
                eng.wait_ge(SEM[sk], val)
                waited[sk] = val

            for op in ops:
                if op["eng"] != engname:
                    continue
                for d in op["deps"]:
                    dop = ops[d]
                    if self._skip(op, dop, d):
                        continue
                    wait(dop["sem"], dop["val"])
                if op["dma"] and op["val"] > 16:
                    wait(op["sem"], op["val"] - 16)
                ins = op["fn"](eng)
                if op["cc"]:
                    ins.then_inc(SEM[op["sem"]])
                elif op["dma"]:
                    ins.then_inc(SEM[op["sem"]], 16)
                elif op["marked"]:
                    ins.then_inc(SEM[op["sem"]], 1)
            for sk, val in finals:
                wait(sk, val)

        block = st.enter_context(nc.Block())

        @block.tensor
        def _(e):
            run("pe", e)

        @block.scalar
        def _(e):
            run("act", e)

        @block.vector
        def _(e):
            run("dve", e)

        @block.gpsimd
        def _(e):
            run("pool", e)

        @block.sync
        def _(e):
            run("sp", e)


class Geo:
    def __init__(self, PL, SL):
        self.PL, self.SL = PL, SL
        self.NCP, self.NCS = PL // 128, SL // 128
        self.NCH = self.NCP + self.NCS
        self.T = PL + SL
        self.TC = self.T + 256
        self.NT = self.T // 512
        self.NDT = self.T // 256
        self.NHB = (PL // 256 + 1) + (SL // 256 + 1)

    def seg(self, n):
        return 0 if n < self.NCP else 1

    def xrow(self, n):
        return n * 128

    def x1row(self, n):
        return n * 128 + (1 if n >= self.NCP else 0)

    def fmcol(self, n):
        return n * 128 + (128 if n >= self.NCP else 0)


def build(PL=2048, SL=4096, debug=False, npasses=6):
    G = Geo(PL, SL)
    T, TC, NCP, NCS, NCH = G.T, G.TC, G.NCP, G.NCS, G.NCH
    nc = bass.Bass("TRN2", target_bir_lowering=False)

    def din(name, shape, dt=F32):
        return nc.dram_tensor(name, list(shape), dt, kind="ExternalInput").ap()

    x_in = din("x", [T, D])
    w_in = din("w_in", [DEPTH, D, INW])
    wba = din("wba", [DEPTH, 512, D])
    wbr = din("wbr", [DEPTH, 512, D])
    wo = din("wo", [DEPTH, D, D])
    wf1 = din("wf1", [DEPTH, D, 2 * DFF])
    wf2 = din("wf2", [DEPTH, DFF, D])
    g_mix = din("g_mix", [DEPTH, D])
    g_ffn = din("g_ffn", [DEPTH, D])
    g_fin = din("g_fin", [1, D])
    g_ret = din("g_ret", [DEPTH, 512])
    sinkd = din("sink", [DEPTH, 8])
    ldf = din("ldf", [DEPTH, 4])
    ldb = din("ldb", [DEPTH, 4])
    cwd = din("cw", [DEPTH, 128, NFC * 3])
    cbd = din("cb", [DEPTH, 128, NFC])
    fmtab = din("fmtab", [128, 4, T])
    tmtab = din("tmtab", [T, 3 * 64])
    NCST = 528 + NCS
    cst = din("cst", [128, NCST])
    masks = din("masks", [128, 2 * 512])
    identd = din("ident", [128, 128])
    rotd = din("rot", [128, 256])

    okind = "ExternalOutput"
    y_out = nc.dram_tensor("y", [T, D], F32, kind=okind).ap()

    def scratch(name, shape, dt):
        if debug:
            return nc.dram_tensor(name, list(shape), dt, kind=okind)
        return nc.dram_tensor(name, list(shape), dt)

    FM_t = scratch("FM", [NFM, 128, TC], BF16)
    TM_t = scratch("TMs", [TC, TMW], BF16)
    RB_t = scratch("RB", [NCH, 128, 512], F32)
    X1_t = scratch("X1", [T + 2, D], F32)
    X2_t = scratch("X2", [T, D], F32)
    FM, TMs, RB, X1, X2 = FM_t.ap(), TM_t.ap(), RB_t.ap(), X1_t.ap(), X2_t.ap()
    PKG1_t = nc.dram_tensor("PKG1", [128, 1024], F32)
    G1_t = nc.dram_tensor("G1", [256, 1024], F32)
    PKGK_t = nc.dram_tensor("PKGK", [128, 512], BF16)
    GK_t = nc.dram_tensor("GK", [256, 512], BF16)
    PKG2_t = nc.dram_tensor("PKG2", [2, D], F32)
    G2_t = nc.dram_tensor("G2", [4, D], F32)
    PKG1, G1, PKGK, GK, PKG2, G2 = (t.ap() for t in (PKG1_t, G1_t, PKGK_t, GK_t, PKG2_t, G2_t))

    C_RELQK, C_RELKQ, C_QP1, C_Q128M = 0, 128, 256, 384
    C_IDXK, C_I127, C_FL, C_FR, C_HFL, C_ONE, C_NW = 512, 513, 514, 515, 516, 520, 528

    uid = [0]

    def sb(st, name, shape, dt):
        uid[0] += 1
        return st.enter_context(nc.sbuf_tensor(f"{name}_u{uid[0]}", list(shape), dt))

    def ps(st, name, shape, dt=F32):
        uid[0] += 1
        return st.enter_context(nc.psum_tensor(f"{name}_u{uid[0]}", list(shape), dt))

    def common(st, S):
        identf = sb(st, "identf", [128, 128], F32)
        ident = sb(st, "ident", [128, 128], BF16)
        CST = sb(st, "CST", [128, NCST], F32)
        S.add("sp", lambda e: e.dma_start(out=identf[:], in_=identd[:, :]), writes=["identf"], dma=True)
        S.add("sp", lambda e: e.dma_start(out=CST[:], in_=cst[:, :]), writes=["CST"], dma=True)
        S.add("dve", lambda e: e.tensor_copy(out=ident[:], in_=identf[:]), reads=["identf"], writes=["ident"])
        return ident, CST

    def load_w(S, dst, src, key, maxc=2048):
        ncols = src.shape[-1]
        c0 = 0
        keys = []
        while c0 < ncols:
            c1 = min(ncols, c0 + maxc)
            S.add("pool", lambda e, a=dst[:, c0:c1], b=src[:, c0:c1]: e.dma_start(out=a, in_=b),
                  writes=[(key, c0)], dma=True)
            keys.append((key, c0))
            c0 = c1
        return keys

    def prep_chunk(S, tag, src_ap, nrows, XTs, xkey, XNs, xnkey, SQJ, SS, ss_col, GB, gkey, TP, tpkey,
                   ident, dst_ap, dkeys, rowscale=None, phase="all"):
        xkeys = xkey if isinstance(xkey, list) else [xkey]
        if phase == "post":
            for k in range(8):
                S.add("pe", lambda e, k=k: e.transpose(out=TP[:, k, 0:nrows], in_=XNs[0:nrows, k * 128:(k + 1) * 128],
                                                       identity=ident[0:nrows, 0:nrows]),
                      reads=[xnkey, "ident"], writes=[tpkey])
            S.add("act", lambda e: e.activation(out=dst_ap, in_=TP[:, :, 0:nrows], func=AF.Copy),
                  reads=[tpkey], writes=dkeys)
            return
        if src_ap is not None:
            S.add("sp", lambda e: e.dma_start(out=XTs[0:nrows, :], in_=src_ap), writes=xkeys, dma=True)
        if rowscale is not None:
            S.add("dve", lambda e: e.tensor_scalar(out=XTs[:, :], in0=XTs[:, :], scalar1=rowscale, scalar2=None,
                                                   op0=ALU.mult), reads=["CST"], writes=xkeys)
        sc = SS[:, ss_col:ss_col + 1]
        sk = ("SS", tag, ss_col)
        S.add("act", lambda e: e.activation(out=SQJ[0:nrows, :], in_=XTs[0:nrows, :], func=AF.Square,
                                            accum_out=sc[0:nrows, :]),
              reads=xkeys, writes=[("SQJ", tag), sk])
        S.add("dve", lambda e: e.tensor_scalar(out=sc[0:nrows, :], in0=sc[0:nrows, :], scalar1=1.0 / D, scalar2=EPS,
                                               op0=ALU.mult, op1=ALU.add), reads=[sk], writes=[sk])
        S.add("act", lambda e: e.activation(out=sc[0:nrows, :], in_=sc[0:nrows, :], func=AF.Sqrt), reads=[sk], writes=[sk])
        S.add("dve", lambda e: e.reciprocal(out=sc[0:nrows, :], in_=sc[0:nrows, :]), reads=[sk], writes=[sk])
        S.add("dve", lambda e: e.scalar_tensor_tensor(out=XNs[0:nrows, :], in0=XTs[0:nrows, :], scalar=sc[0:nrows, :],
                                                      in1=GB[0:nrows, :], op0=ALU.mult, op1=ALU.mult),
              reads=xkeys + [sk, gkey], writes=[xnkey])
        if phase == "pre":
            return
        for k in range(8):
            S.add("pe", lambda e, k=k: e.transpose(out=TP[:, k, 0:nrows], in_=XNs[0:nrows, k * 128:(k + 1) * 128],
                                                   identity=ident[0:nrows, 0:nrows]),
                  reads=[xnkey, "ident"], writes=[tpkey])
        S.add("act", lambda e: e.activation(out=dst_ap, in_=TP[:, :, 0:nrows], func=AF.Copy),
              reads=[tpkey], writes=dkeys)

    def pass_A(l, xsrc):
        with ExitStack() as st:
            S = Sched()
            ident, CST = common(st, S)
            WIN = sb(st, "WIN", [128, 8, INW], BF16)
            GB = sb(st, "GB", [128, D], F32)
            XT = [sb(st, f"XT{i}", [128, D], F32) for i in range(4)]
            XN = [sb(st, f"XN{i}", [128, D], BF16) for i in range(4)]
            SQJ = sb(st, "SQJ", [128, D], BF16)
            SS = sb(st, "SS", [128, 8], F32)
            HT = [sb(st, f"HT{i}", [128, 8, 512], BF16) for i in range(2)]
            TAB = [sb(st, f"TAB{i}", [128, 4, 512], F32) for i in range(2)]
            TMT = [sb(st, f"TMT{i}", [128, 3, 64], F32) for i in range(2)]
            T1 = [sb(st, f"T1_{i}", [128, 512], F32) for i in range(2)]
            T2 = [sb(st, f"T2_{i}", [128, 512], F32) for i in range(2)]
            T1T = [sb(st, f"T1T_{i}", [128, 512], F32) for i in range(2)]
            T2T = [sb(st, f"T2T_{i}", [128, 512], F32) for i in range(2)]
            FMO = [sb(st, f"FMO{i}", [128, 512], BF16) for i in range(2)]
            TMO = [sb(st, f"TMO{i}", [128, TMW], BF16) for i in range(2)]
            TP = [ps(st, f"TP{i}", [128, 8, 128], BF16) for i in range(2)]
            PF = [ps(st, f"PF{i}", [128, 512]) for i in range(2)]
            PT = [ps(st, f"PT{i}", [128, 512]) for i in range(3)]
            PR = ps(st, "PR", [128, 512])
            XB = [sb(st, f"XB{i}", [128, 512], BF16) for i in range(2)]
            ROTF = sb(st, "ROTF", [128, 256], F32)
            ROT = sb(st, "ROT", [128, 256], BF16)
            S.add("sp", lambda e: e.dma_start(out=ROTF[:], in_=rotd[:, :]), writes=["ROTF"], dma=True)
            S.add("dve", lambda e: e.tensor_copy(out=ROT[:], in_=ROTF[:]), reads=["ROTF"], writes=["ROT"])

            S.add("sp", lambda e: e.dma_start(out=GB[:], in_=g_mix[l:l + 1, :].partition_broadcast(128)),
                  writes=["GB"], dma=True)
            WINK = {}
            for k in range(8):
                WINK[k] = load_w(S, WIN[:, k, :], w_in[l, k * 128:(k + 1) * 128, :], ("WIN", k), maxc=1792)

            cctr = [0]
            fctr = [0]
            tctr = [0]

            def prep_tile(tt, phase="all", only=None):
                hs = tt % 2
                for c in range(4):
                    if only is not None and c != only:
                        continue
                    n = tt * 4 + c
                    i = c
                    r0 = G.xrow(n)
                    prep_chunk(S, "A", xsrc[r0:r0 + 128, :], 128, XT[i], ("XT", i), XN[i], ("XN", i), SQJ, SS, i,
                               GB, "GB", TP[i % 2], ("TP", i % 2), ident, HT[hs][:, :, c * 128:(c + 1) * 128],
                               [("HT", hs, c)], phase=phase)

            def main_tile(tt, mid=None):
                hs = tt % 2
                hkeys = [("HT", hs, c) for c in range(4)]
                n0 = tt * 4
                c0 = G.fmcol(n0)
                r0 = G.xrow(n0)
                S.add("sp", lambda e: e.dma_start(out=TAB[hs][:], in_=fmtab[:, :, r0:r0 + 512]),
                      writes=[("TAB", hs)], dma=True)
                fm_pend = []
                for f in range(NFM):
                    pf = fctr[0] % 2
                    p = fctr[0] % 2
                    fctr[0] += 1
                    for k in range(8):
                        S.add("pe", lambda e, k=k, f=f, pf=pf: e.matmul(PF[pf][:], lhsT=WIN[:, k, f * 128:(f + 1) * 128],
                                                                       rhs=HT[hs][:, k, :], start=(k == 0), stop=(k == 7)),
                              reads=WINK[k] + hkeys, writes=[("PF", pf)])
                    att = f < 5
                    ci, si = (0, 1) if att else (2, 3)
                    ro = 0 if att else 128
                    S.add("act", lambda e, p=p, pf=pf: e.activation(out=XB[p][:], in_=PF[pf][:], func=AF.Copy),
                          reads=[("PF", pf)], writes=[("XB", p)])
                    S.add("dve", lambda e, p=p, ci=ci: e.tensor_tensor(out=T1[p][:], in0=XB[p][:], in1=TAB[hs][:, ci, :],
                                                                      op=ALU.mult),
                          reads=[("XB", p), ("TAB", hs)], writes=[("T1", p)])

                    def stage2(p=p, ro=ro, si=si, f=f):
                        S.add("pe", lambda e: e.matmul(PR[:], lhsT=ROT[:, ro:ro + 128], rhs=XB[p][:], start=True, stop=True),
                              reads=[("XB", p), "ROT"], writes=["PR"])
                        S.add("dve", lambda e: e.tensor_tensor(out=T2[p][:], in0=PR[:], in1=TAB[hs][:, si, :], op=ALU.mult),
                              reads=["PR", ("TAB", hs)], writes=[("T2", p)])
                        S.add("pool", lambda e: e.tensor_tensor(out=FMO[p][:], in0=T1[p][:], in1=T2[p][:], op=ALU.add),
                              reads=[("T1", p), ("T2", p)], writes=[("FMO", p)])
                        S.add("sp", lambda e: e.dma_start(out=FM[f, :, c0:c0 + 512], in_=FMO[p][:]),
                              reads=[("FMO", p)], writes=[("FMd", f, tt)], dma=True)
                    if fm_pend:
                        fm_pend.pop(0)()
                    fm_pend.append(stage2)
                while fm_pend:
                    fm_pend.pop(0)()
                S.mark(f"A tile {tt} FM done")
                for c in range(4):
                    S.mark(f"A tile {tt} TM chunk {c}")
                    n = n0 + c
                    o_ = n % 2
                    rr = G.xrow(n)
                    S.add("sp", lambda e, o_=o_, rr=rr: e.dma_start(
                        out=TMT[o_][:], in_=tmtab[rr:rr + 128, :].rearrange("p (a b) -> p a b", a=3)),
                          writes=[("TMT", o_)], dma=True)
                    for ct in range(8):
                        p = tctr[0] % 3
                        tctr[0] += 1
                        w = 128 if ct == 7 else 512
                        wc0 = 1664 + ct * 512
                        for k in range(8):
                            S.add("pe", lambda e, k=k, p=p, w=w, wc0=wc0, c=c: e.matmul(
                                PT[p][:, 0:w], lhsT=HT[hs][:, k, c * 128:(c + 1) * 128], rhs=WIN[:, k, wc0:wc0 + w],
                                start=(k == 0), stop=(k == 7)),
                                  reads=WINK[k] + [("HT", hs, c)], writes=[("PT", p)])
                        okey = ("TMO", o_, ct)
                        if ct == 0:
                            q = p % 2
                            pv = PT[p][:].rearrange("p (h t d) -> p h t d", h=4, t=2)
                            t1v = T1T[q][:].rearrange("p (g d) -> p g d", d=64)
                            t2v = T2T[q][:].rearrange("p (h t d) -> p h t d", h=4, t=2)
                            S.add("dve", lambda e, p=p, o_=o_, t1v=t1v: e.tensor_tensor(
                                out=t1v, in0=PT[p][:].rearrange("p (g d) -> p g d", d=64),
                                in1=TMT[o_][:, 0:1, :].broadcast_to([128, 8, 64]), op=ALU.mult),
                                  reads=[("PT", p), ("TMT", o_)], writes=[("T1T", q)])
                            S.add("dve", lambda e, pv=pv, t2v=t2v, o_=o_: e.tensor_tensor(
                                out=t2v[:, :, 0, :], in0=pv[:, :, 1, :],
                                in1=TMT[o_][:, 1:2, :].broadcast_to([128, 4, 64]), op=ALU.mult),
                                  reads=[("PT", p), ("TMT", o_)], writes=[("T2T", q, "a")])
                            S.add("dve", lambda e, pv=pv, t2v=t2v, o_=o_: e.tensor_tensor(
                                out=t2v[:, :, 1, :], in0=pv[:, :, 0, :],
                                in1=TMT[o_][:, 2:3, :].broadcast_to([128, 4, 64]), op=ALU.mult),
                                  reads=[("PT", p), ("TMT", o_)], writes=[("T2T", q, "b")])
                            S.add("pool", lambda e, q=q, o_=o_: e.tensor_tensor(
                                out=TMO[o_][:, KR0:KR0 + 512], in0=T1T[q][:], in1=T2T[q][:], op=ALU.add),
                                  reads=[("T1T", q), ("T2T", q, "a"), ("T2T", q, "b")], writes=[okey])
                        else:
                            func = AF.Copy if ct in (1, 7) else (AF.Silu if ct == 2 else AF.Sigmoid)
                            dc0 = {1: VR0, 2: GR0, 7: VA0}.get(ct, GT0 + (ct - 3) * 512)
                            S.add("act", lambda e, p=p, w=w, dc0=dc0, func=func, o_=o_: e.activation(
                                out=TMO[o_][:, dc0:dc0 + w], in_=PT[p][:, 0:w], func=func),
                                  reads=[("PT", p)], writes=[okey])
                    rc = G.fmcol(n)
                    S.add("sp", lambda e, o_=o_, rc=rc: e.dma_start(out=TMs[rc:rc + 128, :], in_=TMO[o_][:]),
                          reads=[("TMO", o_, ct) for ct in range(8)], writes=[("TMd", n)], dma=True)
                    if mid is not None:
                        mid(c)

            prep_tile(0)
            for tt in range(G.NT):
                mid = None
                if tt + 1 < G.NT:
                    prep_tile(tt + 1, "pre")
                    mid = lambda c, t=tt + 1: prep_tile(t, "post", only=c)
                main_tile(tt, mid)
            S.emit(nc, st)

    def pass_B(l, xsrc):
        with ExitStack() as st:
            S = Sched()
            ident, CST = common(st, S)
            WBA = sb(st, "WBA", [64, 8, D], BF16)
            WBR = sb(st, "WBR", [128, 4, D], BF16)
            WO = sb(st, "WO", [128, 8, D], BF16)
            MK = sb(st, "MK", [128, 2, 512], BF16)
            MKF = sb(st, "MKF", [128, 2, 512], BF16)
            LFB = sb(st, "LFB", [128, 8], F32)
            ZET = sb(st, "ZET", [128, 8], F32)
            CDC = sb(st, "CDC", [128, 8], F32)
            ESK = sb(st, "ESK", [128, 8], F32)
            DTOT = sb(st, "DTOT", [128, 4, 128], BF16)
            DTMP = sb(st, "DTMP", [128, 128], F32)
            DTMP2 = sb(st, "DTMP2", [128, 128], F32)
            XIF = sb(st, "XIF", [128, 4, 128], F32)
            XIB = sb(st, "XIB", [128, 4, 128], F32)
            WBW = sb(st, "WBW", [128, NCS, 4], F32)
            NG = sb(st, "NG", [128, 512], F32)
            SINF = sb(st, "SINF", [128, 512], F32)
            SINB = sb(st, "SINB", [128, 512], F32)
            RBST = [sb(st, f"RBST{i}", [128, 512], F32) for i in range(2)]
            RFST = [sb(st, f"RFST{i}", [128, 512], F32) for i in range(2)]
            KVL = [sb(st, f"KVL{i}", [128, 1024], BF16) for i in range(2)]
            VZ = [sb(st, f"VZ{i}", [128, 512], BF16) for i in range(2)]
            QA = [sb(st, f"QA{i}", [128, 4, 128], BF16) for i in range(2)]
            KA = [sb(st, f"KA{i}", [128, 384], BF16) for i in range(2)]
            VA1 = [sb(st, f"VA1_{i}", [128, 3, 2, 128], BF16) for i in range(2)]
            QR = [sb(st, f"QR{i}", [128, 4, 128], BF16) for i in range(2)]
            KR = [sb(st, f"KR{i}", [128, 4, 128], BF16) for i in range(2)]
            GG = [sb(st, f"GG{i}", [128, 2560], BF16) for i in range(3)]
            RBL = [sb(st, f"RBL{i}", [128, 512], F32) for i in range(2)]
            XR = [sb(st, f"XR{i}", [128, D], F32) for i in range(3)]
            PTs = [sb(st, f"PTs{i}", [128, 512], BF16) for i in range(4)]
            def two(name, shape, dt):
                return [sb(st, f"{name}{i}", shape, dt) for i in range(2)]
            DENs = two("DEN", [64, 512], F32)
            ATTs = [[sb(st, f"ATT{i}_{k}", [64, 4, 128], BF16) for k in range(2)] for i in range(2)]
            INMs = two("INM", [128, 4, 128], BF16)
            QXFs = two("QXF", [128, 4, 128], BF16)
            QXBs = two("QXB", [128, 4, 128], BF16)
            RFBs = two("RFB", [128, 512], BF16)
            RBBs = two("RBB", [128, 512], BF16)
            RBTs = two("RBT", [128, 512], F32)
            SQs = two("SQ", [128, 512], F32)
            SSRs = two("SSR", [128, 4], F32)
            TRs = two("TR", [128, 512], F32)
            T2Rs = two("T2R", [128, 512], F32)
            RETs = two("RET", [128, 512], BF16)
            RETTs = two("RETT", [128, 4, 128], BF16)
            M1s = two("M1", [128, 512], F32)
            M2s = two("M2", [128, 512], F32)
            MERs = two("MER", [128, D], BF16)
            MTs = two("MT", [128, 8, 128], BF16)
            SP = [ps(st, f"SP{i}", [128, 512]) for i in range(2)]
            OP = ps(st, "OP", [128, 512])
            IP = ps(st, "IP", [128, 4, 128])
            ORp = ps(st, "ORp", [128, 4, 128])
            TPB = ps(st, "TPB", [128, 8, 128], BF16)
            APs = ps(st, "APs", [128, 512])
            RPs = ps(st, "RPs", [128, 512])

            WBAK, WBRK, WOK = {}, {}, {}
            for h in range(8):
                WBAK[h] = load_w(S, WBA[:, h, :], wba[l, h * 64:(h + 1) * 64, :], ("WBA", h), maxc=1024)
            for h in range(4):
                WBRK[h] = load_w(S, WBR[:, h, :], wbr[l, h * 128:(h + 1) * 128, :], ("WBR", h), maxc=1024)
            for k in range(8):
                WOK[k] = load_w(S, WO[:, k, :], wo[l, k * 128:(k + 1) * 128, :], ("WO", k), maxc=1024)
            S.add("pool", lambda e: e.dma_start(out=MK[:].rearrange("p a b -> p (a b)"), in_=masks[:, :]),
                  writes=["MK"], dma=True)
            S.add("sp", lambda e: e.dma_start(out=LFB[:, 0:4], in_=ldf[l:l + 1, :].partition_broadcast(128)),
                  writes=["LFB"], dma=True)
            S.add("sp", lambda e: e.dma_start(out=LFB[:, 4:8], in_=ldb[l:l + 1, :].partition_broadcast(128)),
                  writes=["LFB"], dma=True)
            S.add("sp", lambda e: e.dma_start(out=ESK[:], in_=sinkd[l:l + 1, :].partition_broadcast(128)),
                  writes=["ESK"], dma=True)
            S.add("sp", lambda e: e.dma_start(out=NG[:], in_=g_ret[l:l + 1, :].partition_broadcast(128)),
                  writes=["NG"], dma=True)
            S.add("act", lambda e: e.activation(out=ESK[:], in_=ESK[:], func=AF.Exp), reads=["ESK"], writes=["ESK"])
            for j, cf in ((0, C_FL), (1, C_FR)):
                S.add("dve", lambda e, j=j, cf=cf: e.tensor_scalar(out=MKF[:, j, :], in0=MK[:, j, :],
                                                                  scalar1=CST[:, cf:cf + 1], scalar2=None, op0=ALU.mult),
                      reads=["MK", "CST"], writes=[("MKF", j)])
            KSC = 128.0 ** -0.5
            S.add("act", lambda e: e.activation(out=ZET[:, 0:4], in_=LFB[:, 0:4], func=AF.Exp,
                                                scale=CST[:, C_I127:C_I127 + 1]), reads=["LFB", "CST"], writes=["ZETa"])
            S.add("act", lambda e: e.activation(out=ZET[:, 4:8], in_=LFB[:, 4:8], func=AF.Exp,
                                                scale=CST[:, C_IDXK:C_IDXK + 1]), reads=["LFB", "CST"], writes=["ZETb"])
            S.add("dve", lambda e: e.tensor_scalar(out=ZET[:], in0=ZET[:], scalar1=KSC, scalar2=None, op0=ALU.mult),
                  reads=["ZETa", "ZETb"], writes=["ZET"])
            S.add("act", lambda e: e.activation(out=CDC[:], in_=LFB[:], func=AF.Exp, scale=128.0),
                  reads=["LFB"], writes=["CDC"])
            for h in range(4):
                S.add("dve", lambda e, h=h: e.tensor_scalar(out=DTMP[:], in0=CST[:, C_RELQK:C_RELQK + 128],
                                                           scalar1=LFB[:, h:h + 1], scalar2=None, op0=ALU.mult),
                      reads=["CST", "LFB"], writes=["DTMP"])
                S.add("dve", lambda e, h=h: e.scalar_tensor_tensor(out=DTMP2[:], in0=CST[:, C_RELKQ:C_RELKQ + 128],
                                                                  scalar=LFB[:, 4 + h:5 + h], in1=DTMP[:],
                                                                  op0=ALU.mult, op1=ALU.add),
                      reads=["CST", "LFB", "DTMP"], writes=["DTMP2"])
                S.add("act", lambda e: e.activation(out=DTMP[:], in_=DTMP2[:], func=AF.Exp),
                      reads=["DTMP2"], writes=["DTMP"])
                S.add("dve", lambda e, h=h: e.tensor_scalar(out=DTOT[:, h, :], in0=DTMP[:], scalar1=KSC, scalar2=None,
                                                           op0=ALU.mult), reads=["DTMP"], writes=[("DTOT", h)])
                S.add("act", lambda e, h=h: e.activation(out=XIF[:, h, :], in_=CST[:, C_QP1:C_QP1 + 128], func=AF.Exp,
                                                        scale=LFB[:, h:h + 1]), reads=["CST", "LFB"], writes=[("XIF", h)])
                S.add("act", lambda e, h=h: e.activation(out=XIB[:, h, :], in_=CST[:, C_Q128M:C_Q128M + 128], func=AF.Exp,
                                                        scale=LFB[:, 4 + h:5 + h]), reads=["CST", "LFB"], writes=[("XIB", h)])
            DTK = [("DTOT", h) for h in range(4)]
            XIFK = [("XIF", h) for h in range(4)]
            XIBK = [("XIB", h) for h in range(4)]
            S.add("dve", lambda e: e.tensor_tensor(out=WBW[:], in0=CST[:, C_NW:C_NW + NCS].unsqueeze(2).broadcast_to([128, NCS, 4]),
                                                   in1=LFB[:, 4:8].unsqueeze(1).broadcast_to([128, NCS, 4]), op=ALU.mult),
                  reads=["CST", "LFB"], writes=["WBW"])
            S.add("act", lambda e: e.activation(out=WBW[:], in_=WBW[:], func=AF.Exp), reads=["WBW"], writes=["WBW"])
            for i in range(2):
                S.add("dve", lambda e, i=i: e.memset(VA1[i][:, :, :, 64:128], 1.0), writes=[("VA1o", i)])

            def v4(t):
                return t[:].rearrange("p (h e) -> p h e", h=4)

            def bc4(col_ap):
                return col_ap.unsqueeze(2).broadcast_to([128, 4, 128])

            ldc = [0]

            RBSTP = [sb(st, f"RBSTP{i}", [128, 512], F32) for i in range(2)]
            KVS = [[sb(st, f"KVS{a_}_{i}", [128, 1024], BF16) for i in range(2)] for a_ in range(3)]
            VZS = [[sb(st, f"VZS{a_}_{i}", [128, 512], BF16) for i in range(2)] for a_ in range(3)]

            def scan(chunks, direction, ST, skey, store_rb, sid, PSB, pkey):
                KVL = KVS[sid]
                VZ = VZS[sid]
                ldc = [0]
                IP = PSB
                cur = 0
                first = True
                for n in chunks:
                    if first:
                        S.add("dve", lambda e, cur=cur: e.memset(ST[cur][:], 0.0), writes=[(skey, cur)])
                        first = False
                    if store_rb:
                        S.add("sp", lambda e, n=n, cur=cur: e.dma_start(out=RB[n, :, :], in_=ST[cur][:]),
                              reads=[(skey, cur)], writes=[("RB", n)], dma=True)
                    i = ldc[0] % 2
                    ldc[0] += 1
                    rc = G.fmcol(n)
                    S.add("sp", lambda e, i=i, rc=rc: e.dma_start(out=KVL[i][:], in_=TMs[rc:rc + 128, 0:1024]),
                          writes=[("KVS", sid, i)], dma=True)
                    zc = 0 if direction == "f" else 4
                    S.add("dve", lambda e, i=i, zc=zc: e.tensor_tensor(
                        out=v4(VZ[i]), in0=KVL[i][:, 512:1024].rearrange("p (h e) -> p h e", h=4),
                        in1=bc4(ZET[:, zc:zc + 4]), op=ALU.mult),
                          reads=[("KVS", sid, i), "ZET"], writes=[("VZS", sid, i)])
                    for h in range(4):
                        S.add("pe", lambda e, i=i, h=h: e.matmul(IP[:, h, :], lhsT=KVL[i][:, h * 128:(h + 1) * 128],
                                                                rhs=VZ[i][:, h * 128:(h + 1) * 128], start=True, stop=True),
                              reads=[("KVS", sid, i), ("VZS", sid, i)], writes=[pkey])
                    nxt = 1 - cur
                    S.add("dve", lambda e, cur=cur, nxt=nxt, zc=zc: e.tensor_tensor(
                        out=v4(ST[nxt]), in0=v4(ST[cur]), in1=bc4(CDC[:, zc:zc + 4]), op=ALU.mult),
                          reads=[(skey, cur), "CDC"], writes=[(skey, nxt)])
                    S.add("dve", lambda e, nxt=nxt: e.tensor_tensor(
                        out=ST[nxt][:], in0=ST[nxt][:], in1=IP[:].rearrange("p h e -> p (h e)"), op=ALU.add),
                          reads=[pkey, (skey, nxt)], writes=[(skey, nxt)])
                    cur = nxt
                return cur

            OPv = OP[:].rearrange("p (h e) -> p h e", h=4)
            streams = []
            S.cap = []
            cur = scan(list(range(NCH - 1, NCP - 1, -1)), "b", RBST, "RBST", True, 0, IP, "IP")
            S.add("sp", lambda e, cur=cur: e.dma_start(out=PKG1[:, 512:1024], in_=RBST[cur][:]),
                  reads=[("RBST", cur)], writes=["PKG1b"], dma=True)
            streams.append(S.cap)
            S.cap = []
            cur = scan(list(range(NCP, NCH)), "f", RFST, "RFST", False, 1, ORp, "ORp")
            S.add("sp", lambda e, cur=cur: e.dma_start(out=PKG1[:, 0:512], in_=RFST[cur][:]),
                  reads=[("RFST", cur)], writes=["PKG1a"], dma=True)
            streams.append(S.cap)
            S.cap = []
            scan(list(range(NCP - 1, -1, -1)), "b", RBSTP, "RBSTP", True, 2, OPv, "OP")
            streams.append(S.cap)
            S.cap = None
            S.interleave(streams)
            cS0 = G.fmcol(NCP)
            cS1 = G.fmcol(NCH - 1)
            S.add("sp", lambda e: e.dma_start(out=PKGK[:, 0:128], in_=FM[4, :, cS0:cS0 + 128]), writes=["PKGK"], dma=True)
            S.add("sp", lambda e: e.dma_start(out=PKGK[:, 128:256], in_=FM[4, :, cS1:cS1 + 128]), writes=["PKGK"], dma=True)
            S.add("sp", lambda e: e.dma_start(out=PKGK[:, 256:384], in_=TMs[cS0:cS0 + 128, VA0:VA0 + 128]), writes=["PKGK"], dma=True)
            S.add("sp", lambda e: e.dma_start(out=PKGK[:, 384:512], in_=TMs[cS1:cS1 + 128, VA0:VA0 + 128]), writes=["PKGK"], dma=True)
            S.add("pool", lambda e: e.collective_compute("AllGather", ALU.bypass, replica_groups=GROUPS,
                                                         ins=[PKG1_t.ap().opt()], outs=[G1_t.ap().opt()]),
                  reads=["PKG1a", "PKG1b"], writes=["G1"], cc=True)
            S.add("pool", lambda e: e.collective_compute("AllGather", ALU.bypass, replica_groups=GROUPS,
                                                         ins=[PKGK_t.ap().opt()], outs=[GK_t.ap().opt()]),
                  reads=["PKGK"], writes=["GK"], cc=True)

            def unpack():
                cL = PL
                cR = PL + 128 + SL
                S.add("sp", lambda e: e.dma_start(out=FM[4, :, cL:cL + 128], in_=GK[0:128, 128:256]),
                      reads=["GK"], writes=[("FMh", "L")], dma=True)
                S.add("sp", lambda e: e.dma_start(out=FM[4, :, cR:cR + 128], in_=GK[128:256, 0:128]),
                      reads=["GK"], writes=[("FMh", "R")], dma=True)
                S.add("sp", lambda e: e.dma_start(out=TMs[cL:cL + 128, VA0:VA0 + 128], in_=GK[0:128, 384:512]),
                      reads=["GK"], writes=[("TMh", "L")], dma=True)
                S.add("sp", lambda e: e.dma_start(out=TMs[cR:cR + 128, VA0:VA0 + 128], in_=GK[128:256, 256:384]),
                      reads=["GK"], writes=[("TMh", "R")], dma=True)
                S.add("sp", lambda e: e.dma_start(out=SINF[:], in_=G1[0:128, 0:512]), reads=["G1"], writes=["SINF"], dma=True)
                S.add("sp", lambda e: e.dma_start(out=SINB[:], in_=G1[128:256, 512:1024]), reads=["G1"], writes=["SINB"], dma=True)
                S.add("dve", lambda e: e.tensor_scalar(out=SINF[:], in0=SINF[:], scalar1=CST[:, C_FL:C_FL + 1], scalar2=None,
                                                       op0=ALU.mult), reads=["SINF", "CST"], writes=["SINF"])
                S.add("dve", lambda e: e.tensor_scalar(out=SINB[:], in0=SINB[:], scalar1=CST[:, C_FR:C_FR + 1], scalar2=None,
                                                       op0=ALU.mult), reads=["SINB", "CST"], writes=["SINB"])

            def loads(n):
                i = n % 2
                sg = G.seg(n)
                c0 = G.fmcol(n)
                lo = NCP if sg else 0
                hi = NCH if sg else NCP
                jl = 0 if (n > lo or sg == 1) else 1
                jh = 2 if (n < hi - 1 or sg == 1) else 1
                extra = []
                if sg == 1 and n == lo:
                    extra = [("FMh", "L"), ("TMh", "L")]
                if sg == 1 and n == hi - 1:
                    extra = extra + [("FMh", "R"), ("TMh", "R")]
                S.add("sp", lambda e: e.dma_start(out=QA[i][:], in_=FM[0:4, :, c0:c0 + 128].rearrange("f p t -> p f t")),
                      writes=[("QA", i)], dma=True)
                ka0 = c0 + (jl - 1) * 128
                nk = (jh - jl + 1) * 128
                S.add("sp", lambda e: e.dma_start(out=KA[i][:, jl * 128:jl * 128 + nk], in_=FM[4, :, ka0:ka0 + nk]),
                      reads=extra, writes=[("KA", i)], dma=True)
                for j in range(jl, jh + 1):
                    rj = c0 + (j - 1) * 128
                    S.add("sp", lambda e, j=j, rj=rj: e.dma_start(
                        out=VA1[i][:, j, :, 0:64], in_=TMs[rj:rj + 128, VA0:VA0 + 128].rearrange("p (k d) -> p k d", k=2)),
                          reads=extra, writes=[("VA1", i, j)], dma=True)
                S.add("sp", lambda e: e.dma_start(out=QR[i][:], in_=FM[5:9, :, c0:c0 + 128].rearrange("f p t -> p f t")),
                      writes=[("QR", i)], dma=True)
                S.add("sp", lambda e: e.dma_start(out=KR[i][:], in_=FM[9:13, :, c0:c0 + 128].rearrange("f p t -> p f t")),
                      writes=[("KR", i)], dma=True)
                S.add("sp", lambda e: e.dma_start(out=KVL[i][:], in_=TMs[c0:c0 + 128, 0:1024]), writes=[("KVL", i)], dma=True)
                i3 = n % 3
                S.add("sp", lambda e: e.dma_start(out=GG[i3][:], in_=TMs[c0:c0 + 128, GR0:GR0 + 2560]), writes=[("GG", i3)], dma=True)
                S.add("sp", lambda e: e.dma_start(out=RBL[i][:], in_=RB[n, :, :]), reads=[("RB", n)], writes=[("RBL", i)], dma=True)
                r0 = G.xrow(n)
                S.add("sp", lambda e: e.dma_start(out=XR[i3][:], in_=xsrc[r0:r0 + 128, :]), writes=[("XR", i3)], dma=True)
                return jl, jh

            rf = [0]
            ptc = [0]
            spc = [0]

            def phase1a(n, jl, jh):
                i = n % 2
                sg = G.seg(n)
                lo = NCP if sg else 0
                hi = NCH if sg else NCP
                DEN, ATT = DENs[i], ATTs[i]
                for kvh in range(2):
                    pts = []
                    for j in range(jl, jh + 1):
                        sp_ = spc[0] % 2
                        spc[0] += 1
                        pt = ptc[0] % 4
                        ptc[0] += 1
                        pts.append((j, pt))
                        S.add("pe", lambda e, j=j, sp_=sp_, kvh=kvh: e.matmul(
                            SP[sp_][:], lhsT=KA[i][kvh * 64:(kvh + 1) * 64, j * 128:(j + 1) * 128],
                            rhs=QA[i][kvh * 64:(kvh + 1) * 64, :, :].rearrange("p g q -> p (g q)"), start=True, stop=True),
                              reads=[("KA", i), ("QA", i)], writes=[("SP", sp_)])
                        S.add("act", lambda e, sp_=sp_, pt=pt: e.activation(out=PTs[pt][:], in_=SP[sp_][:], func=AF.Exp, scale=0.125),
                              reads=[("SP", sp_)], writes=[("PTs", pt)])
                        if j != 1:
                            mj = 0 if j == 0 else 1
                            edge = sg == 1 and ((j == 0 and n == lo) or (j == 2 and n == hi - 1))
                            msk = MKF if edge else MK
                            mkey = ("MKF", mj) if edge else "MK"
                            S.add("pool", lambda e, pt=pt, msk=msk, mj=mj: e.tensor_tensor(
                                out=PTs[pt][:], in0=PTs[pt][:], in1=msk[:, mj, :], op=ALU.mult),
                                  reads=[("PTs", pt), mkey], writes=[("PTs", pt)])
                    for idx, (j, pt) in enumerate(pts):
                        S.add("pe", lambda e, j=j, pt=pt, kvh=kvh, idx=idx, np_=len(pts): e.matmul(
                            OP[:], lhsT=VA1[i][:, j, kvh, :], rhs=PTs[pt][:], start=(idx == 0), stop=(idx == np_ - 1)),
                              reads=[("VA1", i, j), ("VA1o", i), ("PTs", pt)], writes=["OP"])
                    S.add("dve", lambda e, kvh=kvh: e.tensor_tensor(
                        out=DEN[:].rearrange("p (g q) -> p g q", g=4), in0=OP[64:128, :].rearrange("p (g q) -> p g q", g=4),
                        in1=ESK[0:64, kvh * 4:(kvh + 1) * 4].unsqueeze(2).broadcast_to([64, 4, 128]), op=ALU.add),
                          reads=["OP", "ESK"], writes=[("DEN", i)])
                    S.add("dve", lambda e: e.reciprocal(out=DEN[:], in_=DEN[:]), reads=[("DEN", i)], writes=[("DEN", i)])
                    S.add("dve", lambda e, kvh=kvh: e.tensor_tensor(
                        out=ATT[kvh][:].rearrange("p g q -> p (g q)"), in0=OP[0:64, :], in1=DEN[:], op=ALU.mult),
                          reads=["OP", ("DEN", i)], writes=[("ATT", i, kvh)])
            def phase1r(n):
                i = n % 2
                sg = G.seg(n)
                lo = NCP if sg else 0
                hi = NCH if sg else NCP
                INM, QXF, QXB, RFB, RBB, RBT = INMs[i], QXFs[i], QXBs[i], RFBs[i], RBBs[i], RBTs[i]
                SQ, SSR, TR, T2R, RET = SQs[i], SSRs[i], TRs[i], T2Rs[i], RETs[i]
                if n == lo:
                    if sg == 0:
                        S.add("dve", lambda e, c=rf[0]: e.memset(RFST[c][:], 0.0), writes=[("RFST", rf[0])])
                    else:
                        S.add("dve", lambda e, c=rf[0]: e.tensor_copy(out=RFST[c][:], in_=SINF[:]),
                              reads=["SINF"], writes=[("RFST", rf[0])])
                cur = rf[0]
                for h in range(4):
                    S.add("pe", lambda e, h=h: e.matmul(IP[:, h, :], lhsT=KR[i][:, h, :], rhs=QR[i][:, h, :], start=True, stop=True),
                          reads=[("KR", i), ("QR", i)], writes=["IP"])
                S.add("dve", lambda e: e.tensor_tensor(out=INM[:], in0=IP[:], in1=DTOT[:], op=ALU.mult),
                      reads=["IP"] + DTK, writes=[("INM", i)])
                S.add("pool", lambda e: e.tensor_tensor(out=QXF[:], in0=QR[i][:], in1=XIF[:], op=ALU.mult),
                      reads=[("QR", i)] + XIFK, writes=[("QXF", i)])
                S.add("pool", lambda e: e.tensor_tensor(out=QXB[:], in0=QR[i][:], in1=XIB[:], op=ALU.mult),
                      reads=[("QR", i)] + XIBK, writes=[("QXB", i)])
                S.add("act", lambda e, cur=cur: e.activation(out=RFB[:], in_=RFST[cur][:], func=AF.Copy),
                      reads=[("RFST", cur)], writes=[("RFB", i)])
                if sg == 0:
                    S.add("act", lambda e: e.activation(out=RBB[:], in_=RBL[i][:], func=AF.Copy),
                          reads=[("RBL", i)], writes=[("RBB", i)])
                else:
                    jloc = n - lo
                    S.add("pool", lambda e, jloc=jloc: e.tensor_tensor(out=v4(RBT), in0=v4(SINB), in1=bc4(WBW[:, jloc, :]), op=ALU.mult),
                          reads=["SINB", "WBW"], writes=[("RBT", i)])
                    S.add("pool", lambda e: e.tensor_tensor(out=RBB[:], in0=RBT[:], in1=RBL[i][:], op=ALU.add),
                          reads=[("RBT", i), ("RBL", i)], writes=[("RBB", i)])
                for h in range(4):
                    hs_ = slice(h * 128, (h + 1) * 128)
                    S.add("pe", lambda e, h=h, hs_=hs_: e.matmul(ORp[:, h, :], lhsT=INM[:, h, :], rhs=KVL[i][:, 512 + h * 128:512 + (h + 1) * 128],
                                                               start=True, stop=False),
                          reads=[("INM", i), ("KVL", i)], writes=["ORp"])
                    S.add("pe", lambda e, h=h, hs_=hs_: e.matmul(ORp[:, h, :], lhsT=QXF[:, h, :], rhs=RFB[:, hs_], start=False, stop=False),
                          reads=[("QXF", i), ("RFB", i)], writes=["ORp"])
                    S.add("pe", lambda e, h=h, hs_=hs_: e.matmul(ORp[:, h, :], lhsT=QXB[:, h, :], rhs=RBB[:, hs_], start=False, stop=True),
                          reads=[("QXB", i), ("RBB", i)], writes=["ORp"])
                vz = i
                S.add("pool", lambda e: e.tensor_tensor(out=v4(VZ[vz]), in0=KVL[i][:, 512:1024].rearrange("p (h e) -> p h e", h=4),
                                                        in1=bc4(ZET[:, 0:4]), op=ALU.mult),
                      reads=[("KVL", i), "ZET"], writes=[("VZ", vz)])
                S.add("act", lambda e: e.activation(out=SQ[:], in_=ORp[:].rearrange("p h e -> p (h e)"), func=AF.Square),
                      reads=["ORp"], writes=[("SQ", i)])
                S.add("dve", lambda e: e.tensor_tensor(out=v4(TR), in0=ORp[:], in1=bc4(CST[:, C_ONE:C_ONE + 4]), op=ALU.mult),
                      reads=["ORp", "CST"], writes=[("TR", i)])
                for h in range(4):
                    S.add("pe", lambda e, h=h: e.matmul(IP[:, h, :], lhsT=KVL[i][:, h * 128:(h + 1) * 128],
                                                        rhs=VZ[vz][:, h * 128:(h + 1) * 128], start=True, stop=True),
                          reads=[("KVL", i), ("VZ", vz)], writes=["IP"])
                nxt = 1 - cur
                S.add("pool", lambda e, cur=cur, nxt=nxt: e.tensor_tensor(out=v4(RFST[nxt]), in0=v4(RFST[cur]), in1=bc4(CDC[:, 0:4]), op=ALU.mult),
                      reads=[("RFST", cur), "CDC"], writes=[("RFST", nxt)])
                S.add("dve", lambda e, nxt=nxt: e.tensor_tensor(out=RFST[nxt][:], in0=RFST[nxt][:], in1=IP[:].rearrange("p h e -> p (h e)"), op=ALU.add),
                      reads=["IP", ("RFST", nxt)], writes=[("RFST", nxt)])
                rf[0] = nxt
                S.add("dve", lambda e: e.tensor_reduce(out=SSR[:], in_=v4(SQ), axis=mybir.AxisListType.X, op=ALU.add),
                      reads=[("SQ", i)], writes=[("SSR", i)])
                S.add("dve", lambda e: e.tensor_scalar(out=SSR[:], in0=SSR[:], scalar1=1.0 / 128, scalar2=EPS, op0=ALU.mult, op1=ALU.add),
                      reads=[("SSR", i)], writes=[("SSR", i)])
                S.add("act", lambda e: e.activation(out=SSR[:], in_=SSR[:], func=AF.Sqrt), reads=[("SSR", i)], writes=[("SSR", i)])
                S.add("dve", lambda e: e.reciprocal(out=SSR[:], in_=SSR[:]), reads=[("SSR", i)], writes=[("SSR", i)])
                S.add("pool", lambda e: e.tensor_tensor(out=T2R[:], in0=NG[:], in1=GG[n % 3][:, 0:512], op=ALU.mult),
                      reads=["NG", ("GG", n % 3)], writes=[("T2R", i)])
                S.add("pool", lambda e: e.tensor_tensor(out=v4(TR), in0=v4(TR), in1=bc4(SSR[:, 0:4]), op=ALU.mult),
                      reads=[("TR", i), ("SSR", i)], writes=[("TR", i)])
                S.add("pool", lambda e: e.tensor_tensor(out=RET[:], in0=TR[:], in1=T2R[:], op=ALU.mult),
                      reads=[("TR", i), ("T2R", i)], writes=[("RET", i)])

            def phase2(n):
                i = n % 2
                i3 = n % 3
                ATT, RET, RETT, M1, M2, MER, MT = ATTs[i], RETs[i], RETTs[i], M1s[i], M2s[i], MERs[i], MTs[i]
                for h in range(4):
                    S.add("pe", lambda e, h=h: e.transpose(out=TPB[:, h, :], in_=RET[:, h * 128:(h + 1) * 128], identity=ident[:]),
                          reads=[("RET", i), "ident"], writes=["TPB"])
                S.add("act", lambda e: e.activation(out=RETT[:], in_=TPB[:, 0:4, :], func=AF.Copy), reads=["TPB"], writes=[("RETT", i)])
                for ct in range(2):
                    cs = slice(ct * 512, (ct + 1) * 512)
                    for hh in range(8):
                        kvh, g = hh // 4, hh % 4
                        S.add("pe", lambda e, hh=hh, kvh=kvh, g=g, cs=cs: e.matmul(APs[:], lhsT=ATT[kvh][:, g, :], rhs=WBA[:, hh, cs],
                                                                                 start=(hh == 0), stop=(hh == 7)),
                              reads=[("ATT", i, kvh)] + WBAK[hh], writes=["APs"])
                    for h in range(4):
                        S.add("pe", lambda e, h=h, cs=cs: e.matmul(RPs[:], lhsT=RETT[:, h, :], rhs=WBR[:, h, cs], start=(h == 0), stop=(h == 3)),
                              reads=[("RETT", i)] + WBRK[h], writes=["RPs"])
                    S.add("dve", lambda e, ct=ct: e.tensor_tensor(out=M1[:], in0=APs[:], in1=GG[i3][:, 512 + ct * 512:1024 + ct * 512], op=ALU.mult),
                          reads=["APs", ("GG", i3)], writes=[("M1", i)])
                    S.add("dve", lambda e, ct=ct: e.tensor_tensor(out=M2[:], in0=RPs[:], in1=GG[i3][:, 1536 + ct * 512:2048 + ct * 512], op=ALU.mult),
                          reads=["RPs", ("GG", i3)], writes=[("M2", i)])
                    S.add("pool", lambda e, cs=cs: e.tensor_tensor(out=MER[:, cs], in0=M1[:], in1=M2[:], op=ALU.add),
                          reads=[("M1", i), ("M2", i)], writes=[("MER", i, ct)])
                for k in range(8):
                    S.add("pe", lambda e, k=k: e.transpose(out=TPB[:, k, :], in_=MER[:, k * 128:(k + 1) * 128], identity=ident[:]),
                          reads=[("MER", i, k // 4), "ident"], writes=["TPB"])
                S.add("act", lambda e: e.activation(out=MT[:], in_=TPB[:], func=AF.Copy), reads=["TPB"], writes=[("MT", i)])
                for ct in range(2):
                    cs = slice(ct * 512, (ct + 1) * 512)
                    yp, ypk = (APs, "APs") if ct == 0 else (RPs, "RPs")
                    for k in range(8):
                        S.add("pe", lambda e, k=k, cs=cs, yp=yp: e.matmul(yp[:], lhsT=MT[:, k, :], rhs=WO[:, k, cs], start=(k == 0), stop=(k == 7)),
                              reads=[("MT", i)] + WOK[k], writes=[ypk])
                    S.add("dve", lambda e, cs=cs, yp=yp: e.tensor_tensor(out=XR[i3][:, cs], in0=yp[:], in1=XR[i3][:, cs], op=ALU.add),
                          reads=[ypk, ("XR", i3)], writes=[("XR", i3)])
                r1 = G.x1row(n)
                S.add("sp", lambda e: e.dma_start(out=X1[r1:r1 + 128, :], in_=XR[i3][:]), reads=[("XR", i3)], writes=[("X1", n)], dma=True)

            jj = {}
            jj[0] = loads(0)
            for n in range(NCH):
                if n + 1 < NCH:
                    if n + 1 == NCP:
                        unpack()
                    jj[n + 1] = loads(n + 1)
                streams = []
                for f_ in ((lambda: phase1a(n, *jj[n])), (lambda: phase1r(n)), ((lambda: phase2(n - 1)) if n >= 1 else None)):
                    if f_ is None:
                        continue
                    S.cap = []
                    f_()
                    streams.append(S.cap)
                    S.cap = None
                S.interleave(streams)
            phase2(NCH - 1)
            S.emit(nc, st)

    def pass_D(l, last):
        with ExitStack() as st:
            S = Sched()
            ident, CST = common(st, S)
            WF1 = sb(st, "WF1", [128, 8, 2 * DFF], BF16)
            WF2 = sb(st, "WF2", [128, NFC, D], BF16)
            GB = sb(st, "GBF", [128, D], F32)
            GBL = sb(st, "GBL", [128, D], F32)
            CW = sb(st, "CW", [128, NFC * 3], F32)
            CB = sb(st, "CB", [128, NFC], F32)
            XT = [sb(st, f"XT{i}", [128, D], F32) for i in range(4)]
            XN = [sb(st, f"XN{i}", [128, D], BF16) for i in range(2)]
            SQJ = sb(st, "SQJ", [128, D], BF16)
            SS = sb(st, "SS", [128, 16], F32)
            H2T = [sb(st, f"H2T{i}", [128, 8, 258], BF16) for i in range(2)]
            HALOX = sb(st, "HALOX", [128, D], F32)
            HALOT = sb(st, "HALOT", [128, 8, 128], BF16)
            GU = [sb(st, f"GU{i}", [128, 256], BF16) for i in range(3)]
            C1 = [sb(st, f"C1_{i}", [128, 256], F32) for i in range(2)]
            C2 = [sb(st, f"C2_{i}", [128, 256], F32) for i in range(2)]
            C3 = [sb(st, f"C3_{i}", [128, 256], F32) for i in range(2)]
            GE = [sb(st, f"GE{i}", [128, 256], F32) for i in range(2)]
            AS = [sb(st, f"AS{i}", [128, 258], F32) for i in range(2)]
            US = [sb(st, f"US{i}", [128, 256], F32) for i in range(3)]
            TP = ps(st, "TP", [128, 8, 128], BF16)
            PA = [ps(st, f"PA{i}", [128, 512]) for i in range(2)]
            PU = ps(st, "PU", [128, 512])
            PY = [ps(st, f"PY{i}", [128, 512]) for i in range(4)]

            S.add("sp", lambda e: e.dma_start(out=GB[:], in_=g_ffn[l:l + 1, :].partition_broadcast(128)), writes=["GB"], dma=True)
            S.add("sp", lambda e: e.dma_start(out=GBL[:], in_=g_fin[0:1, :].partition_broadcast(128)), writes=["GBL"], dma=True)
            S.add("sp", lambda e: e.dma_start(out=CW[:], in_=cwd[l, :, :]), writes=["CW"], dma=True)
            S.add("sp", lambda e: e.dma_start(out=CB[:], in_=cbd[l, :, :]), writes=["CB"], dma=True)
            rS0 = G.x1row(NCP)
            rS1 = G.x1row(NCH - 1) + 127
            S.add("sp", lambda e: e.dma_start(out=PKG2[0:1, :], in_=X1[rS0:rS0 + 1, :]), writes=["PKG2"], dma=True)
            S.add("sp", lambda e: e.dma_start(out=PKG2[1:2, :], in_=X1[rS1:rS1 + 1, :]), writes=["PKG2"], dma=True)
            S.add("pool", lambda e: e.collective_compute("AllGather", ALU.bypass, replica_groups=GROUPS,
                                                         ins=[PKG2_t.ap().opt()], outs=[G2_t.ap().opt()]),
                  reads=["PKG2"], writes=["G2"], cc=True)
            WF1K, WF2K = {}, {}
            for k in range(8):
                WF1K[k] = load_w(S, WF1[:, k, :], wf1[l, k * 128:(k + 1) * 128, :], ("WF1", k), maxc=1408)
            for fc in range(NFC):
                WF2K[fc] = load_w(S, WF2[:, fc, :], wf2[l, fc * 128:(fc + 1) * 128, :], ("WF2", fc), maxc=1024)
            S.add("dve", lambda e: e.memset(HALOX[:], 0.0), writes=[("HALOX", q_) for q_ in range(1, 64)])
            nbP = PL // 256
            nbS = SL // 256
            S.add("sp", lambda e: e.dma_start(out=HALOX[1:2, :], in_=X1[0:1, :]), writes=[("HALOX", 1)], dma=True)
            for b in range(1, nbP):
                S.add("sp", lambda e, b=b: e.dma_start(out=HALOX[2 * b:2 * b + 2, :], in_=X1[256 * b - 1:256 * b + 1, :]),
                      writes=[("HALOX", 10 + b)], dma=True)
            S.add("sp", lambda e: e.dma_start(out=HALOX[2 * nbP:2 * nbP + 1, :], in_=X1[PL - 1:PL, :]), writes=[("HALOX", 3)], dma=True)
            hb = 2 * (nbP + 1)
            for b in range(1, nbS):
                r = PL + 1 + 256 * b - 1
                S.add("sp", lambda e, b=b, r=r: e.dma_start(out=HALOX[hb + 2 * b:hb + 2 * b + 2, :], in_=X1[r:r + 2, :]),
                      writes=[("HALOX", 30 + b)], dma=True)
            S.add("sp", lambda e: e.dma_start(out=HALOX[hb + 1:hb + 2, :], in_=X1[PL + 1:PL + 2, :]), writes=[("HALOX", 5)], dma=True)
            S.add("sp", lambda e: e.dma_start(out=HALOX[hb + 2 * nbS:hb + 2 * nbS + 1, :], in_=X1[PL + SL:PL + SL + 1, :]),
                  writes=[("HALOX", 6)], dma=True)
            S.add("sp", lambda e: e.dma_start(out=HALOX[hb:hb + 1, :], in_=G2[1:2, :]), reads=["G2"], writes=[("HALOX", 7)], dma=True)
            S.add("sp", lambda e: e.dma_start(out=HALOX[hb + 2 * nbS + 1:hb + 2 * nbS + 2, :], in_=G2[2:3, :]),
                  reads=["G2"], writes=[("HALOX", 8)], dma=True)
            prep_chunk(S, "H", None, 128, HALOX, [("HALOX", q_) for q_ in range(1, 64)], XN[0], ("XN", 0), SQJ, SS, 15, GB, "GB", TP, "TP", ident,
                       HALOT[:], ["HALOT"], rowscale=CST[:, C_HFL:C_HFL + 1])

            S.mark("D halo done")
            tiles = []
            for b in range(nbP):
                tiles.append((b * 2, 2 * b, 2 * (b + 1) + 1))
            for b in range(nbS):
                tiles.append((NCP + b * 2, hb + 2 * b, hb + 2 * (b + 1) + 1))
            cctr = [0]

            def prep_tile(ti, phase="all", only=None):
                n0, hl, hr = tiles[ti]
                hs = ti % 2
                for c in range(2):
                    if only is not None and c != only:
                        continue
                    n = n0 + c
                    xs = (ti % 2) * 2 + c
                    i = c
                    r1 = G.x1row(n)
                    prep_chunk(S, "D", X1[r1:r1 + 128, :], 128, XT[xs], ("XT", xs), XN[i], ("XN", i), SQJ, SS, xs,
                               GB, "GB", TP, "TP", ident, H2T[hs][:, :, 1 + c * 128:1 + (c + 1) * 128], [("H2T", hs, c)],
                               phase=phase)
                if phase == "pre" or only == 0:
                    return
                S.add("dve", lambda e: e.tensor_copy(out=H2T[hs][:, :, 0:1], in_=HALOT[:, :, hl:hl + 1]),
                      reads=["HALOT"], writes=[("H2T", hs, "l")])
                S.add("dve", lambda e: e.tensor_copy(out=H2T[hs][:, :, 257:258], in_=HALOT[:, :, hr:hr + 1]),
                      reads=["HALOT"], writes=[("H2T", hs, "r")])

            fcc = [0]

            def ffn_out(ti, fc, g):
                for sbk in range(2):
                    for ct in range(2):
                        S.add("pe", lambda e, sbk=sbk, ct=ct, fc=fc, g=g: e.matmul(
                            PY[sbk * 2 + ct][:], lhsT=GU[g][:, sbk * 128:(sbk + 1) * 128], rhs=WF2[:, fc, ct * 512:(ct + 1) * 512],
                            start=(fc == 0), stop=(fc == NFC - 1)),
                              reads=[("GU", g)] + WF2K[fc], writes=[("PY", sbk * 2 + ct)])

            def main_tile(ti, mid, early=None):
                n0, hl, hr = tiles[ti]
                hs = ti % 2
                hk = [("H2T", hs, 0), ("H2T", hs, 1), ("H2T", hs, "l"), ("H2T", hs, "r")]
                def st1(fc):
                    p = fc % 2
                    for k in range(8):
                        S.add("pe", lambda e, k=k: e.matmul(PA[p][:, 0:258], lhsT=WF1[:, k, fc * 128:(fc + 1) * 128],
                                                            rhs=H2T[hs][:, k, :], start=(k == 0), stop=(k == 7)),
                              reads=WF1K[k] + hk, writes=[("PA", p)])
                    for k in range(8):
                        S.add("pe", lambda e, k=k: e.matmul(PU[:, 0:256], lhsT=WF1[:, k, DFF + fc * 128:DFF + (fc + 1) * 128],
                                                            rhs=H2T[hs][:, k, 1:257], start=(k == 0), stop=(k == 7)),
                              reads=WF1K[k] + hk, writes=["PU"])
                    S.add("act", lambda e: e.activation(out=AS[p][:], in_=PA[p][:, 0:258], func=AF.Copy),
                          reads=[("PA", p)], writes=[("AS", p)])
                    S.add("act", lambda e: e.activation(out=US[fc % 3][:], in_=PU[:, 0:256], func=AF.Copy),
                          reads=["PU"], writes=[("US", fc % 3)])

                def st2(fc):
                    p = fc % 2
                    S.add("act", lambda e: e.activation(out=C1[p][:], in_=AS[p][:, 1:257], func=AF.Identity,
                                                        scale=CW[:, fc * 3 + 1:fc * 3 + 2], bias=CB[:, fc:fc + 1]),
                          reads=[("AS", p), "CW", "CB"], writes=[("C1", p)])
                    S.add("dve", lambda e: e.scalar_tensor_tensor(out=C2[p][:], in0=AS[p][:, 0:256], scalar=CW[:, fc * 3:fc * 3 + 1],
                                                                  in1=C1[p][:], op0=ALU.mult, op1=ALU.add),
                          reads=[("AS", p), ("C1", p), "CW"], writes=[("C2", p)])
                    S.add("dve", lambda e: e.scalar_tensor_tensor(out=C3[p][:], in0=AS[p][:, 2:258], scalar=CW[:, fc * 3 + 2:fc * 3 + 3],
                                                                  in1=C2[p][:], op0=ALU.mult, op1=ALU.add),
                          reads=[("AS", p), ("C2", p), "CW"], writes=[("C3", p)])

                def st3(fc):
                    p = fc % 2
                    g = fc % 3
                    S.add("act", lambda e: e.activation(out=GE[p][:], in_=C3[p][:], func=AF.Gelu),
                          reads=[("C3", p)], writes=[("GE", p)])
                    S.add("pool", lambda e: e.tensor_tensor(out=GU[g][:], in0=US[g][:], in1=GE[p][:], op=ALU.mult),
                          reads=[("US", g), ("GE", p)], writes=[("GU", g)])

                for t in range(NFC + 3):
                    if t == 1 and early is not None:
                        early()
                    if t == 7 and mid is not None:
                        mid(0)
                    if t == 14 and mid is not None:
                        mid(1)
                    if t < NFC:
                        S.mark(f"D tile {ti} fc {t}")
                        st1(t)
                    if 0 <= t - 3 < NFC:
                        ffn_out(ti, t - 3, (t - 3) % 3)
                    if 0 <= t - 1 < NFC:
                        st2(t - 1)
                    if 0 <= t - 2 < NFC:
                        st3(t - 2)
                S.mark(f"D tile {ti} ffn done")
                for sbk in range(2):
                    xs = (ti % 2) * 2 + sbk
                    n = n0 + sbk
                    for ct in range(2):
                        cs = slice(ct * 512, (ct + 1) * 512)
                        S.add("dve", lambda e, xs=xs, cs=cs, sbk=sbk, ct=ct: e.tensor_tensor(
                            out=XT[xs][:, cs], in0=PY[sbk * 2 + ct][:], in1=XT[xs][:, cs], op=ALU.add),
                              reads=[("PY", sbk * 2 + ct), ("XT", xs)], writes=[("XT", xs)])
                    r0 = G.xrow(n)
                    if not last:
                        S.add("sp", lambda e, xs=xs, r0=r0: e.dma_start(out=X2[r0:r0 + 128, :], in_=XT[xs][:]),
                              reads=[("XT", xs)], writes=[("X2", n)], dma=True)
                    else:
                        sc = SS[:, 8 + xs:9 + xs]
                        sk = ("SSF", xs)
                        S.add("act", lambda e, xs=xs, sc=sc: e.activation(out=SQJ[:], in_=XT[xs][:], func=AF.Square, accum_out=sc),
                              reads=[("XT", xs)], writes=[("SQJ", "D"), sk])
                        S.add("dve", lambda e, sc=sc: e.tensor_scalar(out=sc, in0=sc, scalar1=1.0 / D, scalar2=EPS, op0=ALU.mult, op1=ALU.add),
                              reads=[sk], writes=[sk])
                        S.add("act", lambda e, sc=sc: e.activation(out=sc, in_=sc, func=AF.Sqrt), reads=[sk], writes=[sk])
                        S.add("dve", lambda e, sc=sc: e.reciprocal(out=sc, in_=sc), reads=[sk], writes=[sk])
                        S.add("dve", lambda e, xs=xs, sc=sc: e.scalar_tensor_tensor(out=XT[xs][:], in0=XT[xs][:], scalar=sc, in1=GBL[:],
                                                                                   op0=ALU.mult, op1=ALU.mult),
                              reads=[("XT", xs), sk, "GBL"], writes=[("XT", xs)])
                        S.add("sp", lambda e, xs=xs, r0=r0: e.dma_start(out=y_out[r0:r0 + 128, :], in_=XT[xs][:]),
                              reads=[("XT", xs)], writes=[("Y", n)], dma=True)

            prep_tile(0)
            for ti in range(len(tiles)):
                mid = (lambda c, t=ti + 1: prep_tile(t, "post", only=c)) if ti + 1 < len(tiles) else None
                early = (lambda t=ti + 1: prep_tile(t, "pre")) if ti + 1 < len(tiles) else None
                main_tile(ti, mid, early)
            S.emit(nc, st)

    Sched.GLOBAL.clear()
    Sched.NINST[0] = 0
    gstack = ExitStack()
    Sched.GLOBAL["stack"] = gstack
    plist = []
    for l in range(DEPTH):
        xsrc = x_in if l == 0 else X2
        plist.append(lambda l=l, xsrc=xsrc: pass_A(l, xsrc))
        plist.append(lambda l=l, xsrc=xsrc: pass_B(l, xsrc))
        plist.append(lambda l=l: pass_D(l, l == DEPTH - 1))
    with gstack:
        for f in plist[:npasses]:
            f()
    return nc, G


def _consts(G, core):
    PL, SL, T, NCS = G.PL, G.SL, G.T, G.NCS
    h = core % 2
    pos = np.concatenate([np.arange(PL), h * SL + np.arange(SL)]).astype(np.float32)

    def tabs(half):
        fr = (np.float32(10000.0) ** (-np.arange(half, dtype=np.float32) / np.float32(half))).astype(np.float32)
        ang = (pos[:, None] * fr[None, :]).astype(np.float32)
        return np.cos(ang).astype(np.float32), np.sin(ang).astype(np.float32)

    ca, sa = tabs(32)
    cr, sr = tabs(64)
    fmtab = np.zeros((128, 4, T), np.float32)
    p = np.arange(128)
    da = p % 64
    fmtab[:, 0, :] = ca[:, da % 32].T
    fmtab[:, 1, :] = sa[:, da % 32].T
    fmtab[:, 2, :] = cr[:, p % 64].T
    fmtab[:, 3, :] = sr[:, p % 64].T
    rot = np.zeros((128, 256), np.float32)
    for m in range(128):
        d = m % 64
        if d < 32:
            rot[m + 32, m] = -1.0
        else:
            rot[m - 32, m] = 1.0
        if m < 64:
            rot[m + 64, 128 + m] = -1.0
        else:
            rot[m - 64, 128 + m] = 1.0
    tmtab = np.concatenate([cr, -sr, sr], axis=1).astype(np.float32)
    NCST = 528 + NCS
    cst = np.zeros((128, NCST), np.float32)
    k = np.arange(128)[:, None].astype(np.float32)
    q = np.arange(128)[None, :].astype(np.float32)
    cst[:, 0:128] = np.maximum(q - k, 0)
    cst[:, 128:256] = np.maximum(k - q, 0)
    cst[:, 256:384] = q + 1
    cst[:, 384:512] = 128 - q
    cst[:, 512] = k[:, 0]
    cst[:, 513] = 127 - k[:, 0]
    cst[:, 514] = float(h)
    cst[:, 515] = float(1 - h)
    hfl = np.ones(128, np.float32)
    hb = 2 * (PL // 256 + 1)
    hfl[hb] = float(h)
    hfl[hb + 2 * (SL // 256) + 1] = float(1 - h)
    cst[:, 516] = hfl
    cst[:, 520:524] = 1.0
    cst[:, 528:528 + NCS] = (128.0 * (NCS - 1 - np.arange(NCS)))[None, :]
    mk = np.zeros((128, 2, 512), np.float32)
    kk = np.arange(128)[:, None]
    qq = np.arange(128)[None, :]
    mk[:, 0, :] = np.tile((kk >= qq).astype(np.float32), (1, 4))
    mk[:, 1, :] = np.tile((kk <= qq).astype(np.float32), (1, 4))
    return dict(fmtab=fmtab, tmtab=tmtab, cst=cst, masks=mk.reshape(128, 1024), ident=np.eye(128, dtype=np.float32), rot=rot)


def _perm_w_in(w_in):
    cols = []
    for g in range(4):
        cols += list(range(g * 64, (g + 1) * 64)) + list(range((4 + g) * 64, (5 + g) * 64))
    cols += list(range(512, 640))
    cols += list(range(768, 1280))
    cols += list(range(1280, 1792))
    cols += list(range(1280, 1792)) + list(range(1792, 2304)) + list(range(2304, 2816)) + list(range(2816, 4864)) + list(range(640, 768))
    return np.ascontiguousarray(w_in[:, :, np.array(cols)])


def _shared_inputs(inp):
    f = lambda a: np.ascontiguousarray(np.asarray(a, dtype=np.float32))
    cw = f(inp["conv_w"])
    cwl = np.ascontiguousarray(cw.reshape(DEPTH, 3, NFC, 128).transpose(0, 3, 2, 1).reshape(DEPTH, 128, NFC * 3))
    cb = f(inp["conv_b"])
    cbl = np.ascontiguousarray(cb.reshape(DEPTH, NFC, 128).transpose(0, 2, 1))
    return dict(
        w_in=_perm_w_in(f(inp["w_in"])), wba=f(inp["w_branch_attn"]), wbr=f(inp["w_branch_ret"]), wo=f(inp["w_out"]),
        wf1=f(inp["w_ffn_in"]), wf2=f(inp["w_ffn_out"]), g_mix=f(inp["norm_mix_g"]), g_ffn=f(inp["norm_ffn_g"]),
        g_fin=f(inp["final_norm_g"]).reshape(1, D), g_ret=f(inp["ret_norm_g"]), sink=f(inp["attn_sink"]),
        ldf=f(inp["ret_log_decay_f"]), ldb=f(inp["ret_log_decay_b"]), cw=cwl, cb=cbl)


_CACHE = {}


def run(inp, PL, SL, debug=False, npasses=6, trace=False):
    key = (PL, SL, debug, npasses)
    if key not in _CACHE:
        _CACHE[key] = build(PL, SL, debug, npasses)
    nc, G = _CACHE[key]
    shared = _shared_inputs(inp)
    xp = np.asarray(inp["x_prompt"], dtype=np.float32)
    xs = np.asarray(inp["x_sample"], dtype=np.float32)
    in_maps = []
    for c in range(8):
        m = dict(shared)
        m.update(_consts(G, c))
        h = c % 2
        m["x"] = np.ascontiguousarray(np.concatenate([xp[c], xs[c // 2, h * SL:(h + 1) * SL]], axis=0))
        in_maps.append(m)
    if trace:
        res = run_bass_kernel_spmd(nc, in_maps, core_ids=list(range(8)), trace=True)
    else:
        res = run_bass_kernel_spmd(nc, in_maps, core_ids=list(range(8)))
    yp = np.stack([res.results[c]["y"][:PL] for c in range(8)], axis=0)
    ys = np.stack([np.concatenate([res.results[2 * s]["y"][PL:], res.results[2 * s + 1]["y"][PL:]], axis=0) for s in range(4)], axis=0)
    return (yp.astype(np.float32), ys.astype(np.float32)), res


def kernel(**inputs):
    out, _ = run(inputs, 2048, 4096)
    return out
```
